# Optimizing a Trainium2 kernel written in Bass

```python
import jax, jax.numpy as jnp
from jax import lax
import numpy as np

D_MODEL = 2048
BATCH = 2
SEQ = 16384
DEPTH = 1

ROPE_THETA = 500000.0
NORM_EPS = 1e-6
BLOCK = 128
MAX_POS_OFFSET = 1024

MLA_HEADS = D_MODEL // 128
MLA_NOPE = 128
MLA_ROPE = 64
MLA_V = 128
MLA_QK = MLA_NOPE + MLA_ROPE
MLA_KV_RANK = D_MODEL // 4
MLA_WIDTH = MLA_HEADS * MLA_V

DIL_GROUPS = ((128, 1), (512, 4), (2048, 16))
DIL_N_GROUPS = 3
DIL_HEADS = D_MODEL // 256
DIL_HEAD_DIM = 128
DIL_ROT = DIL_HEAD_DIM // 4
DIL_WIDTH = DIL_HEADS * DIL_HEAD_DIM
DIL_QKV = 3 * DIL_N_GROUPS * DIL_WIDTH

IN_SIZES = (MLA_HEADS * MLA_QK, MLA_KV_RANK + MLA_ROPE, MLA_WIDTH, DIL_QKV, DIL_WIDTH, D_MODEL, D_MODEL)
IN_SPLITS = (3072, 3648, 5696, 14912, 15936, 17984)
IN_TOTAL = 20032

kernel_name = 'hybrid_mla_dilated_gated_block'


def rms_norm(x, g):
    xf = x.astype(jnp.float32)
    y = xf * lax.rsqrt(jnp.mean(jnp.square(xf), axis=-1, keepdims=True) + NORM_EPS)
    return (y * g.astype(jnp.float32)).astype(x.dtype)


def rotary(x, positions):
    r = x.shape[-1]
    inv_freq = 1.0 / (ROPE_THETA ** (jnp.arange(0, r, 2, dtype=jnp.float32) / r))
    ang = positions.astype(jnp.float32)[:, :, None, None] * inv_freq
    cos, sin = jnp.cos(ang), jnp.sin(ang)
    xf = x.astype(jnp.float32)
    x1, x2 = xf[..., : r // 2], xf[..., r // 2:]
    return jnp.concatenate([x1 * cos - x2 * sin, x2 * cos + x1 * sin], axis=-1).astype(x.dtype)


def partial_rotary(x, positions):
    return jnp.concatenate([rotary(x[..., :DIL_ROT], positions), x[..., DIL_ROT:]], axis=-1)


def causal_dense_attention(q, k, v):
    B, S, H, Dk = q.shape
    n_blk = S // BLOCK
    scale = Dk ** -0.5
    qb = jnp.moveaxis(q.reshape(B, n_blk, BLOCK, H, Dk), 1, 0)
    k_pos = jnp.arange(S)

    def one_block(args):
        q_blk, blk = args
        s = jnp.einsum('bqhd,bkhd->bhqk', q_blk, k, preferred_element_type=jnp.float32) * scale
        q_pos = blk * BLOCK + jnp.arange(BLOCK)
        s = jnp.where(k_pos[None, :] <= q_pos[:, None], s, -jnp.inf)
        p = jax.nn.softmax(s, axis=-1)
        return jnp.einsum('bhqk,bkhd->bqhd', p.astype(v.dtype), v)

    out = lax.map(one_block, (qb, jnp.arange(n_blk)))
    return jnp.moveaxis(out, 0, 1).reshape(B, S, H, v.shape[-1])


def dilated_group_attention(q, k, v, window, dilation):
    B, S, H, D = q.shape
    n_steps = window // dilation
    span = dilation * BLOCK
    s_pad = -(-S // span) * span
    n_blk = s_pad // span

    def to_blocks(t):
        t = jnp.pad(t, ((0, 0), (0, s_pad - S), (0, 0), (0, 0)))
        t = jnp.swapaxes(t.reshape(B, s_pad // dilation, dilation, H, D), 1, 2)
        return t.reshape(B, dilation, n_blk, BLOCK, H, D)

    def with_prev(t):
        prev = jnp.pad(t[:, :, :-1], ((0, 0), (0, 0), (1, 0), (0, 0), (0, 0), (0, 0)))
        return jnp.concatenate([prev, t], axis=3)

    qb = to_blocks(q)
    kb = with_prev(to_blocks(k))
    vb = with_prev(to_blocks(v))
    s = jnp.einsum('brnqhd,brnkhd->brnhqk', qb, kb, preferred_element_type=jnp.float32) * (D ** -0.5)
    q_idx = BLOCK + jnp.arange(BLOCK)[:, None]
    k_idx = jnp.arange(2 * BLOCK)[None, :]
    rel = q_idx - k_idx
    band = (rel >= 0) & (rel <= n_steps)
    has_prev = (jnp.arange(n_blk) > 0)[:, None, None] | (k_idx >= BLOCK)[None]
    valid = band[None] & has_prev
    s = jnp.where(valid[None, None, :, None], s, -jnp.inf)
    m = jnp.max(s, axis=-1, keepdims=True)
    e = jnp.exp(s - m)
    l = jnp.sum(e, axis=-1, keepdims=True)
    o = jnp.einsum('brnhqk,brnkhd->brnqhd', (e / l).astype(v.dtype), vb)
    lse = jnp.swapaxes((m + jnp.log(l))[..., 0], 3, 4)

    def from_blocks(t):
        t = t.reshape((B, dilation, s_pad // dilation) + t.shape[4:])
        t = jnp.swapaxes(t, 1, 2)
        return t.reshape((B, s_pad) + t.shape[3:])[:, :S]

    return from_blocks(o), from_blocks(lse)


def mla_branch(q_all, kv_a, positions, g_kv, w_kv_b):
    B, S, _ = q_all.shape
    q = q_all.reshape(B, S, MLA_HEADS, MLA_QK)
    q = jnp.concatenate([q[..., :MLA_NOPE], rotary(q[..., MLA_NOPE:], positions)], axis=-1)
    c_kv = rms_norm(kv_a[..., :MLA_KV_RANK], g_kv)
    k_pe = rotary(kv_a[..., MLA_KV_RANK:][:, :, None, :], positions)
    kv = jnp.einsum('bsr,rhe->bshe', c_kv, w_kv_b.reshape(MLA_KV_RANK, MLA_HEADS, MLA_NOPE + MLA_V))
    k = jnp.concatenate([kv[..., :MLA_NOPE], jnp.broadcast_to(k_pe, (B, S, MLA_HEADS, MLA_ROPE))], axis=-1)
    v = kv[..., MLA_NOPE:]
    return causal_dense_attention(q, k, v).reshape(B, S, MLA_WIDTH)


def dilated_branch(qkv, positions):
    B, S, _ = qkv.shape
    qkv = qkv.reshape(B, S, 3, DIL_N_GROUPS * DIL_HEADS, DIL_HEAD_DIM)
    q = partial_rotary(qkv[:, :, 0], positions).reshape(B, S, DIL_N_GROUPS, DIL_HEADS, DIL_HEAD_DIM)
    k = partial_rotary(qkv[:, :, 1], positions).reshape(B, S, DIL_N_GROUPS, DIL_HEADS, DIL_HEAD_DIM)
    v = qkv[:, :, 2].reshape(B, S, DIL_N_GROUPS, DIL_HEADS, DIL_HEAD_DIM)
    outs, lses = [], []
    for g, (window, dilation) in enumerate(DIL_GROUPS):
        o, lse = dilated_group_attention(q[:, :, g], k[:, :, g], v[:, :, g], window, dilation)
        outs.append(o)
        lses.append(lse)
    wts = jax.nn.softmax(jnp.stack(lses, axis=0), axis=0)
    o = jnp.sum(wts[..., None] * jnp.stack(outs, axis=0).astype(jnp.float32), axis=0)
    return o.astype(qkv.dtype).reshape(B, S, DIL_WIDTH)


def setup_inputs(seed: int = 0) -> dict:
    key = jax.random.key(seed)
    ks = jax.random.split(key, 16)

    def nrm(k, shape, fan_in):
        return jax.random.normal(k, shape, jnp.float32) * (fan_in ** -0.5)

    x = jax.random.normal(ks[0], (BATCH, SEQ, D_MODEL), jnp.float32)
    c = jax.random.normal(ks[1], (BATCH, D_MODEL), jnp.float32)
    offset = jax.random.randint(ks[2], (BATCH, 1), 0, MAX_POS_OFFSET, dtype=jnp.int32)
    positions = jnp.arange(SEQ, dtype=jnp.int32)[None, :] + offset
    w_ada = nrm(ks[3], (DEPTH, D_MODEL, 3 * D_MODEL), D_MODEL)
    b_ada = 0.02 * jax.random.normal(ks[4], (DEPTH, 3 * D_MODEL), jnp.float32)
    g_pre = 1.0 + 0.02 * jax.random.normal(ks[5], (DEPTH, D_MODEL), jnp.float32)
    g_post = 1.0 + 0.02 * jax.random.normal(ks[6], (DEPTH, D_MODEL), jnp.float32)
    w_in = nrm(ks[7], (DEPTH, D_MODEL, IN_TOTAL), D_MODEL)
    g_kv = 1.0 + 0.02 * jax.random.normal(ks[8], (DEPTH, MLA_KV_RANK), jnp.float32)
    w_kv_b = nrm(ks[9], (DEPTH, MLA_KV_RANK, MLA_HEADS * (MLA_NOPE + MLA_V)), MLA_KV_RANK)
    w_mla_proj = nrm(ks[10], (DEPTH, MLA_WIDTH, D_MODEL), MLA_WIDTH)
    w_dil_proj = nrm(ks[11], (DEPTH, DIL_WIDTH, D_MODEL), DIL_WIDTH)
    w_o = nrm(ks[12], (DEPTH, D_MODEL, D_MODEL), D_MODEL)
    return {'x': x, 'c': c, 'positions': positions, 'w_ada': w_ada, 'b_ada': b_ada,
            'g_pre': g_pre, 'g_post': g_post, 'w_in': w_in, 'g_kv': g_kv, 'w_kv_b': w_kv_b,
            'w_mla_proj': w_mla_proj, 'w_dil_proj': w_dil_proj, 'w_o': w_o}


def reference(x, c, positions, w_ada, b_ada, g_pre, g_post, w_in, g_kv, w_kv_b,
              w_mla_proj, w_dil_proj, w_o):
    for layer in range(DEPTH):
        mod = jnp.einsum('bd,de->be', jax.nn.silu(c), w_ada[layer]) + b_ada[layer]
        shift, scale, gate = jnp.split(mod, 3, axis=-1)
        h = rms_norm(x, g_pre[layer]) * (1.0 + scale[:, None, :]) + shift[:, None, :]
        proj = jnp.einsum('bsd,de->bse', h, w_in[layer])
        q_mla, kv_a, z_mla, qkv_dil, z_dil, gl_mla, gl_dil = jnp.split(proj, IN_SPLITS, axis=-1)
        a = mla_branch(q_mla, kv_a, positions, g_kv[layer], w_kv_b[layer]) * jax.nn.silu(z_mla)
        y_mla = jnp.einsum('bse,ed->bsd', a, w_mla_proj[layer])
        bb = dilated_branch(qkv_dil, positions) * jax.nn.silu(z_dil)
        y_dil = jnp.einsum('bse,ed->bsd', bb, w_dil_proj[layer])
        merged = jax.nn.sigmoid(gl_mla) * y_mla + jax.nn.sigmoid(gl_dil) * y_dil
        out = jnp.einsum('bsd,de->bse', merged, w_o[layer])
        x = x + gate[:, None, :] * rms_norm(out, g_post[layer])
    return x
```

```python
import math
import time
from contextlib import ExitStack

import numpy as np
import concourse.bass as bass
import concourse.mybir as mybir
from concourse.bass_utils import run_bass_kernel_spmd

F32 = mybir.dt.float32
BF16 = mybir.dt.bfloat16
I32 = mybir.dt.int32
AF = mybir.ActivationFunctionType
ALU = mybir.AluOpType

NEG = -30000.0
EPS = 1e-6
THETA = 500000.0
DIL = ((128, 1), (512, 4), (2048, 16))
DBACK = 2048


class Cfg:
    def __init__(s, D=2048, H=16, R=512, DH=8, S=16384, NCB=4):
        s.D, s.H, s.R, s.DH, s.S, s.NCB = D, H, R, DH, S, NCB
        s.ND, s.NR = D // 128, R // 128
        s.CH = S // NCB
        s.W = S
        s.NU = s.CH // 512
        s.NB = s.W // 128
        s.DW = DBACK + s.CH
        s.NBD = s.DW // 128
        s.NGD = s.DW // 512
        s.QM, s.KVA, s.ZM, s.QKVD, s.ZD = H * 192, R + 64, H * 128, 9 * DH * 128, DH * 128
        s.o_q = 0
        s.o_kva = s.QM
        s.o_zm = s.o_kva + s.KVA
        s.o_qkvd = s.o_zm + s.ZM
        s.o_zd = s.o_qkvd + s.QKVD
        s.o_glm = s.o_zd + s.ZD
        s.o_gld = s.o_glm + D
        s.IN = s.o_gld + D
        s.HV = H * 128
        s.DV = DH * 128


class Buf:
    __slots__ = ("t", "name", "writes", "reads", "excl")

    def __init__(self, t, name, excl=False):
        self.t, self.name, self.writes, self.reads, self.excl = t, name, {}, {}, excl

    def __getitem__(self, idx):
        return self.t[idx]


class Prog:
    NDMA = 16
    GEN = 20000

    def __init__(self, nc, stack):
        self.nc = nc
        self.eng = {"pe": nc.tensor, "act": nc.scalar, "dve": nc.vector, "pool": nc.gpsimd, "sp": nc.sync}
        self.semobj, self.cnt, self.ring, self.ring_pos = {}, {}, {}, {}
        self.stack = stack
        self.gen = {}
        for e in ("pe", "act", "dve", "pool"):
            self.gen[e] = 0
            self.semobj[("e", e, 0)] = stack.enter_context(nc.semaphore("s_" + e))
            self.cnt[e] = 0
        for q in ("sp", "pool"):
            self.ring[q] = []
            for i in range(self.NDMA):
                s = stack.enter_context(nc.semaphore("d_%s%d" % (q, i)))
                self.semobj[("d", q, i)] = s
                self.ring[q].append(0)
            self.ring_pos[q] = 0
        self.seen = {e: {} for e in self.eng}
        self.ninst = 0
        self.uid = 0

    def sbuf(self, st, name, shape, dtype):
        self.uid += 1
        return Buf(st.enter_context(self.nc.sbuf_tensor("%s_%d" % (name, self.uid), list(shape), dtype)), name)

    def psum(self, st, name, shape, dtype):
        self.uid += 1
        if dtype == BF16 and shape[-1] * 2 < 2048:
            shape = list(shape[:-1]) + [1024]
        return Buf(st.enter_context(self.nc.psum_tensor("%s_%d" % (name, self.uid), list(shape), dtype)), name, True)

    def dram(self, name, shape, dtype, kind="Internal"):
        return Buf(self.nc.dram_tensor(name, list(shape), dtype, kind=kind).ap(), name)

    def _wait(self, e, key, val):
        if val <= 0 or self.seen[e].get(key, 0) >= val:
            return
        self.seen[e][key] = val
        self.eng[e].wait_ge(self.semobj[key], val)

    def _deps(self, e, reads, writes, wnow, skip_self):
        deps = {}
        for b in reads:
            for k, v in b.writes.items():
                if deps.get(k, 0) < v:
                    deps[k] = v
            if b.excl:
                for k, v in b.reads.items():
                    if deps.get(k, 0) < v:
                        deps[k] = v
        for b in writes:
            for d in (b.writes, b.reads):
                for k, v in d.items():
                    if deps.get(k, 0) < v:
                        deps[k] = v
        for b in wnow:
            for k, v in b.reads.items():
                if deps.get(k, 0) < v:
                    deps[k] = v
        for k, v in deps.items():
            if skip_self and k[0] == "e" and k[1] == e:
                continue
            self._wait(e, k, v)

    def _commit(self, key, val, reads, writes, wnow):
        for b in reads:
            if b.reads.get(key, 0) < val:
                b.reads[key] = val
        for b in writes:
            b.writes = {key: val}
            b.reads = {}
        for b in wnow:
            b.writes[key] = val

    def op(self, e, fn, reads=(), writes=(), acc=()):
        self._deps(e, list(reads) + list(acc), writes, (), e == "pe")
        inst = fn()
        if self.cnt[e] >= self.GEN:
            self.gen[e] += 1
            self.cnt[e] = 0
            self.semobj[("e", e, self.gen[e])] = self.stack.enter_context(self.nc.semaphore("s_%s%d" % (e, self.gen[e])))
        self.cnt[e] += 1
        key = ("e", e, self.gen[e])
        inst.then_inc(self.semobj[key], 1)
        self._commit(key, self.cnt[e], reads, writes, acc)
        self.ninst += 1

    def dma(self, q, out, in_, reads=(), writes=(), wnow=(), **kw):
        pos = self.ring_pos[q]
        self.ring_pos[q] = (pos + 1) % self.NDMA
        key = ("d", q, pos)
        self._wait(q, key, self.ring[q][pos])
        self._deps(q, reads, writes, wnow, False)
        inst = self.eng[q].dma_start(out=out, in_=in_, **kw)
        self.ring[q][pos] += 16
        inst.then_inc(self.semobj[key], 16)
        self._commit(key, self.ring[q][pos], reads, writes, wnow)
        self.ninst += 1

    def barrier(self):
        for e in self.eng:
            for k in list(self.semobj):
                if k[0] == "e":
                    if k[2] != self.gen[k[1]]:
                        continue
                    v = self.cnt[k[1]]
                else:
                    v = self.ring[k[1]][k[2]]
                self._wait(e, k, v)


def build(cfg, stop_after=99):
    c = cfg
    D, H, R, DH, ND, NR, NB, W, CH, NU = c.D, c.H, c.R, c.DH, c.ND, c.NR, c.NB, c.W, c.CH, c.NU
    DW, NBD, NGD, HV, DV = c.DW, c.NBD, c.NGD, c.HV, c.DV
    nc = bass.Bass("TRN2", target_bir_lowering=False)

    def din(name, shape, dt=F32):
        return nc.dram_tensor(name, list(shape), dt, kind="ExternalInput").ap()

    xw = din("xw", [W, D])
    posw = din("posw", [128, NB], I32)
    kbias_d = din("kbias", [128, NB])
    c_col = din("c_col", [128, ND])
    b_ada = din("b_ada", [1, 3 * D])
    g_pre = din("g_pre", [1, D])
    g_post = din("g_post", [1, D])
    g_kv = din("g_kv", [1, R])
    w_ada = din("w_ada", [D, 3 * D])
    w_in = din("w_in", [D, c.IN])
    w_kvb = din("w_kv_b", [R, H * 256])
    w_mla = din("w_mla_proj", [HV, D])
    w_dil = din("w_dil_proj", [DV, D])
    w_o = din("w_o", [D, D])
    ident_d = din("ident", [128, 128])
    freq_d = din("freq", [128, 32])
    mmask_d = din("mmask", [4, 128, 512])
    NDM = [5, 8, 20]
    dmask_d = din("dmask", [sum(NDM), 128, 512])
    y = nc.dram_tensor("y", [CH, D], F32, kind="ExternalOutput").ap()

    st0 = ExitStack()
    P = Prog(nc, st0)
    yB = Buf(y, "y")
    modrows = P.dram("modrows", [3, D], F32)
    hT_s = P.dram("hT_s", [ND, 128, DW], BF16)
    KT_s = P.dram("KT_s", [H, 128, W], BF16)
    KPE_s = P.dram("KPE_s", [64, W], BF16)
    V_s = P.dram("V_s", [H, 128, NB, 128], BF16)
    QN_s = P.dram("QN_s", [H, 128, CH], BF16)
    QR_s = P.dram("QR_s", [H // 2, 128, CH], BF16)
    ZM_s = P.dram("ZM_s", [H, 128, CH], BF16)
    ZD_s = P.dram("ZD_s", [DH, 128, CH], BF16)
    GLM_s = P.dram("GLM_s", [ND, 128, CH], BF16)
    GLD_s = P.dram("GLD_s", [ND, 128, CH], BF16)
    QD_s = P.dram("QD_s", [3 * DH, 128, CH], BF16)
    KD_s = P.dram("KD_s", [3 * DH, 128, DW], BF16)
    VD_s = P.dram("VD_s", [3 * DH, 128, NBD, 128], BF16)
    A_s = P.dram("A_s", [H, 128, CH], BF16)
    B_s = P.dram("B_s", [DH, 128, CH], BF16)
    M_s = P.dram("M_s", [ND, 128, CH], BF16)

    T = nc.tensor
    A = nc.scalar
    V = nc.vector
    mm = T.matmul

    identb = P.sbuf(st0, "identb", [128, 128], BF16)
    onesb = P.sbuf(st0, "onesb", [128, 128], BF16)
    kbias = P.sbuf(st0, "kbias", [128, NB], F32)
    P.dma("pool", identb[:, :], ident_d[:, :], writes=[identb])
    P.dma("sp", kbias[:, :], kbias_d[:, :], writes=[kbias])
    P.op("dve", lambda: V.memset(onesb[:, :], 1.0), writes=[onesb])

    def rs_from_ss(ss, rs, n):
        P.op("act", lambda: A.activation(rs[:, :], ss[:, :], AF.Sqrt, bias=epsb[:, :], scale=1.0 / n),
             reads=[ss, epsb], writes=[rs])
        P.op("dve", lambda: V.reciprocal(rs[:, :], rs[:, :]), reads=[rs], writes=[rs])

    epsb = P.sbuf(st0, "epsb", [128, 1], F32)
    P.op("dve", lambda: V.memset(epsb[:, :], EPS), writes=[epsb])
    npib = P.sbuf(st0, "npib", [128, 1], F32)
    P.op("dve", lambda: V.memset(npib[:, :], -math.pi), writes=[npib])

    st12 = ExitStack()
    COS = P.sbuf(st12, "COS", [128, NB, 32], F32)
    SIN = P.sbuf(st12, "SIN", [128, NB, 32], F32)
    with ExitStack() as st:
        sc = P.sbuf(st, "sc", [128, ND], F32)
        P.dma("sp", sc[:, :], c_col[:, :], writes=[sc])
        P.op("act", lambda: A.activation(sc[:, :], sc[:, :], AF.Silu), reads=[sc], writes=[sc])
        GA = 512 if (3 * D) % 512 == 0 else (3 * D) // 2
        slab = [P.sbuf(st, "aslab%d" % i, [128, ND, GA], F32) for i in range(2)]
        modrow = P.sbuf(st, "modrow", [1, 3 * D], F32)
        rowt = P.sbuf(st, "rowt", [1, 3 * D], F32)
        pm = [P.psum(st, "pm%d" % i, [128, 512], F32) for i in range(2)]
        w_ada_v = w_ada.rearrange("(c p) n -> p c n", p=128)
        ng = 3 * D // GA
        for g in range(ng):
            sl = slab[g % 2]
            P.dma("sp", sl[:, :, :], w_ada_v[:, :, g * GA:(g + 1) * GA], writes=[sl])
            pp = pm[g % 2]
            for dc in range(ND):
                P.op("pe", lambda dc=dc: mm(pp[0:1, 0:GA], sc[:, dc:dc + 1], sl[:, dc, :], start=(dc == 0), stop=(dc == ND - 1)),
                     reads=[sc, sl], writes=[pp] if dc == 0 else (), acc=() if dc == 0 else [pp])
            P.op("dve", lambda g=g: V.tensor_copy(modrow[0:1, g * GA:(g + 1) * GA], pp[0:1, 0:GA]),
                 reads=[pp], acc=[modrow])
        P.dma("sp", rowt[0:1, :], b_ada[:, :], writes=[rowt])
        P.op("dve", lambda: V.tensor_tensor(modrow[0:1, :], modrow[0:1, :], rowt[0:1, :], ALU.add),
             reads=[rowt], writes=[modrow])
        P.dma("sp", rowt[0:1, 0:D], g_pre[:, :], reads=[], writes=[rowt])
        P.dma("sp", rowt[0:1, D:2 * D], g_post[:, :], wnow=[rowt])
        P.op("dve", lambda: V.scalar_tensor_tensor(modrow[0:1, D:2 * D], modrow[0:1, D:2 * D], 1.0, rowt[0:1, 0:D],
                                                   ALU.add, ALU.mult), reads=[rowt], writes=[modrow])
        P.op("dve", lambda: V.tensor_tensor(modrow[0:1, 2 * D:3 * D], modrow[0:1, 2 * D:3 * D], rowt[0:1, D:2 * D], ALU.mult),
             reads=[rowt], writes=[modrow])
        P.dma("sp", modrows[0:1, :], modrow[0:1, D:2 * D], reads=[modrow], wnow=[modrows])
        P.dma("sp", modrows[1:2, :], modrow[0:1, 0:D], reads=[modrow], wnow=[modrows])
        P.dma("sp", modrows[2:3, :], modrow[0:1, 2 * D:3 * D], reads=[modrow], wnow=[modrows])
        P.barrier()
    with ExitStack() as st:
        posi = P.sbuf(st, "posi", [128, NB], I32)
        posf = P.sbuf(st, "posf", [128, NB], F32)
        freq = P.sbuf(st, "freq", [128, 32], F32)
        ang = P.sbuf(st, "ang", [128, NB, 32], F32)
        P.dma("sp", posi[:, :], posw[:, :], writes=[posi])
        P.dma("sp", freq[:, :], freq_d[:, :], writes=[freq])
        P.op("dve", lambda: V.tensor_copy(posf[:, :], posi[:, :]), reads=[posi], writes=[posf])
        for b in range(NB):
            P.op("dve", lambda b=b: V.tensor_scalar(ang[:, b, :], freq[:, :], posf[:, b:b + 1], None, ALU.mult),
                 reads=[posf, freq], writes=[ang] if b == 0 else (), acc=() if b == 0 else [ang])
        angf = ang[:, :, :].rearrange("p a b -> p (a b)")
        ti = P.sbuf(st, "ti", [128, NB * 32], I32)
        nf = P.sbuf(st, "nf", [128, NB * 32], F32)
        msk = P.sbuf(st, "msk", [128, NB * 32], F32)
        C1 = 6.28125
        C2 = 2.0 * math.pi - C1
        PI_LO = 3.1415925
        for tab, sh in ((SIN, 0.0), (COS, 0.5 * math.pi)):
            tf = tab[:, :, :].rearrange("p a b -> p (a b)")
            P.op("dve", lambda tf=tf, sh=sh: V.tensor_scalar(tf, angf, sh, None, ALU.add), reads=[ang], writes=[tab])
            P.op("dve", lambda tf=tf: V.tensor_scalar(nf[:, :], tf, 1.0 / (2.0 * math.pi), None, ALU.mult), reads=[tab], writes=[nf])
            P.op("dve", lambda: V.tensor_copy(ti[:, :], nf[:, :]), reads=[nf], writes=[ti])
            P.op("dve", lambda: V.tensor_copy(nf[:, :], ti[:, :]), reads=[ti], writes=[nf])
            P.op("dve", lambda tf=tf: V.scalar_tensor_tensor(tf, nf[:, :], -C1, tf, ALU.mult, ALU.add), reads=[nf], writes=[tab])
            P.op("dve", lambda tf=tf: V.scalar_tensor_tensor(tf, nf[:, :], -C2, tf, ALU.mult, ALU.add), reads=[nf], writes=[tab])
            P.op("dve", lambda tf=tf: V.tensor_scalar(msk[:, :], tf, math.pi, None, ALU.is_gt), reads=[tab], writes=[msk])
            P.op("dve", lambda tf=tf: V.scalar_tensor_tensor(tf, msk[:, :], -2.0 * math.pi, tf, ALU.mult, ALU.add), reads=[msk], writes=[tab])
            P.op("dve", lambda tf=tf: V.tensor_scalar(msk[:, :], tf, -math.pi, None, ALU.is_lt), reads=[tab], writes=[msk])
            P.op("dve", lambda tf=tf: V.scalar_tensor_tensor(tf, msk[:, :], 2.0 * math.pi, tf, ALU.mult, ALU.add), reads=[msk], writes=[tab])
            P.op("dve", lambda tf=tf: V.tensor_scalar(tf, tf, PI_LO, -PI_LO, ALU.min, ALU.max), reads=[], writes=[tab])
            P.op("act", lambda tf=tf: A.activation(tf, tf, AF.Sin), reads=[], writes=[tab])
        P.barrier()

    if stop_after <= 0:
        st12.close(); st0.close()
        return nc, P
    w_in_v = w_in.rearrange("(c p) n -> p c n", p=128)
    VN = min(512, HV)
    NCG = HV // VN
    HPG = VN // 128
    KH = min(8, H)
    with ExitStack() as st:
        G1b = P.sbuf(st, "G1b", [128, D], F32)
        SHb = P.sbuf(st, "SHb", [128, D], F32)
        gkvb = P.sbuf(st, "gkvb", [128, R], F32)
        P.dma("sp", G1b[:, :], modrows[0].partition_broadcast(128), reads=[modrows], writes=[G1b])
        P.dma("sp", SHb[:, :], modrows[1].partition_broadcast(128), reads=[modrows], writes=[SHb])
        P.dma("sp", gkvb[:, :], g_kv[0].partition_broadcast(128), writes=[gkvb])
        Wkva = P.sbuf(st, "Wkva", [128, ND, c.KVA], BF16)
        P.dma("pool", Wkva[:, :, :], w_in_v[:, :, c.o_kva:c.o_kva + c.KVA], writes=[Wkva])
        WkK = P.sbuf(st, "WkK", [128, NR, HV], BF16)
        WkV = P.sbuf(st, "WkV", [128, NR, HV], BF16)
        wkv_v = w_kvb.rearrange("(c p) (h e) -> p c h e", p=128, e=256)
        for cc in range(NR):
            P.dma("pool", WkK[:, cc, :].rearrange("p (h e) -> p h e", e=128), wkv_v[:, cc, :, 0:128],
                  writes=[WkK] if cc == 0 else (), wnow=() if cc == 0 else [WkK])
            P.dma("pool", WkV[:, cc, :].rearrange("p (h e) -> p h e", e=128), wkv_v[:, cc, :, 128:256],
                  writes=[WkV] if cc == 0 else (), wnow=() if cc == 0 else [WkV])
        xb = [P.sbuf(st, "xb%d" % i, [128, D], F32) for i in range(2)]
        hb = [P.sbuf(st, "hb%d" % i, [128, D], BF16) for i in range(2)]
        ss = [P.sbuf(st, "ss%d" % i, [128, 1], F32) for i in range(2)]
        rs = [P.sbuf(st, "rs%d" % i, [128, 1], F32) for i in range(2)]
        ss2 = [P.sbuf(st, "ssb%d" % i, [128, 1], F32) for i in range(2)]
        rs2 = [P.sbuf(st, "rsb%d" % i, [128, 1], F32) for i in range(2)]
        hTb = [P.sbuf(st, "hTb%d" % i, [128, ND, 128], BF16) for i in range(4)]
        ckvb = [P.sbuf(st, "ckvb%d" % i, [128, R], BF16) for i in range(2)]
        junk = P.sbuf(st, "junk", [128, R], BF16)
        Tt = [P.sbuf(st, "Tt%d" % i, [128, 128], F32) for i in range(2)]
        kpeb = [P.sbuf(st, "kpeb%d" % i, [128, 64], BF16) for i in range(2)]
        ckvTg = [P.sbuf(st, "ckvTg%d" % i, [128, NR, 512], BF16) for i in range(2)]
        kpeTg = [P.sbuf(st, "kpeTg%d" % i, [64, 512], BF16) for i in range(2)]
        Vg = [P.sbuf(st, "Vg%d" % i, [128, H, 4, 128], BF16) for i in range(2)]
        Kst = [P.sbuf(st, "Kst%d" % i, [128, KH, 512], BF16) for i in range(2)]
        pT = [P.psum(st, "pT%d" % i, [128, 512], BF16) for i in range(2)]
        pkv = P.psum(st, "pkv", [128, 512], F32)
        pkp = P.psum(st, "pkp", [128, 512], F32)
        pV = [P.psum(st, "pV%d" % i, [128, 512], F32) for i in range(2)]
        pK = [P.psum(st, "pK%d" % i, [128, 512], F32) for i in range(2)]
        nT = 0
        nV = 0
        nK = 0
        V_s_v = V_s[:, :, :, :].rearrange("h p b d -> p h b d")
        KT_s_v = KT_s[:, :, :].rearrange("h p t -> p h t")
        hT_s_v = hT_s[:, :, :].rearrange("c p t -> p c t")
        for r in range(NB):
            g, bi = r // 4, r % 4
            x_, h_, ss_, rs_ = xb[r % 2], hb[r % 2], ss[r % 2], rs[r % 2]
            P.dma("sp", x_[:, :], xw[r * 128:(r + 1) * 128, :], writes=[x_])
            P.op("act", lambda: A.activation(h_[:, :], x_[:, :], AF.Square, accum_out=ss_[:, :]),
                 reads=[x_], writes=[h_, ss_])
            rs_from_ss(ss_, rs_, D)
            P.op("dve", lambda: V.scalar_tensor_tensor(x_[:, :], x_[:, :], rs_[:, :], G1b[:, :], ALU.mult, ALU.mult),
                 reads=[rs_, G1b], writes=[x_])
            P.op("dve", lambda: V.tensor_tensor(h_[:, :], x_[:, :], SHb[:, :], ALU.add), reads=[x_, SHb], writes=[h_])
            hT_ = hTb[bi]
            for q4 in range(ND // 4 if ND >= 4 else 1):
                nq = min(4, ND)
                pt_ = pT[nT % 2]
                nT += 1
                for a in range(nq):
                    dc = q4 * 4 + a
                    P.op("pe", lambda dc=dc, a=a: T.transpose(pt_[:, a * 128:(a + 1) * 128], h_[:, dc * 128:(dc + 1) * 128], identb[:, :]),
                         reads=[h_, identb], writes=[pt_] if a == 0 else (), acc=() if a == 0 else [pt_])
                src = pt_[:, 0:nq * 128].rearrange("p (a b) -> p a b", b=128)
                dst = hT_[:, q4 * 4:q4 * 4 + nq, :]
                if q4 % 2 == 0:
                    P.op("act", lambda: A.copy(dst, src), reads=[pt_], writes=[hT_] if q4 == 0 else (), acc=() if q4 == 0 else [hT_])
                else:
                    P.op("dve", lambda: V.tensor_copy(dst, src), reads=[pt_], acc=[hT_])
            if r >= NB - NBD:
                rd = r - (NB - NBD)
                P.dma("sp", hT_s_v[:, :, rd * 128:(rd + 1) * 128], hT_[:, :, :], reads=[hT_], wnow=[hT_s])
            for dc in range(ND):
                P.op("pe", lambda dc=dc: mm(pkv[:, 0:R], hT_[:, dc, :], Wkva[:, dc, 0:R], start=(dc == 0), stop=(dc == ND - 1)),
                     reads=[hT_, Wkva], writes=[pkv] if dc == 0 else (), acc=() if dc == 0 else [pkv])
            for dc in range(ND):
                P.op("pe", lambda dc=dc: mm(pkp[:, 0:64], hT_[:, dc, :], Wkva[:, dc, R:R + 64], start=(dc == 0), stop=(dc == ND - 1)),
                     reads=[hT_, Wkva], writes=[pkp] if dc == 0 else (), acc=() if dc == 0 else [pkp])
            s2, r2, ck_ = ss2[r % 2], rs2[r % 2], ckvb[r % 2]
            P.op("act", lambda: A.activation(junk[:, :], pkv[:, 0:R], AF.Square, accum_out=s2[:, :]),
                 reads=[pkv], writes=[junk, s2])
            rs_from_ss(s2, r2, R)
            P.op("dve", lambda: V.scalar_tensor_tensor(ck_[:, :], pkv[:, 0:R], r2[:, :], gkvb[:, :], ALU.mult, ALU.mult),
                 reads=[pkv, r2, gkvb], writes=[ck_])
            t_, kp_ = Tt[r % 2], kpeb[r % 2]
            cb, sb = COS[:, r, :], SIN[:, r, :]
            x1, x2 = pkp[:, 0:32], pkp[:, 32:64]
            P.op("dve", lambda: V.tensor_tensor(t_[:, 0:32], x1, cb, ALU.mult), reads=[pkp, COS], writes=[t_])
            P.op("dve", lambda: V.tensor_tensor(t_[:, 32:64], x2, sb, ALU.mult), reads=[pkp, SIN], acc=[t_])
            P.op("dve", lambda: V.tensor_tensor(t_[:, 64:96], x2, cb, ALU.mult), reads=[pkp, COS], acc=[t_])
            P.op("dve", lambda: V.tensor_tensor(t_[:, 96:128], x1, sb, ALU.mult), reads=[pkp, SIN], acc=[t_])
            P.op("dve", lambda: V.tensor_tensor(kp_[:, 0:32], t_[:, 0:32], t_[:, 32:64], ALU.subtract), reads=[t_], writes=[kp_])
            P.op("dve", lambda: V.tensor_tensor(kp_[:, 32:64], t_[:, 64:96], t_[:, 96:128], ALU.add), reads=[t_], acc=[kp_])
            cg_, kg_ = ckvTg[g % 2], kpeTg[g % 2]
            pt_ = pT[nT % 2]
            nT += 1
            for a in range(NR):
                P.op("pe", lambda a=a: T.transpose(pt_[:, a * 128:(a + 1) * 128], ck_[:, a * 128:(a + 1) * 128], identb[:, :]),
                     reads=[ck_, identb], writes=[pt_] if a == 0 else (), acc=() if a == 0 else [pt_])
            P.op("act", lambda: A.copy(cg_[:, :, bi * 128:(bi + 1) * 128], pt_[:, 0:NR * 128].rearrange("p (a b) -> p a b", b=128)),
                 reads=[pt_], writes=[cg_] if bi == 0 else (), acc=() if bi == 0 else [cg_])
            pt_ = pT[nT % 2]
            nT += 1
            P.op("pe", lambda: T.transpose(pt_[0:64, 0:128], kp_[:, 0:64], identb[:, :]), reads=[kp_, identb], writes=[pt_])
            P.op("dve", lambda: V.tensor_copy(kg_[0:64, bi * 128:(bi + 1) * 128], pt_[0:64, 0:128]),
                 reads=[pt_], writes=[kg_] if bi == 0 else (), acc=() if bi == 0 else [kg_])
            vg_ = Vg[g % 2]
            for cg in range(NCG):
                pv_ = pV[nV % 2]
                nV += 1
                for cc in range(NR):
                    P.op("pe", lambda cc=cc, cg=cg: mm(pv_[:, 0:VN], cg_[:, cc, bi * 128:(bi + 1) * 128], WkV[:, cc, cg * VN:(cg + 1) * VN],
                                                       start=(cc == 0), stop=(cc == NR - 1)),
                         reads=[cg_, WkV], writes=[pv_] if cc == 0 else (), acc=() if cc == 0 else [pv_])
                first = (bi == 0 and cg == 0)
                P.op("act" if cg % 2 == 0 else "dve",
                     (lambda cg=cg: A.copy(vg_[:, cg * HPG:(cg + 1) * HPG, bi, :], pv_[:, 0:VN].rearrange("p (h d) -> p h d", d=128))) if cg % 2 == 0 else
                     (lambda cg=cg: V.tensor_copy(vg_[:, cg * HPG:(cg + 1) * HPG, bi, :], pv_[:, 0:VN].rearrange("p (h d) -> p h d", d=128))),
                     reads=[pv_], writes=[vg_] if first else (), acc=() if first else [vg_])
            if bi == 3:
                for h0 in range(0, H, KH):
                    ks_ = Kst[nK % 2]
                    nK += 1
                    for hh in range(KH):
                        h = h0 + hh
                        pk_ = pK[h % 2]
                        for cc in range(NR):
                            P.op("pe", lambda cc=cc, h=h: mm(pk_[:, :], WkK[:, cc, h * 128:(h + 1) * 128], cg_[:, cc, :],
                                                             start=(cc == 0), stop=(cc == NR - 1)),
                                 reads=[cg_, WkK], writes=[pk_] if cc == 0 else (), acc=() if cc == 0 else [pk_])
                        P.op("act" if h % 2 == 0 else "dve",
                             (lambda hh=hh: A.copy(ks_[:, hh, :], pk_[:, :])) if h % 2 == 0 else
                             (lambda hh=hh: V.tensor_copy(ks_[:, hh, :], pk_[:, :])),
                             reads=[pk_], writes=[ks_] if hh == 0 else (), acc=() if hh == 0 else [ks_])
                    P.dma("sp", KT_s_v[:, h0:h0 + KH, g * 512:(g + 1) * 512], ks_[:, :, :], reads=[ks_], wnow=[KT_s])
                P.dma("sp", KPE_s[:, g * 512:(g + 1) * 512], kg_[0:64, :], reads=[kg_], wnow=[KPE_s])
                P.dma("sp", V_s_v[:, :, 4 * g:4 * g + 4, :].rearrange("p h b d -> p h (b d)"),
                      vg_[:, :, :, :].rearrange("p h b d -> p h (b d)"), reads=[vg_], wnow=[V_s])
        P.barrier()

    if stop_after <= 1:
        st12.close(); st0.close()
        return nc, P
    OWN0 = NGD - NU
    with ExitStack() as st:
        Wsl = [P.sbuf(st, "Wsl%d" % i, [128, ND, 1024], BF16) for i in range(2)]
        hTg = [P.sbuf(st, "hTg%d" % i, [128, ND, 512], BF16) for i in range(2)]
        stg = [P.sbuf(st, "stg%d" % i, [128, 8, 512], BF16) for i in range(2)]
        rot = [P.sbuf(st, "rot%d" % i, [128, 1024], F32) for i in range(2)]
        qb = [P.sbuf(st, "qb%d" % i, [128, 1024], BF16) for i in range(2)]
        vst = [P.sbuf(st, "vst%d" % i, [128, DH, 4, 128], BF16) for i in range(2)]
        pF = [P.psum(st, "pF%d" % i, [128, 512], F32) for i in range(3)]
        pT2 = [P.psum(st, "pTb%d" % i, [128, 512], BF16) for i in range(2)]
        cnt = {"sl": 0, "hg": 0, "pf": 0, "st": 0, "rot": 0, "pt": 0, "vs": 0}

        def load_slab(pieces):
            sl = Wsl[cnt["sl"] % 2]
            cnt["sl"] += 1
            first = True
            for (d0, n, src) in pieces:
                P.dma("pool", sl[:, :, d0:d0 + n], src, writes=[sl] if first else (), wnow=() if first else [sl])
                first = False
            return sl

        def load_h(tg):
            hg = hTg[cnt["hg"] % 2]
            cnt["hg"] += 1
            P.dma("sp", hg[:, :, :], hT_s_v_[:, :, tg * 512:(tg + 1) * 512], reads=[hT_s], writes=[hg])
            return hg

        hT_s_v_ = hT_s[:, :, :].rearrange("c p t -> p c t")

        def fm_job(sl, ncols, func, dst_v, tgs):
            nch = ncols // 128
            for tg in tgs:
                hg = load_h(tg)
                sg = stg[cnt["st"] % 2]
                cnt["st"] += 1
                for cc in range(nch):
                    pf = pF[cnt["pf"] % 3]
                    cnt["pf"] += 1
                    for dc in range(ND):
                        P.op("pe", lambda dc=dc, cc=cc: mm(pf[:, :], sl[:, dc, cc * 128:(cc + 1) * 128], hg[:, dc, :],
                                                           start=(dc == 0), stop=(dc == ND - 1)),
                             reads=[sl, hg], writes=[pf] if dc == 0 else (), acc=() if dc == 0 else [pf])
                    P.op("act", lambda cc=cc: A.activation(sg[:, cc, :], pf[:, :], func), reads=[pf],
                         writes=[sg] if cc == 0 else (), acc=() if cc == 0 else [sg])
                t0 = (tg - OWN0) * 512
                P.dma("sp", dst_v[:, :, t0:t0 + 512], sg[:, 0:nch, :], reads=[sg], wnow=[dst_v_buf[0]])

        dst_v_buf = [None]

        def tm_block(sl, ncols, hg, bi):
            outs = []
            for hf in range((ncols + 511) // 512):
                n = min(512, ncols - hf * 512)
                pf = pF[cnt["pf"] % 3]
                cnt["pf"] += 1
                for dc in range(ND):
                    P.op("pe", lambda dc=dc, hf=hf, n=n: mm(pf[:, 0:n], hg[:, dc, bi * 128:(bi + 1) * 128], sl[:, dc, hf * 512:hf * 512 + n],
                                                            start=(dc == 0), stop=(dc == ND - 1)),
                         reads=[sl, hg], writes=[pf] if dc == 0 else (), acc=() if dc == 0 else [pf])
                outs.append((pf, n))
            return outs

        ROTM = [9]

        def rotary_tm(outs, nh, hd, half, r, step, q_):
            col = 0
            if ROTM[0] < 1:
                return
            for i, (pf, n) in enumerate(outs):
                P.op("act", lambda pf=pf, n=n, col=col: A.copy(q_[:, col:col + n], pf[:, 0:n]), reads=[pf],
                     writes=[q_] if i == 0 else (), acc=() if i == 0 else [q_])
                col += n
            if ROTM[0] < 2:
                return
            cb = COS[:, r, 0:32:step]
            sb = SIN[:, r, 0:32:step]
            col = 0
            first = True
            for i, (pf, n) in enumerate(outs):
                for hh in range(n // hd):
                    rt = rot[cnt["rot"] % 2]
                    cnt["rot"] += 1
                    b0 = hh * hd
                    x1, x2 = pf[:, b0:b0 + half], pf[:, b0 + half:b0 + 2 * half]
                    o1, o2 = q_[:, col + b0:col + b0 + half], q_[:, col + b0 + half:col + b0 + 2 * half]
                    h_ = half
                    P.op("dve", lambda: V.tensor_tensor(rt[:, 0:h_], x1, cb, ALU.mult), reads=[pf, COS], writes=[rt])
                    P.op("dve", lambda: V.tensor_tensor(rt[:, h_:2 * h_], x2, sb, ALU.mult), reads=[pf, SIN], acc=[rt])
                    P.op("dve", lambda: V.tensor_tensor(rt[:, 2 * h_:3 * h_], x2, cb, ALU.mult), reads=[pf, COS], acc=[rt])
                    P.op("dve", lambda: V.tensor_tensor(rt[:, 3 * h_:4 * h_], x1, sb, ALU.mult), reads=[pf, SIN], acc=[rt])
                    P.op("dve", lambda: V.tensor_tensor(o1, rt[:, 0:h_], rt[:, h_:2 * h_], ALU.subtract), reads=[rt],
                         writes=[q_] if first else (), acc=() if first else [q_])
                    P.op("dve", lambda: V.tensor_tensor(o2, rt[:, 2 * h_:3 * h_], rt[:, 3 * h_:4 * h_], ALU.add), reads=[rt], acc=[q_])
                    first = False
                col += n

        def transpose_to_stage(q_, nchunk, sg, bi, first):
            for c0 in range(0, nchunk, 4):
                nq = min(4, nchunk - c0)
                pt_ = pT2[cnt["pt"] % 2]
                cnt["pt"] += 1
                for a in range(nq):
                    cc = c0 + a
                    P.op("pe", lambda cc=cc, a=a: T.transpose(pt_[:, a * 128:(a + 1) * 128], q_[:, cc * 128:(cc + 1) * 128], identb[:, :]),
                         reads=[q_, identb], writes=[pt_] if a == 0 else (), acc=() if a == 0 else [pt_])
                src = pt_[:, 0:nq * 128].rearrange("p (a b) -> p a b", b=128)
                dst = sg[:, c0:c0 + nq, bi * 128:(bi + 1) * 128]
                fw = first and c0 == 0
                if (c0 // 4) % 2 == 0:
                    P.op("act", lambda: A.copy(dst, src), reads=[pt_], writes=[sg] if fw else (), acc=() if fw else [sg])
                else:
                    P.op("dve", lambda: V.tensor_copy(dst, src), reads=[pt_], writes=[sg] if fw else (), acc=() if fw else [sg])

        own_tgs = list(range(OWN0, NGD))
        all_tgs = list(range(NGD))
        if stop_after > 1.1:
            for h0 in range(0, H, 8):
                nh = min(8, H - h0)
                src = w_in_v[:, :, c.o_q + h0 * 192:c.o_q + (h0 + nh) * 192].rearrange("p c (h j) -> p c h j", j=192)
                sl = Wsl[cnt["sl"] % 2]
                cnt["sl"] += 1
                for dc in range(ND):
                    P.dma("pool", sl[:, dc, 0:nh * 128].rearrange("p (h j) -> p h j", j=128), src[:, dc, :, 0:128],
                          writes=[sl] if dc == 0 else (), wnow=() if dc == 0 else [sl])
                dst_v_buf[0] = QN_s
                fm_job(sl, nh * 128, AF.Copy, QN_s[h0:h0 + nh, :, :].rearrange("h p t -> p h t"), own_tgs)
        if stop_after > 1.3:
            for (o0, n_tot, func, dbuf) in ((c.o_zm, c.ZM, AF.Silu, ZM_s), (c.o_zd, c.ZD, AF.Silu, ZD_s),
                                             (c.o_glm, D, AF.Sigmoid, GLM_s), (c.o_gld, D, AF.Sigmoid, GLD_s)):
                for c0 in range(0, n_tot, 1024):
                    n = min(1024, n_tot - c0)
                    sl = load_slab([(0, n, w_in_v[:, :, o0 + c0:o0 + c0 + n])])
                    dst_v_buf[0] = dbuf
                    fm_job(sl, n, func, dbuf[c0 // 128:(c0 + n) // 128, :, :].rearrange("h p t -> p h t"), own_tgs)
        if stop_after > 1.5:
            for h0 in range(0, H, 16):
                nh = min(16, H - h0)
                src = w_in_v[:, :, c.o_q + h0 * 192:c.o_q + (h0 + nh) * 192].rearrange("p c (h j) -> p c h j", j=192)
                sl = Wsl[cnt["sl"] % 2]
                cnt["sl"] += 1
                for dc in range(ND):
                    P.dma("pool", sl[:, dc, 0:nh * 64].rearrange("p (h j) -> p h j", j=64), src[:, dc, :, 128:192],
                          writes=[sl] if dc == 0 else (), wnow=() if dc == 0 else [sl])
                QRM = 9
                for tg in (own_tgs if QRM >= 2 else []):
                    hg = load_h(tg)
                    sg = stg[cnt["st"] % 2]
                    cnt["st"] += 1
                    for bi in range(4):
                        r = (NB - NBD) + tg * 4 + bi
                        outs = tm_block(sl, nh * 64, hg, bi)
                        q_ = qb[(cnt.__setitem__("qb", cnt.get("qb", 0) + 1) or cnt["qb"]) % 2]
                        rotary_tm(outs, nh, 64, 32, r, 1, q_)
                        if QRM >= 3:
                            transpose_to_stage(q_, nh // 2, sg, bi, bi == 0)
                    t0 = (tg - OWN0) * 512
                    if QRM >= 4:
                        P.dma("sp", QR_s[h0 // 2:(h0 + nh) // 2, :, t0:t0 + 512].rearrange("h p t -> p h t"), sg[:, 0:nh // 2, :],
                              reads=[sg], wnow=[QR_s])
        if stop_after > 1.7:
            for which in range(3):
                for g in range(3):
                    o0 = c.o_qkvd + which * 3 * DV + g * DV
                    sl = load_slab([(0, DV, w_in_v[:, :, o0:o0 + DV])])
                    tgs = own_tgs if which == 0 else all_tgs
                    for tg in tgs:
                        hg = load_h(tg)
                        if which < 2:
                            sg = stg[cnt["st"] % 2]
                            cnt["st"] += 1
                        else:
                            vs = vst[cnt["vs"] % 2]
                            cnt["vs"] += 1
                        for bi in range(4):
                            r = (NB - NBD) + tg * 4 + bi
                            outs = tm_block(sl, DV, hg, bi)
                            if which < 2:
                                q_ = qb[(cnt.__setitem__("qb", cnt.get("qb", 0) + 1) or cnt["qb"]) % 2]
                                rotary_tm(outs, DH, 128, 16, r, 2, q_)
                                transpose_to_stage(q_, DH, sg, bi, bi == 0)
                            else:
                                col = 0
                                for i, (pf, n) in enumerate(outs):
                                    fw = (bi == 0 and i == 0)
                                    P.op("act", lambda pf=pf, n=n, col=col: A.copy(vs[:, col // 128:(col + n) // 128, bi, :],
                                                                                   pf[:, 0:n].rearrange("p (h d) -> p h d", d=128)), reads=[pf],
                                         writes=[vs] if fw else (), acc=() if fw else [vs])
                                    col += n
                        if which == 0:
                            t0 = (tg - OWN0) * 512
                            P.dma("sp", QD_s[g * DH:(g + 1) * DH, :, t0:t0 + 512].rearrange("h p t -> p h t"), sg[:, 0:DH, :],
                                  reads=[sg], wnow=[QD_s])
                        elif which == 1:
                            P.dma("sp", KD_s[g * DH:(g + 1) * DH, :, tg * 512:(tg + 1) * 512].rearrange("h p t -> p h t"), sg[:, 0:DH, :],
                                  reads=[sg], wnow=[KD_s])
                        else:
                            P.dma("sp", VD_s[g * DH:(g + 1) * DH, :, 4 * tg:4 * tg + 4, :].rearrange("h p b d -> p h (b d)"),
                                  vs[:, :, :, :].rearrange("p h b d -> p h (b d)"), reads=[vs], wnow=[VD_s])
        P.barrier()
    st12.close()

    if stop_after <= 2:
        st0.close()
        return nc, P

    KC = 16
    with ExitStack() as st:
        mmask = P.sbuf(st, "mmask", [128, 4, 512], BF16)
        P.dma("pool", mmask[:, :, :], mmask_d.rearrange("m p q -> p m q"), writes=[mmask])
        QNt = [P.sbuf(st, "QNt%d" % i, [128, 512], BF16) for i in range(2)]
        QRt = [P.sbuf(st, "QRt%d" % i, [64, 512], BF16) for i in range(2)]
        Zt = [P.sbuf(st, "Zt%d" % i, [128, 512], BF16) for i in range(2)]
        KTc = [P.sbuf(st, "KTc%d" % i, [128, KC * 128], BF16) for i in range(3)]
        KPc = [P.sbuf(st, "KPc%d" % i, [64, KC * 128], BF16) for i in range(3)]
        Vc = [P.sbuf(st, "Vc%d" % i, [128, KC, 128], BF16) for i in range(3)]
        ptb = [P.sbuf(st, "ptb%d" % i, [128, 512], BF16) for i in range(3)]
        rD = [P.sbuf(st, "rD%d" % i, [128, 512], F32) for i in range(2)]
        at = [P.sbuf(st, "at%d" % i, [128, 512], F32) for i in range(2)]
        ab = [P.sbuf(st, "ab%d" % i, [128, 512], BF16) for i in range(2)]
        pS = [P.psum(st, "pS%d" % i, [128, 512], F32) for i in range(3)]
        pO = [P.psum(st, "pO%d" % i, [128, 512], F32) for i in range(2)]
        pD = [P.psum(st, "pD%d" % i, [128, 512], F32) for i in range(2)]
        sc_mla = 192.0 ** -0.5
        jobs = [(m, h) for m in range(NU) for h in range(H)]
        chunks = []
        for ji, (m, h) in enumerate(jobs):
            nkb = (NB - CH // 128) + 4 * m + 4
            for c0 in range(0, nkb, KC):
                chunks.append((ji, h, c0, min(KC, nkb - c0)))

        def load_chunk(ci):
            ji, h, c0, n = chunks[ci]
            k_, p_, v_ = KTc[ci % 3], KPc[ci % 3], Vc[ci % 3]
            P.dma("sp", k_[:, 0:n * 128], KT_s[h, :, c0 * 128:(c0 + n) * 128], reads=[KT_s], writes=[k_])
            P.dma("sp", p_[:, 0:n * 128], KPE_s[:, c0 * 128:(c0 + n) * 128], reads=[KPE_s], writes=[p_])
            P.dma("sp", v_[:, 0:n, :], V_s[h, :, c0:c0 + n, :], reads=[V_s], writes=[v_])

        def load_q(ji):
            m, h = jobs[ji]
            t0 = m * 512
            P.dma("sp", QNt[ji % 2][:, :], QN_s[h, :, t0:t0 + 512], reads=[QN_s], writes=[QNt[ji % 2]])
            P.dma("sp", QRt[ji % 2][:, :], QR_s[h // 2, (h % 2) * 64:(h % 2) * 64 + 64, t0:t0 + 512], reads=[QR_s], writes=[QRt[ji % 2]])
            P.dma("sp", Zt[ji % 2][:, :], ZM_s[h, :, t0:t0 + 512], reads=[ZM_s], writes=[Zt[ji % 2]])

        load_q(0)
        load_chunk(0)
        if len(chunks) > 1:
            load_chunk(1)
        ci = 0
        nt = 0
        for ji, (m, h) in enumerate(jobs):
            if ji + 1 < len(jobs):
                load_q(ji + 1)
            qn, qr, zt = QNt[ji % 2], QRt[ji % 2], Zt[ji % 2]
            po, pd = pO[ji % 2], pD[ji % 2]
            nkb = (NB - CH // 128) + 4 * m + 4
            kb = 0
            while kb < nkb:
                _, _, c0, n = chunks[ci]
                k_, p_, v_ = KTc[ci % 3], KPc[ci % 3], Vc[ci % 3]
                for i in range(n):
                    kb = c0 + i
                    dj = kb - (nkb - 4)
                    ps = pS[nt % 3]
                    pt = ptb[nt % 3]
                    nt += 1
                    P.op("pe", lambda i=i: mm(ps[:, :], k_[:, i * 128:(i + 1) * 128], qn[:, :], start=True, stop=False),
                         reads=[k_, qn], writes=[ps])
                    P.op("pe", lambda i=i, dj=dj: mm(ps[:, :], p_[0:64, i * 128:(i + 1) * 128], qr[0:64, :], start=False, stop=(dj < 0)),
                         reads=[p_, qr], acc=[ps])
                    if dj >= 0:
                        P.op("pe", lambda dj=dj: mm(ps[:, :], identb[:, :], mmask[:, dj, :], start=False, stop=True),
                             reads=[identb, mmask], acc=[ps])
                    P.op("act", lambda kb=kb: A.activation(pt[:, :], ps[:, :], AF.Exp, bias=kbias[:, kb:kb + 1], scale=sc_mla),
                         reads=[ps, kbias], writes=[pt])
                    P.op("pe", lambda i=i, kb=kb: mm(po[:, :], v_[:, i, :], pt[:, :], start=(kb == 0), stop=(kb == nkb - 1)),
                         reads=[v_, pt], writes=[po] if kb == 0 else (), acc=() if kb == 0 else [po])
                    P.op("pe", lambda kb=kb: mm(pd[:, :], onesb[:, :], pt[:, :], start=(kb == 0), stop=(kb == nkb - 1)),
                         reads=[onesb, pt], writes=[pd] if kb == 0 else (), acc=() if kb == 0 else [pd])
                kb = c0 + n
                ci += 1
                if ci + 1 < len(chunks):
                    load_chunk(ci + 1)
            rd_, at_, ab_ = rD[ji % 2], at[ji % 2], ab[ji % 2]
            P.op("dve", lambda: V.reciprocal(rd_[:, :], pd[:, :]), reads=[pd], writes=[rd_])
            P.op("dve", lambda: V.tensor_tensor(at_[:, :], po[:, :], rd_[:, :], ALU.mult), reads=[po, rd_], writes=[at_])
            P.op("dve", lambda: V.tensor_tensor(ab_[:, :], at_[:, :], zt[:, :], ALU.mult), reads=[at_, zt], writes=[ab_])
            P.dma("sp", A_s[h, :, m * 512:(m + 1) * 512], ab_[:, :], reads=[ab_], wnow=[A_s])
        P.barrier()

    if stop_after <= 3:
        st0.close()
        return nc, P
    with ExitStack() as st:
        dmask = P.sbuf(st, "dmask", [128, sum(NDM), 512], BF16)
        for i in range(0, sum(NDM), 4):
            n = min(4, sum(NDM) - i)
            P.dma("pool", dmask[:, i:i + n, :], dmask_d[i:i + n].rearrange("m p q -> p m q"),
                  writes=[dmask] if i == 0 else (), wnow=() if i == 0 else [dmask])
        Qt = [P.sbuf(st, "Qt%d" % i, [128, 512], BF16) for i in range(3)]
        Kt = [P.sbuf(st, "Kt%d" % i, [128, 20 * 128], BF16) for i in range(3)]
        Vt = [P.sbuf(st, "Vt%d" % i, [128, 20, 128], BF16) for i in range(3)]
        Zt = [P.sbuf(st, "Zdt%d" % i, [128, 512], BF16) for i in range(2)]
        ptb = [P.sbuf(st, "ptd%d" % i, [128, 512], BF16) for i in range(3)]
        rD = [P.sbuf(st, "rDd%d" % i, [128, 512], F32) for i in range(2)]
        at = [P.sbuf(st, "atd%d" % i, [128, 512], F32) for i in range(2)]
        ab = [P.sbuf(st, "abd%d" % i, [128, 512], BF16) for i in range(2)]
        pS = [P.psum(st, "pSd%d" % i, [128, 512], F32) for i in range(3)]
        pO = [P.psum(st, "pOd%d" % i, [128, 512], F32) for i in range(2)]
        pD = [P.psum(st, "pDd%d" % i, [128, 512], F32) for i in range(2)]
        sc_dil = 128.0 ** -0.5
        moff = [0, NDM[0], NDM[0] + NDM[1]]
        nback = [1, 4, 16]
        gjobs = [(m, hd, g) for m in range(NU) for hd in range(DH) for g in range(3)]

        def load_g(gi):
            m, hd, g = gjobs[gi]
            gh = g * DH + hd
            nb_ = nback[g] + 4
            b0 = (DBACK // 128) + 4 * m - nback[g]
            P.dma("sp", Qt[gi % 3][:, :], QD_s[gh, :, m * 512:(m + 1) * 512], reads=[QD_s], writes=[Qt[gi % 3]])
            P.dma("sp", Kt[gi % 3][:, 0:nb_ * 128], KD_s[gh, :, b0 * 128:(b0 + nb_) * 128], reads=[KD_s], writes=[Kt[gi % 3]])
            P.dma("sp", Vt[gi % 3][:, 0:nb_, :], VD_s[gh, :, b0:b0 + nb_, :], reads=[VD_s], writes=[Vt[gi % 3]])

        load_g(0)
        load_g(1)
        nt = 0
        for gi, (m, hd, g) in enumerate(gjobs):
            if gi + 2 < len(gjobs):
                load_g(gi + 2)
            ji = gi // 3
            po, pd = pO[ji % 2], pD[ji % 2]
            if g == 0:
                P.dma("sp", Zt[ji % 2][:, :], ZD_s[hd, :, m * 512:(m + 1) * 512], reads=[ZD_s], writes=[Zt[ji % 2]])
            q_, k_, v_ = Qt[gi % 3], Kt[gi % 3], Vt[gi % 3]
            nb_ = nback[g] + 4
            b0 = (DBACK // 128) + 4 * m - nback[g]
            for i in range(nb_):
                ps = pS[nt % 3]
                pt = ptb[nt % 3]
                nt += 1
                first = (g == 0 and i == 0)
                last = (g == 2 and i == nb_ - 1)
                wb = (NB - NBD) + b0 + i
                P.op("pe", lambda i=i: mm(ps[:, :], k_[:, i * 128:(i + 1) * 128], q_[:, :], start=True, stop=False),
                     reads=[k_, q_], writes=[ps])
                P.op("pe", lambda i=i, g=g: mm(ps[:, :], identb[:, :], dmask[:, moff[g] + i, :], start=False, stop=True),
                     reads=[identb, dmask], acc=[ps])
                P.op("act", lambda wb=wb: A.activation(pt[:, :], ps[:, :], AF.Exp, bias=kbias[:, wb:wb + 1], scale=sc_dil),
                     reads=[ps, kbias], writes=[pt])
                P.op("pe", lambda i=i, first=first, last=last: mm(po[:, :], v_[:, i, :], pt[:, :], start=first, stop=last),
                     reads=[v_, pt], writes=[po] if first else (), acc=() if first else [po])
                P.op("pe", lambda first=first, last=last: mm(pd[:, :], onesb[:, :], pt[:, :], start=first, stop=last),
                     reads=[onesb, pt], writes=[pd] if first else (), acc=() if first else [pd])
            if g == 2:
                rd_, at_, ab_, zt = rD[ji % 2], at[ji % 2], ab[ji % 2], Zt[ji % 2]
                P.op("dve", lambda: V.reciprocal(rd_[:, :], pd[:, :]), reads=[pd], writes=[rd_])
                P.op("dve", lambda: V.tensor_tensor(at_[:, :], po[:, :], rd_[:, :], ALU.mult), reads=[po, rd_], writes=[at_])
                P.op("dve", lambda: V.tensor_tensor(ab_[:, :], at_[:, :], zt[:, :], ALU.mult), reads=[at_, zt], writes=[ab_])
                P.dma("sp", B_s[hd, :, m * 512:(m + 1) * 512], ab_[:, :], reads=[ab_], wnow=[B_s])
        P.barrier()

    if stop_after <= 4:
        st0.close()
        return nc, P
    with ExitStack() as st:
        Wm = P.sbuf(st, "Wm", [128, H, D], BF16)
        Wd = P.sbuf(st, "Wd", [128, DH, D], BF16)
        wm_v = w_mla.rearrange("(h p) n -> p h n", p=128)
        wd_v = w_dil.rearrange("(h p) n -> p h n", p=128)
        first = True
        for n0 in range(0, D, 1024):
            n = min(1024, D - n0)
            P.dma("pool", Wm[:, :, n0:n0 + n], wm_v[:, :, n0:n0 + n], writes=[Wm] if first else (), wnow=() if first else [Wm])
            P.dma("pool", Wd[:, :, n0:n0 + n], wd_v[:, :, n0:n0 + n], writes=[Wd] if first else (), wnow=() if first else [Wd])
            first = False
        aT = [P.sbuf(st, "aT%d" % i, [128, H, 512], BF16) for i in range(2)]
        bT = [P.sbuf(st, "bT%d" % i, [128, DH, 512], BF16) for i in range(2)]
        glm = [P.sbuf(st, "glm%d" % i, [128, 512], BF16) for i in range(2)]
        gld = [P.sbuf(st, "gld%d" % i, [128, 512], BF16) for i in range(2)]
        t1 = [P.sbuf(st, "t1%d" % i, [128, 512], F32) for i in range(2)]
        t2 = [P.sbuf(st, "t2%d" % i, [128, 512], F32) for i in range(2)]
        mst = [P.sbuf(st, "mst%d" % i, [128, ND, 512], BF16) for i in range(2)]
        pY = [P.psum(st, "pY%d" % i, [128, 512], F32) for i in range(2)]
        pZ = [P.psum(st, "pZ%d" % i, [128, 512], F32) for i in range(2)]
        k = 0
        for m in range(NU):
            a_, b_, ms = aT[m % 2], bT[m % 2], mst[m % 2]
            P.dma("sp", a_[:, :, :], A_s[:, :, m * 512:(m + 1) * 512].rearrange("h p t -> p h t"), reads=[A_s], writes=[a_])
            P.dma("sp", b_[:, :, :], B_s[:, :, m * 512:(m + 1) * 512].rearrange("h p t -> p h t"), reads=[B_s], writes=[b_])
            for cc in range(ND):
                gm, gd, py, pz, u1, u2 = glm[k % 2], gld[k % 2], pY[k % 2], pZ[k % 2], t1[k % 2], t2[k % 2]
                k += 1
                P.dma("sp", gm[:, :], GLM_s[cc, :, m * 512:(m + 1) * 512], reads=[GLM_s], writes=[gm])
                P.dma("sp", gd[:, :], GLD_s[cc, :, m * 512:(m + 1) * 512], reads=[GLD_s], writes=[gd])
                for h in range(H):
                    P.op("pe", lambda h=h, cc=cc: mm(py[:, :], Wm[:, h, cc * 128:(cc + 1) * 128], a_[:, h, :], start=(h == 0), stop=(h == H - 1)),
                         reads=[Wm, a_], writes=[py] if h == 0 else (), acc=() if h == 0 else [py])
                for h in range(DH):
                    P.op("pe", lambda h=h, cc=cc: mm(pz[:, :], Wd[:, h, cc * 128:(cc + 1) * 128], b_[:, h, :], start=(h == 0), stop=(h == DH - 1)),
                         reads=[Wd, b_], writes=[pz] if h == 0 else (), acc=() if h == 0 else [pz])
                P.op("dve", lambda: V.tensor_tensor(u1[:, :], py[:, :], gm[:, :], ALU.mult), reads=[py, gm], writes=[u1])
                P.op("dve", lambda: V.tensor_tensor(u2[:, :], pz[:, :], gd[:, :], ALU.mult), reads=[pz, gd], writes=[u2])
                P.op("dve", lambda cc=cc: V.tensor_tensor(ms[:, cc, :], u1[:, :], u2[:, :], ALU.add), reads=[u1, u2],
                     writes=[ms] if cc == 0 else (), acc=() if cc == 0 else [ms])
            P.dma("sp", M_s[:, :, m * 512:(m + 1) * 512].rearrange("c p t -> p c t"), ms[:, :, :], reads=[ms], wnow=[M_s])
        P.barrier()

    with ExitStack() as st:
        Wo = P.sbuf(st, "Wo", [128, ND, D], BF16)
        wo_v = w_o.rearrange("(c p) n -> p c n", p=128)
        first = True
        for n0 in range(0, D, 1024):
            n = min(1024, D - n0)
            P.dma("pool", Wo[:, :, n0:n0 + n], wo_v[:, :, n0:n0 + n], writes=[Wo] if first else (), wnow=() if first else [Wo])
            first = False
        GGb = P.sbuf(st, "GGb", [128, D], F32)
        P.dma("sp", GGb[:, :], modrows[2].partition_broadcast(128), reads=[modrows], writes=[GGb])
        mT = [P.sbuf(st, "mT%d" % i, [128, ND, 512], BF16) for i in range(2)]
        xb = [P.sbuf(st, "xo%d" % i, [128, D], F32) for i in range(2)]
        ob = [P.sbuf(st, "ob%d" % i, [128, D], F32) for i in range(2)]
        jb = P.sbuf(st, "jb", [128, D], BF16)
        ss = [P.sbuf(st, "sso%d" % i, [128, 1], F32) for i in range(2)]
        rs = [P.sbuf(st, "rso%d" % i, [128, 1], F32) for i in range(2)]
        NO = min(512, D)
        NOG = D // NO
        pOut = [P.psum(st, "pOut%d" % i, [128, NO], F32) for i in range(min(8, 2 * NOG))]
        k = 0
        for m in range(NU):
            mt = mT[m % 2]
            P.dma("sp", mt[:, :, :], M_s[:, :, m * 512:(m + 1) * 512].rearrange("c p t -> p c t"), reads=[M_s], writes=[mt])
            for bi in range(4):
                row = (W - CH) + m * 512 + bi * 128
                x_, o_, ss_, rs_ = xb[k % 2], ob[k % 2], ss[k % 2], rs[k % 2]
                P.dma("sp", x_[:, :], xw[row:row + 128, :], writes=[x_])
                pss = []
                for og in range(NOG):
                    po = pOut[(k * NOG + og) % len(pOut)]
                    pss.append(po)
                    for cc in range(ND):
                        P.op("pe", lambda cc=cc, og=og: mm(po[:, :], mt[:, cc, bi * 128:(bi + 1) * 128], Wo[:, cc, og * NO:(og + 1) * NO],
                                                           start=(cc == 0), stop=(cc == ND - 1)),
                             reads=[mt, Wo], writes=[po] if cc == 0 else (), acc=() if cc == 0 else [po])
                    P.op("act" if og % 2 == 0 else "dve",
                         (lambda og=og: A.copy(o_[:, og * NO:(og + 1) * NO], po[:, :])) if og % 2 == 0 else
                         (lambda og=og: V.tensor_copy(o_[:, og * NO:(og + 1) * NO], po[:, :])),
                         reads=[po], writes=[o_] if og == 0 else (), acc=() if og == 0 else [o_])
                P.op("act", lambda: A.activation(jb[:, :], o_[:, :], AF.Square, accum_out=ss_[:, :]), reads=[o_], writes=[jb, ss_])
                rs_from_ss(ss_, rs_, D)
                P.op("dve", lambda: V.scalar_tensor_tensor(o_[:, :], o_[:, :], rs_[:, :], GGb[:, :], ALU.mult, ALU.mult),
                     reads=[rs_, GGb], writes=[o_])
                P.op("dve", lambda: V.tensor_tensor(o_[:, :], o_[:, :], x_[:, :], ALU.add), reads=[x_], writes=[o_])
                orow = m * 512 + bi * 128
                P.dma("sp", y[orow:orow + 128, :], o_[:, :], reads=[o_], wnow=[yB])
                k += 1
        P.barrier()
    st0.close()
    return nc, P


def make_masks():
    kk = np.arange(128)[:, None]
    qq = np.arange(512)[None, :]
    mm_ = np.stack([np.where(j * 128 + kk <= qq, 0.0, NEG) for j in range(4)]).astype(np.float32)
    dms = []
    for (win, d), nb in zip(DIL, (1, 4, 16)):
        for i in range(nb + 4):
            o = i - nb
            delta = qq - 128 * o - kk
            ok = (delta >= 0) & (delta % d == 0) & (delta <= win)
            dms.append(np.where(ok, 0.0, NEG))
    return mm_, np.stack(dms).astype(np.float32)


def make_in_maps(cfg, x, c, positions, w_ada, b_ada, g_pre, g_post, w_in, g_kv, w_kv_b, w_mla_proj, w_dil_proj, w_o):
    cf = cfg
    B = x.shape[0]
    mmask, dmask = make_masks()
    ident = np.eye(128, dtype=np.float32)
    freq = np.tile((1.0 / (THETA ** (np.arange(0, 64, 2, dtype=np.float32) / 64.0))).astype(np.float32)[None, :], (128, 1))
    shared = {
        "b_ada": np.ascontiguousarray(b_ada[0][None, :]), "g_pre": np.ascontiguousarray(g_pre[0][None, :]),
        "g_post": np.ascontiguousarray(g_post[0][None, :]), "g_kv": np.ascontiguousarray(g_kv[0][None, :]),
        "w_ada": np.ascontiguousarray(w_ada[0]), "w_in": np.ascontiguousarray(w_in[0]),
        "w_kv_b": np.ascontiguousarray(w_kv_b[0]), "w_mla_proj": np.ascontiguousarray(w_mla_proj[0]),
        "w_dil_proj": np.ascontiguousarray(w_dil_proj[0]), "w_o": np.ascontiguousarray(w_o[0]),
        "ident": ident, "freq": np.ascontiguousarray(freq), "mmask": mmask, "dmask": dmask,
    }
    maps = []
    for b in range(B):
        for j in range(cf.NCB):
            t_end = cf.CH * (j + 1)
            t0 = t_end - cf.W
            nz = max(0, -t0)
            xw = np.zeros((cf.W, cf.D), np.float32)
            xw[nz:] = x[b, max(t0, 0):t_end]
            pw = np.zeros((cf.W,), np.int32)
            pw[nz:] = positions[b, max(t0, 0):t_end]
            kb = np.zeros((cf.W,), np.float32)
            kb[:nz] = NEG
            m = dict(shared)
            m["xw"] = xw
            m["posw"] = np.ascontiguousarray(pw.reshape(cf.NB, 128).T)
            m["kbias"] = np.ascontiguousarray(kb.reshape(cf.NB, 128).T)
            m["c_col"] = np.ascontiguousarray(c[b].reshape(cf.ND, 128).T)
            maps.append(m)
    return maps


_CACHE = {}


def kernel(x, c, positions, w_ada, b_ada, g_pre, g_post, w_in, g_kv, w_kv_b, w_mla_proj, w_dil_proj, w_o):
    cfg = Cfg()
    t0 = time.time()
    args = [np.asarray(a) for a in (x, c, positions, w_ada, b_ada, g_pre, g_post, w_in, g_kv, w_kv_b, w_mla_proj, w_dil_proj, w_o)]
    maps = make_in_maps(cfg, *args)
    if "nc" not in _CACHE:
        _CACHE["nc"] = build(cfg)[0]
    nc = _CACHE["nc"]
    print("[kernel] build+layout %.1fs" % (time.time() - t0), flush=True)
    res = run_bass_kernel_spmd(nc, maps, core_ids=list(range(len(maps))))
    print("[kernel] launch done %.1fs" % (time.time() - t0), flush=True)
    B = args[0].shape[0]
    out = np.empty((B, cfg.S, cfg.D), np.float32)
    i = 0
    for b in range(B):
        for j in range(cfg.NCB):
            out[b, cfg.CH * j:cfg.CH * (j + 1)] = res.results[i]["y"]
            i += 1
    return out
```

```python
import math
import time
from contextlib import ExitStack

import numpy as np
import concourse.bass as bass
import concourse.mybir as mybir
from concourse.bass_utils import run_bass_kernel_spmd

F32 = mybir.dt.float32
BF16 = mybir.dt.bfloat16
I32 = mybir.dt.int32
AF = mybir.ActivationFunctionType
ALU = mybir.AluOpType

NEG = -30000.0
EPS = 1e-6
THETA = 500000.0
DIL = ((128, 1), (512, 4), (2048, 16))
DBACK = 2048


class Cfg:
    def __init__(s, D=2048, H=16, R=512, DH=8, S=16384, NCB=4):
        s.D, s.H, s.R, s.DH, s.S, s.NCB = D, H, R, DH, S, NCB
        s.ND, s.NR = D // 128, R // 128
        s.CH = S // NCB
        s.W = S
        s.NU = s.CH // 512
        s.NB = s.W // 128
        s.DW = DBACK + s.CH
        s.NBD = s.DW // 128
        s.NGD = s.DW // 512
        s.QM, s.KVA, s.ZM, s.QKVD, s.ZD = H * 192, R + 64, H * 128, 9 * DH * 128, DH * 128
        s.o_q = 0
        s.o_kva = s.QM
        s.o_zm = s.o_kva + s.KVA
        s.o_qkvd = s.o_zm + s.ZM
        s.o_zd = s.o_qkvd + s.QKVD
        s.o_glm = s.o_zd + s.ZD
        s.o_gld = s.o_glm + D
        s.IN = s.o_gld + D
        s.HV = H * 128
        s.DV = DH * 128


class Buf:
    __slots__ = ("t", "name", "writes", "reads", "excl")

    def __init__(self, t, name, excl=False):
        self.t, self.name, self.writes, self.reads, self.excl = t, name, {}, {}, excl

    def __getitem__(self, idx):
        return self.t[idx]


class Prog:
    NDMA = 16
    GEN = 20000

    def __init__(self, nc, stack):
        self.nc = nc
        self.eng = {"pe": nc.tensor, "act": nc.scalar, "dve": nc.vector, "pool": nc.gpsimd, "sp": nc.sync}
        self.semobj, self.cnt, self.ring, self.ring_pos = {}, {}, {}, {}
        self.stack = stack
        self.gen = {}
        for e in ("pe", "act", "dve", "pool"):
            self.gen[e] = 0
            self.semobj[("e", e, 0)] = stack.enter_context(nc.semaphore("s_" + e))
            self.cnt[e] = 0
        for q in ("sp", "pool"):
            self.ring[q] = []
            for i in range(self.NDMA):
                s = stack.enter_context(nc.semaphore("d_%s%d" % (q, i)))
                self.semobj[("d", q, i)] = s
                self.ring[q].append(0)
            self.ring_pos[q] = 0
        self.seen = {e: {} for e in self.eng}
        self.ninst = 0
        self.uid = 0

    def sbuf(self, st, name, shape, dtype):
        self.uid += 1
        return Buf(st.enter_context(self.nc.sbuf_tensor("%s_%d" % (name, self.uid), list(shape), dtype)), name)

    def psum(self, st, name, shape, dtype):
        self.uid += 1
        if dtype == BF16 and shape[-1] * 2 < 2048:
            shape = list(shape[:-1]) + [1024]
        return Buf(st.enter_context(self.nc.psum_tensor("%s_%d" % (name, self.uid), list(shape), dtype)), name, True)

    def dram(self, name, shape, dtype, kind="Internal"):
        return Buf(self.nc.dram_tensor(name, list(shape), dtype, kind=kind).ap(), name)

    def _wait(self, e, key, val):
        if val <= 0 or self.seen[e].get(key, 0) >= val:
            return
        self.seen[e][key] = val
        self.eng[e].wait_ge(self.semobj[key], val)

    def _deps(self, e, reads, writes, wnow, skip_self):
        deps = {}
        for b in reads:
            for k, v in b.writes.items():
                if deps.get(k, 0) < v:
                    deps[k] = v
            if b.excl:
                for k, v in b.reads.items():
                    if deps.get(k, 0) < v:
                        deps[k] = v
        for b in writes:
            for d in (b.writes, b.reads):
                for k, v in d.items():
                    if deps.get(k, 0) < v:
                        deps[k] = v
        for b in wnow:
            for k, v in b.reads.items():
                if deps.get(k, 0) < v:
                    deps[k] = v
        for k, v in deps.items():
            if skip_self and k[0] == "e" and k[1] == e:
                continue
            self._wait(e, k, v)

    def _commit(self, key, val, reads, writes, wnow):
        for b in reads:
            if b.reads.get(key, 0) < val:
                b.reads[key] = val
        for b in writes:
            b.writes = {key: val}
            b.reads = {}
        for b in wnow:
            b.writes[key] = val

    def op(self, e, fn, reads=(), writes=(), acc=()):
        self._deps(e, list(reads) + list(acc), writes, (), e == "pe")
        inst = fn()
        if self.cnt[e] >= self.GEN:
            self.gen[e] += 1
            self.cnt[e] = 0
            self.semobj[("e", e, self.gen[e])] = self.stack.enter_context(self.nc.semaphore("s_%s%d" % (e, self.gen[e])))
        self.cnt[e] += 1
        key = ("e", e, self.gen[e])
        inst.then_inc(self.semobj[key], 1)
        self._commit(key, self.cnt[e], reads, writes, acc)
        self.ninst += 1

    def dma(self, q, out, in_, reads=(), writes=(), wnow=(), **kw):
        pos = self.ring_pos[q]
        self.ring_pos[q] = (pos + 1) % self.NDMA
        key = ("d", q, pos)
        self._wait(q, key, self.ring[q][pos])
        self._deps(q, reads, writes, wnow, False)
        inst = self.eng[q].dma_start(out=out, in_=in_, **kw)
        self.ring[q][pos] += 16
        inst.then_inc(self.semobj[key], 16)
        self._commit(key, self.ring[q][pos], reads, writes, wnow)
        self.ninst += 1

    def barrier(self):
        for e in self.eng:
            for k in list(self.semobj):
                if k[0] == "e":
                    if k[2] != self.gen[k[1]]:
                        continue
                    v = self.cnt[k[1]]
                else:
                    v = self.ring[k[1]][k[2]]
                self._wait(e, k, v)


def build(cfg, stop_after=99):
    c = cfg
    D, H, R, DH, ND, NR, NB, W, CH, NU = c.D, c.H, c.R, c.DH, c.ND, c.NR, c.NB, c.W, c.CH, c.NU
    DW, NBD, NGD, HV, DV = c.DW, c.NBD, c.NGD, c.HV, c.DV
    nc = bass.Bass("TRN2", target_bir_lowering=False)

    def din(name, shape, dt=F32):
        return nc.dram_tensor(name, list(shape), dt, kind="ExternalInput").ap()

    xw = din("xw", [W, D])
    posw = din("posw", [128, NB], I32)
    kbias_d = din("kbias", [128, NB])
    c_col = din("c_col", [128, ND])
    b_ada = din("b_ada", [1, 3 * D])
    g_pre = din("g_pre", [1, D])
    g_post = din("g_post", [1, D])
    g_kv = din("g_kv", [1, R])
    w_ada = din("w_ada", [D, 3 * D])
    w_in = din("w_in", [D, c.IN])
    w_kvb = din("w_kv_b", [R, H * 256])
    w_mla = din("w_mla_proj", [HV, D])
    w_dil = din("w_dil_proj", [DV, D])
    w_o = din("w_o", [D, D])
    ident_d = din("ident", [128, 128])
    freq_d = din("freq", [128, 32])
    mmask_d = din("mmask", [4, 128, 512])
    NDM = [5, 8, 20]
    dmask_d = din("dmask", [sum(NDM), 128, 512])
    y = nc.dram_tensor("y", [CH, D], F32, kind="ExternalOutput").ap()

    st0 = ExitStack()
    P = Prog(nc, st0)
    yB = Buf(y, "y")
    modrows = P.dram("modrows", [3, D], F32)
    hT_s = P.dram("hT_s", [ND, 128, DW], BF16)
    KT_s = P.dram("KT_s", [H, 128, W], BF16)
    KPE_s = P.dram("KPE_s", [64, W], BF16)
    V_s = P.dram("V_s", [H, 128, NB, 128], BF16)
    QN_s = P.dram("QN_s", [H, 128, CH], BF16)
    QR_s = P.dram("QR_s", [H // 2, 128, CH], BF16)
    ZM_s = P.dram("ZM_s", [H, 128, CH], BF16)
    ZD_s = P.dram("ZD_s", [DH, 128, CH], BF16)
    GLM_s = P.dram("GLM_s", [ND, 128, CH], BF16)
    GLD_s = P.dram("GLD_s", [ND, 128, CH], BF16)
    QD_s = P.dram("QD_s", [3 * DH, 128, CH], BF16)
    KD_s = P.dram("KD_s", [3 * DH, 128, DW], BF16)
    VD_s = P.dram("VD_s", [3 * DH, 128, NBD, 128], BF16)
    A_s = P.dram("A_s", [H, 128, CH], BF16)
    B_s = P.dram("B_s", [DH, 128, CH], BF16)
    M_s = P.dram("M_s", [ND, 128, CH], BF16)

    T = nc.tensor
    A = nc.scalar
    V = nc.vector
    mm = T.matmul

    identb = P.sbuf(st0, "identb", [128, 128], BF16)
    onesb = P.sbuf(st0, "onesb", [128, 128], BF16)
    kbias = P.sbuf(st0, "kbias", [128, NB], F32)
    P.dma("pool", identb[:, :], ident_d[:, :], writes=[identb])
    P.dma("sp", kbias[:, :], kbias_d[:, :], writes=[kbias])
    P.op("dve", lambda: V.memset(onesb[:, :], 1.0), writes=[onesb])

    def rs_from_ss(ss, rs, n):
        P.op("act", lambda: A.activation(rs[:, :], ss[:, :], AF.Sqrt, bias=epsb[:, :], scale=1.0 / n),
             reads=[ss, epsb], writes=[rs])
        P.op("dve", lambda: V.reciprocal(rs[:, :], rs[:, :]), reads=[rs], writes=[rs])

    epsb = P.sbuf(st0, "epsb", [128, 1], F32)
    P.op("dve", lambda: V.memset(epsb[:, :], EPS), writes=[epsb])
    npib = P.sbuf(st0, "npib", [128, 1], F32)
    P.op("dve", lambda: V.memset(npib[:, :], -math.pi), writes=[npib])

    st12 = ExitStack()
    COS = P.sbuf(st12, "COS", [128, NB, 32], F32)
    SIN = P.sbuf(st12, "SIN", [128, NB, 32], F32)
    with ExitStack() as st:
        sc = P.sbuf(st, "sc", [128, ND], F32)
        P.dma("sp", sc[:, :], c_col[:, :], writes=[sc])
        P.op("act", lambda: A.activation(sc[:, :], sc[:, :], AF.Silu), reads=[sc], writes=[sc])
        GA = 512 if (3 * D) % 512 == 0 else (3 * D) // 2
        slab = [P.sbuf(st, "aslab%d" % i, [128, ND, GA], F32) for i in range(2)]
        modrow = P.sbuf(st, "modrow", [1, 3 * D], F32)
        rowt = P.sbuf(st, "rowt", [1, 3 * D], F32)
        pm = [P.psum(st, "pm%d" % i, [128, 512], F32) for i in range(2)]
        w_ada_v = w_ada.rearrange("(c p) n -> p c n", p=128)
        ng = 3 * D // GA
        for g in range(ng):
            sl = slab[g % 2]
            P.dma("sp", sl[:, :, :], w_ada_v[:, :, g * GA:(g + 1) * GA], writes=[sl])
            pp = pm[g % 2]
            for dc in range(ND):
                P.op("pe", lambda dc=dc: mm(pp[0:1, 0:GA], sc[:, dc:dc + 1], sl[:, dc, :], start=(dc == 0), stop=(dc == ND - 1)),
                     reads=[sc, sl], writes=[pp] if dc == 0 else (), acc=() if dc == 0 else [pp])
            P.op("dve", lambda g=g: V.tensor_copy(modrow[0:1, g * GA:(g + 1) * GA], pp[0:1, 0:GA]),
                 reads=[pp], acc=[modrow])
        P.dma("sp", rowt[0:1, :], b_ada[:, :], writes=[rowt])
        P.op("dve", lambda: V.tensor_tensor(modrow[0:1, :], modrow[0:1, :], rowt[0:1, :], ALU.add),
             reads=[rowt], writes=[modrow])
        P.dma("sp", rowt[0:1, 0:D], g_pre[:, :], reads=[], writes=[rowt])
        P.dma("sp", rowt[0:1, D:2 * D], g_post[:, :], wnow=[rowt])
        P.op("dve", lambda: V.scalar_tensor_tensor(modrow[0:1, D:2 * D], modrow[0:1, D:2 * D], 1.0, rowt[0:1, 0:D],
                                                   ALU.add, ALU.mult), reads=[rowt], writes=[modrow])
        P.op("dve", lambda: V.tensor_tensor(modrow[0:1, 2 * D:3 * D], modrow[0:1, 2 * D:3 * D], rowt[0:1, D:2 * D], ALU.mult),
             reads=[rowt], writes=[modrow])
        P.dma("sp", modrows[0:1, :], modrow[0:1, D:2 * D], reads=[modrow], wnow=[modrows])
        P.dma("sp", modrows[1:2, :], modrow[0:1, 0:D], reads=[modrow], wnow=[modrows])
        P.dma("sp", modrows[2:3, :], modrow[0:1, 2 * D:3 * D], reads=[modrow], wnow=[modrows])
        P.barrier()
    with ExitStack() as st:
        posi = P.sbuf(st, "posi", [128, NB], I32)
        posf = P.sbuf(st, "posf", [128, NB], F32)
        freq = P.sbuf(st, "freq", [128, 32], F32)
        ang = P.sbuf(st, "ang", [128, NB, 32], F32)
        P.dma("sp", posi[:, :], posw[:, :], writes=[posi])
        P.dma("sp", freq[:, :], freq_d[:, :], writes=[freq])
        P.op("dve", lambda: V.tensor_copy(posf[:, :], posi[:, :]), reads=[posi], writes=[posf])
        for b in range(NB):
            P.op("dve", lambda b=b: V.tensor_scalar(ang[:, b, :], freq[:, :], posf[:, b:b + 1], None, ALU.mult),
                 reads=[posf, freq], writes=[ang] if b == 0 else (), acc=() if b == 0 else [ang])
        angf = ang[:, :, :].rearrange("p a b -> p (a b)")
        ti = P.sbuf(st, "ti", [128, NB * 32], I32)
        nf = P.sbuf(st, "nf", [128, NB * 32], F32)
        msk = P.sbuf(st, "msk", [128, NB * 32], F32)
        C1 = 6.28125
        C2 = 2.0 * math.pi - C1
        PI_LO = 3.1415925
        for tab, sh in ((SIN, 0.0), (COS, 0.5 * math.pi)):
            tf = tab[:, :, :].rearrange("p a b -> p (a b)")
            P.op("dve", lambda tf=tf, sh=sh: V.tensor_scalar(tf, angf, sh, None, ALU.add), reads=[ang], writes=[tab])
            P.op("dve", lambda tf=tf: V.tensor_scalar(nf[:, :], tf, 1.0 / (2.0 * math.pi), None, ALU.mult), reads=[tab], writes=[nf])
            P.op("dve", lambda: V.tensor_copy(ti[:, :], nf[:, :]), reads=[nf], writes=[ti])
            P.op("dve", lambda: V.tensor_copy(nf[:, :], ti[:, :]), reads=[ti], writes=[nf])
            P.op("dve", lambda tf=tf: V.scalar_tensor_tensor(tf, nf[:, :], -C1, tf, ALU.mult, ALU.add), reads=[nf], writes=[tab])
            P.op("dve", lambda tf=tf: V.scalar_tensor_tensor(tf, nf[:, :], -C2, tf, ALU.mult, ALU.add), reads=[nf], writes=[tab])
            P.op("dve", lambda tf=tf: V.tensor_scalar(msk[:, :], tf, math.pi, None, ALU.is_gt), reads=[tab], writes=[msk])
            P.op("dve", lambda tf=tf: V.scalar_tensor_tensor(tf, msk[:, :], -2.0 * math.pi, tf, ALU.mult, ALU.add), reads=[msk], writes=[tab])
            P.op("dve", lambda tf=tf: V.tensor_scalar(msk[:, :], tf, -math.pi, None, ALU.is_lt), reads=[tab], writes=[msk])
            P.op("dve", lambda tf=tf: V.scalar_tensor_tensor(tf, msk[:, :], 2.0 * math.pi, tf, ALU.mult, ALU.add), reads=[msk], writes=[tab])
            P.op("dve", lambda tf=tf: V.tensor_scalar(tf, tf, PI_LO, -PI_LO, ALU.min, ALU.max), reads=[], writes=[tab])
            P.op("act", lambda tf=tf: A.activation(tf, tf, AF.Sin), reads=[], writes=[tab])
        P.barrier()

    if stop_after <= 0:
        st12.close(); st0.close()
        return nc, P
    w_in_v = w_in.rearrange("(c p) n -> p c n", p=128)
    VN = min(512, HV)
    NCG = HV // VN
    HPG = VN // 128
    KH = min(8, H)
    with ExitStack() as st:
        G1b = P.sbuf(st, "G1b", [128, D], F32)
        SHb = P.sbuf(st, "SHb", [128, D], F32)
        gkvb = P.sbuf(st, "gkvb", [128, R], F32)
        P.dma("sp", G1b[:, :], modrows[0].partition_broadcast(128), reads=[modrows], writes=[G1b])
        P.dma("sp", SHb[:, :], modrows[1].partition_broadcast(128), reads=[modrows], writes=[SHb])
        P.dma("sp", gkvb[:, :], g_kv[0].partition_broadcast(128), writes=[gkvb])
        Wkva = P.sbuf(st, "Wkva", [128, ND, c.KVA], BF16)
        P.dma("pool", Wkva[:, :, :], w_in_v[:, :, c.o_kva:c.o_kva + c.KVA], writes=[Wkva])
        WkK = P.sbuf(st, "WkK", [128, NR, HV], BF16)
        WkV = P.sbuf(st, "WkV", [128, NR, HV], BF16)
        wkv_v = w_kvb.rearrange("(c p) (h e) -> p c h e", p=128, e=256)
        for cc in range(NR):
            P.dma("pool", WkK[:, cc, :].rearrange("p (h e) -> p h e", e=128), wkv_v[:, cc, :, 0:128],
                  writes=[WkK] if cc == 0 else (), wnow=() if cc == 0 else [WkK])
            P.dma("pool", WkV[:, cc, :].rearrange("p (h e) -> p h e", e=128), wkv_v[:, cc, :, 128:256],
                  writes=[WkV] if cc == 0 else (), wnow=() if cc == 0 else [WkV])
        xb = [P.sbuf(st, "xb%d" % i, [128, D], F32) for i in range(2)]
        hb = [P.sbuf(st, "hb%d" % i, [128, D], BF16) for i in range(2)]
        ss = [P.sbuf(st, "ss%d" % i, [128, 1], F32) for i in range(2)]
        rs = [P.sbuf(st, "rs%d" % i, [128, 1], F32) for i in range(2)]
        ss2 = [P.sbuf(st, "ssb%d" % i, [128, 1], F32) for i in range(2)]
        rs2 = [P.sbuf(st, "rsb%d" % i, [128, 1], F32) for i in range(2)]
        hTb = [P.sbuf(st, "hTb%d" % i, [128, ND, 128], BF16) for i in range(4)]
        ckvb = [P.sbuf(st, "ckvb%d" % i, [128, R], BF16) for i in range(2)]
        junk = P.sbuf(st, "junk", [128, R], BF16)
        Tt = [P.sbuf(st, "Tt%d" % i, [128, 128], F32) for i in range(2)]
        kpeb = [P.sbuf(st, "kpeb%d" % i, [128, 64], BF16) for i in range(2)]
        ckvTg = [P.sbuf(st, "ckvTg%d" % i, [128, NR, 512], BF16) for i in range(2)]
        kpeTg = [P.sbuf(st, "kpeTg%d" % i, [64, 512], BF16) for i in range(2)]
        Vg = [P.sbuf(st, "Vg%d" % i, [128, H, 4, 128], BF16) for i in range(2)]
        Kst = [P.sbuf(st, "Kst%d" % i, [128, KH, 512], BF16) for i in range(2)]
        pT = [P.psum(st, "pT%d" % i, [128, 512], BF16) for i in range(2)]
        pkv = P.psum(st, "pkv", [128, 512], F32)
        pkp = P.psum(st, "pkp", [128, 512], F32)
        pV = [P.psum(st, "pV%d" % i, [128, 512], F32) for i in range(2)]
        pK = [P.psum(st, "pK%d" % i, [128, 512], F32) for i in range(2)]
        nT = 0
        nV = 0
        nK = 0
        V_s_v = V_s[:, :, :, :].rearrange("h p b d -> p h b d")
        KT_s_v = KT_s[:, :, :].rearrange("h p t -> p h t")
        hT_s_v = hT_s[:, :, :].rearrange("c p t -> p c t")
        for r in range(NB):
            g, bi = r // 4, r % 4
            x_, h_, ss_, rs_ = xb[r % 2], hb[r % 2], ss[r % 2], rs[r % 2]
            P.dma("sp", x_[:, :], xw[r * 128:(r + 1) * 128, :], writes=[x_])
            P.op("act", lambda: A.activation(h_[:, :], x_[:, :], AF.Square, accum_out=ss_[:, :]),
                 reads=[x_], writes=[h_, ss_])
            rs_from_ss(ss_, rs_, D)
            P.op("dve", lambda: V.scalar_tensor_tensor(x_[:, :], x_[:, :], rs_[:, :], G1b[:, :], ALU.mult, ALU.mult),
                 reads=[rs_, G1b], writes=[x_])
            P.op("dve", lambda: V.tensor_tensor(h_[:, :], x_[:, :], SHb[:, :], ALU.add), reads=[x_, SHb], writes=[h_])
            hT_ = hTb[bi]
            for q4 in range(ND // 4 if ND >= 4 else 1):
                nq = min(4, ND)
                pt_ = pT[nT % 2]
                nT += 1
                for a in range(nq):
                    dc = q4 * 4 + a
                    P.op("pe", lambda dc=dc, a=a: T.transpose(pt_[:, a * 128:(a + 1) * 128], h_[:, dc * 128:(dc + 1) * 128], identb[:, :]),
                         reads=[h_, identb], writes=[pt_] if a == 0 else (), acc=() if a == 0 else [pt_])
                src = pt_[:, 0:nq * 128].rearrange("p (a b) -> p a b", b=128)
                dst = hT_[:, q4 * 4:q4 * 4 + nq, :]
                if q4 % 2 == 0:
                    P.op("act", lambda: A.copy(dst, src), reads=[pt_], writes=[hT_] if q4 == 0 else (), acc=() if q4 == 0 else [hT_])
                else:
                    P.op("dve", lambda: V.tensor_copy(dst, src), reads=[pt_], acc=[hT_])
            if r >= NB - NBD:
                rd = r - (NB - NBD)
                P.dma("sp", hT_s_v[:, :, rd * 128:(rd + 1) * 128], hT_[:, :, :], reads=[hT_], wnow=[hT_s])
            for dc in range(ND):
                P.op("pe", lambda dc=dc: mm(pkv[:, 0:R], hT_[:, dc, :], Wkva[:, dc, 0:R], start=(dc == 0), stop=(dc == ND - 1)),
                     reads=[hT_, Wkva], writes=[pkv] if dc == 0 else (), acc=() if dc == 0 else [pkv])
            for dc in range(ND):
                P.op("pe", lambda dc=dc: mm(pkp[:, 0:64], hT_[:, dc, :], Wkva[:, dc, R:R + 64], start=(dc == 0), stop=(dc == ND - 1)),
                     reads=[hT_, Wkva], writes=[pkp] if dc == 0 else (), acc=() if dc == 0 else [pkp])
            s2, r2, ck_ = ss2[r % 2], rs2[r % 2], ckvb[r % 2]
            P.op("act", lambda: A.activation(junk[:, :], pkv[:, 0:R], AF.Square, accum_out=s2[:, :]),
                 reads=[pkv], writes=[junk, s2])
            rs_from_ss(s2, r2, R)
            P.op("dve", lambda: V.scalar_tensor_tensor(ck_[:, :], pkv[:, 0:R], r2[:, :], gkvb[:, :], ALU.mult, ALU.mult),
                 reads=[pkv, r2, gkvb], writes=[ck_])
            t_, kp_ = Tt[r % 2], kpeb[r % 2]
            cb, sb = COS[:, r, :], SIN[:, r, :]
            x1, x2 = pkp[:, 0:32], pkp[:, 32:64]
            P.op("dve", lambda: V.tensor_tensor(t_[:, 0:32], x1, cb, ALU.mult), reads=[pkp, COS], writes=[t_])
            P.op("dve", lambda: V.tensor_tensor(t_[:, 32:64], x2, sb, ALU.mult), reads=[pkp, SIN], acc=[t_])
            P.op("dve", lambda: V.tensor_tensor(t_[:, 64:96], x2, cb, ALU.mult), reads=[pkp, COS], acc=[t_])
            P.op("dve", lambda: V.tensor_tensor(t_[:, 96:128], x1, sb, ALU.mult), reads=[pkp, SIN], acc=[t_])
            P.op("dve", lambda: V.tensor_tensor(kp_[:, 0:32], t_[:, 0:32], t_[:, 32:64], ALU.subtract), reads=[t_], writes=[kp_])
            P.op("dve", lambda: V.tensor_tensor(kp_[:, 32:64], t_[:, 64:96], t_[:, 96:128], ALU.add), reads=[t_], acc=[kp_])
            cg_, kg_ = ckvTg[g % 2], kpeTg[g % 2]
            pt_ = pT[nT % 2]
            nT += 1
            for a in range(NR):
                P.op("pe", lambda a=a: T.transpose(pt_[:, a * 128:(a + 1) * 128], ck_[:, a * 128:(a + 1) * 128], identb[:, :]),
                     reads=[ck_, identb], writes=[pt_] if a == 0 else (), acc=() if a == 0 else [pt_])
            P.op("act", lambda: A.copy(cg_[:, :, bi * 128:(bi + 1) * 128], pt_[:, 0:NR * 128].rearrange("p (a b) -> p a b", b=128)),
                 reads=[pt_], writes=[cg_] if bi == 0 else (), acc=() if bi == 0 else [cg_])
            pt_ = pT[nT % 2]
            nT += 1
            P.op("pe", lambda: T.transpose(pt_[0:64, 0:128], kp_[:, 0:64], identb[:, :]), reads=[kp_, identb], writes=[pt_])
            P.op("dve", lambda: V.tensor_copy(kg_[0:64, bi * 128:(bi + 1) * 128], pt_[0:64, 0:128]),
                 reads=[pt_], writes=[kg_] if bi == 0 else (), acc=() if bi == 0 else [kg_])
            vg_ = Vg[g % 2]
            for cg in range(NCG):
                pv_ = pV[nV % 2]
                nV += 1
                for cc in range(NR):
                    P.op("pe", lambda cc=cc, cg=cg: mm(pv_[:, 0:VN], cg_[:, cc, bi * 128:(bi + 1) * 128], WkV[:, cc, cg * VN:(cg + 1) * VN],
                                                       start=(cc == 0), stop=(cc == NR - 1)),
                         reads=[cg_, WkV], writes=[pv_] if cc == 0 else (), acc=() if cc == 0 else [pv_])
                first = (bi == 0 and cg == 0)
                P.op("act" if cg % 2 == 0 else "dve",
                     (lambda cg=cg: A.copy(vg_[:, cg * HPG:(cg + 1) * HPG, bi, :], pv_[:, 0:VN].rearrange("p (h d) -> p h d", d=128))) if cg % 2 == 0 else
                     (lambda cg=cg: V.tensor_copy(vg_[:, cg * HPG:(cg + 1) * HPG, bi, :], pv_[:, 0:VN].rearrange("p (h d) -> p h d", d=128))),
                     reads=[pv_], writes=[vg_] if first else (), acc=() if first else [vg_])
            if bi == 3:
                for h0 in range(0, H, KH):
                    ks_ = Kst[nK % 2]
                    nK += 1
                    for hh in range(KH):
                        h = h0 + hh
                        pk_ = pK[h % 2]
                        for cc in range(NR):
                            P.op("pe", lambda cc=cc, h=h: mm(pk_[:, :], WkK[:, cc, h * 128:(h + 1) * 128], cg_[:, cc, :],
                                                             start=(cc == 0), stop=(cc == NR - 1)),
                                 reads=[cg_, WkK], writes=[pk_] if cc == 0 else (), acc=() if cc == 0 else [pk_])
                        P.op("act" if h % 2 == 0 else "dve",
                             (lambda hh=hh: A.copy(ks_[:, hh, :], pk_[:, :])) if h % 2 == 0 else
                             (lambda hh=hh: V.tensor_copy(ks_[:, hh, :], pk_[:, :])),
                             reads=[pk_], writes=[ks_] if hh == 0 else (), acc=() if hh == 0 else [ks_])
                    P.dma("sp", KT_s_v[:, h0:h0 + KH, g * 512:(g + 1) * 512], ks_[:, :, :], reads=[ks_], wnow=[KT_s])
                P.dma("sp", KPE_s[:, g * 512:(g + 1) * 512], kg_[0:64, :], reads=[kg_], wnow=[KPE_s])
                P.dma("sp", V_s_v[:, :, 4 * g:4 * g + 4, :].rearrange("p h b d -> p h (b d)"),
                      vg_[:, :, :, :].rearrange("p h b d -> p h (b d)"), reads=[vg_], wnow=[V_s])
        P.barrier()

    if stop_after <= 1:
        st12.close(); st0.close()
        return nc, P
    OWN0 = NGD - NU
    with ExitStack() as st:
        Wsl = [P.sbuf(st, "Wsl%d" % i, [128, ND, 1024], BF16) for i in range(2)]
        hTg = [P.sbuf(st, "hTg%d" % i, [128, ND, 512], BF16) for i in range(2)]
        stg = [P.sbuf(st, "stg%d" % i, [128, 8, 512], BF16) for i in range(2)]
        rot = [P.sbuf(st, "rot%d" % i, [128, 1024], F32) for i in range(2)]
        qb = [P.sbuf(st, "qb%d" % i, [128, 1024], BF16) for i in range(2)]
        vst = [P.sbuf(st, "vst%d" % i, [128, DH, 4, 128], BF16) for i in range(2)]
        pF = [P.psum(st, "pF%d" % i, [128, 512], F32) for i in range(3)]
        pT2 = [P.psum(st, "pTb%d" % i, [128, 512], BF16) for i in range(2)]
        cnt = {"sl": 0, "hg": 0, "pf": 0, "st": 0, "rot": 0, "pt": 0, "vs": 0}

        def load_slab(pieces):
            sl = Wsl[cnt["sl"] % 2]
            cnt["sl"] += 1
            first = True
            for (d0, n, src) in pieces:
                P.dma("pool", sl[:, :, d0:d0 + n], src, writes=[sl] if first else (), wnow=() if first else [sl])
                first = False
            return sl

        def load_h(tg):
            hg = hTg[cnt["hg"] % 2]
            cnt["hg"] += 1
            P.dma("sp", hg[:, :, :], hT_s_v_[:, :, tg * 512:(tg + 1) * 512], reads=[hT_s], writes=[hg])
            return hg

        hT_s_v_ = hT_s[:, :, :].rearrange("c p t -> p c t")

        def fm_job(sl, ncols, func, dst_v, tgs):
            nch = ncols // 128
            for tg in tgs:
                hg = load_h(tg)
                sg = stg[cnt["st"] % 2]
                cnt["st"] += 1
                for cc in range(nch):
                    pf = pF[cnt["pf"] % 3]
                    cnt["pf"] += 1
                    for dc in range(ND):
                        P.op("pe", lambda dc=dc, cc=cc: mm(pf[:, :], sl[:, dc, cc * 128:(cc + 1) * 128], hg[:, dc, :],
                                                           start=(dc == 0), stop=(dc == ND - 1)),
                             reads=[sl, hg], writes=[pf] if dc == 0 else (), acc=() if dc == 0 else [pf])
                    P.op("act", lambda cc=cc: A.activation(sg[:, cc, :], pf[:, :], func), reads=[pf],
                         writes=[sg] if cc == 0 else (), acc=() if cc == 0 else [sg])
                t0 = (tg - OWN0) * 512
                P.dma("sp", dst_v[:, :, t0:t0 + 512], sg[:, 0:nch, :], reads=[sg], wnow=[dst_v_buf[0]])

        dst_v_buf = [None]

        def tm_block(sl, ncols, hg, bi):
            outs = []
            for hf in range((ncols + 511) // 512):
                n = min(512, ncols - hf * 512)
                pf = pF[cnt["pf"] % 3]
                cnt["pf"] += 1
                for dc in range(ND):
                    P.op("pe", lambda dc=dc, hf=hf, n=n: mm(pf[:, 0:n], hg[:, dc, bi * 128:(bi + 1) * 128], sl[:, dc, hf * 512:hf * 512 + n],
                                                            start=(dc == 0), stop=(dc == ND - 1)),
                         reads=[sl, hg], writes=[pf] if dc == 0 else (), acc=() if dc == 0 else [pf])
                outs.append((pf, n))
            return outs

        ROTM = [9]

        def rotary_tm(outs, nh, hd, half, r, step, q_):
            col = 0
            for i, (pf, n) in enumerate(outs):
                P.op("act", lambda pf=pf, n=n, col=col: A.copy(q_[:, col:col + n], pf[:, 0:n]), reads=[pf],
                     writes=[q_] if i == 0 else (), acc=() if i == 0 else [q_])
                col += n
            col = 0
            for i, (pf, n) in enumerate(outs):
                rt = rot[cnt["rot"] % 2]
                cnt["rot"] += 1
                nhh = n // hd
                pv = pf[:, 0:n].rearrange("p (h j) -> p h j", j=hd)
                x1, x2 = pv[:, :, 0:half], pv[:, :, half:2 * half]
                cb = COS[:, r, 0:32:step].unsqueeze(1).broadcast_to([128, nhh, half])
                sb = SIN[:, r, 0:32:step].unsqueeze(1).broadcast_to([128, nhh, half])
                rv = rt[:, 0:4 * nhh * half].rearrange("p (k h j) -> p k h j", k=4, j=half)
                qv = q_[:, col:col + n].rearrange("p (h j) -> p h j", j=hd)
                P.op("dve", lambda: V.tensor_tensor(rv[:, 0], x1, cb, ALU.mult), reads=[pf, COS], writes=[rt])
                P.op("dve", lambda: V.tensor_tensor(rv[:, 1], x2, sb, ALU.mult), reads=[pf, SIN], acc=[rt])
                P.op("dve", lambda: V.tensor_tensor(rv[:, 2], x2, cb, ALU.mult), reads=[pf, COS], acc=[rt])
                P.op("dve", lambda: V.tensor_tensor(rv[:, 3], x1, sb, ALU.mult), reads=[pf, SIN], acc=[rt])
                P.op("dve", lambda: V.tensor_tensor(qv[:, :, 0:half], rv[:, 0], rv[:, 1], ALU.subtract), reads=[rt],
                     writes=[q_] if i == 0 else (), acc=() if i == 0 else [q_])
                P.op("dve", lambda: V.tensor_tensor(qv[:, :, half:2 * half], rv[:, 2], rv[:, 3], ALU.add), reads=[rt], acc=[q_])
                col += n

        def transpose_to_stage(q_, nchunk, sg, bi, first):
            for c0 in range(0, nchunk, 4):
                nq = min(4, nchunk - c0)
                pt_ = pT2[cnt["pt"] % 2]
                cnt["pt"] += 1
                for a in range(nq):
                    cc = c0 + a
                    P.op("pe", lambda cc=cc, a=a: T.transpose(pt_[:, a * 128:(a + 1) * 128], q_[:, cc * 128:(cc + 1) * 128], identb[:, :]),
                         reads=[q_, identb], writes=[pt_] if a == 0 else (), acc=() if a == 0 else [pt_])
                src = pt_[:, 0:nq * 128].rearrange("p (a b) -> p a b", b=128)
                dst = sg[:, c0:c0 + nq, bi * 128:(bi + 1) * 128]
                fw = first and c0 == 0
                if (c0 // 4) % 2 == 0:
                    P.op("act", lambda: A.copy(dst, src), reads=[pt_], writes=[sg] if fw else (), acc=() if fw else [sg])
                else:
                    P.op("dve", lambda: V.tensor_copy(dst, src), reads=[pt_], writes=[sg] if fw else (), acc=() if fw else [sg])

        own_tgs = list(range(OWN0, NGD))
        all_tgs = list(range(NGD))
        if stop_after > 1.1:
            for h0 in range(0, H, 8):
                nh = min(8, H - h0)
                src = w_in_v[:, :, c.o_q + h0 * 192:c.o_q + (h0 + nh) * 192].rearrange("p c (h j) -> p c h j", j=192)
                sl = Wsl[cnt["sl"] % 2]
                cnt["sl"] += 1
                for dc in range(ND):
                    P.dma("pool", sl[:, dc, 0:nh * 128].rearrange("p (h j) -> p h j", j=128), src[:, dc, :, 0:128],
                          writes=[sl] if dc == 0 else (), wnow=() if dc == 0 else [sl])
                dst_v_buf[0] = QN_s
                fm_job(sl, nh * 128, AF.Copy, QN_s[h0:h0 + nh, :, :].rearrange("h p t -> p h t"), own_tgs)
        if stop_after > 1.3:
            for (o0, n_tot, func, dbuf) in ((c.o_zm, c.ZM, AF.Silu, ZM_s), (c.o_zd, c.ZD, AF.Silu, ZD_s),
                                             (c.o_glm, D, AF.Sigmoid, GLM_s), (c.o_gld, D, AF.Sigmoid, GLD_s)):
                for c0 in range(0, n_tot, 1024):
                    n = min(1024, n_tot - c0)
                    sl = load_slab([(0, n, w_in_v[:, :, o0 + c0:o0 + c0 + n])])
                    dst_v_buf[0] = dbuf
                    fm_job(sl, n, func, dbuf[c0 // 128:(c0 + n) // 128, :, :].rearrange("h p t -> p h t"), own_tgs)
        if stop_after > 1.5:
            for h0 in range(0, H, 16):
                nh = min(16, H - h0)
                src = w_in_v[:, :, c.o_q + h0 * 192:c.o_q + (h0 + nh) * 192].rearrange("p c (h j) -> p c h j", j=192)
                sl = Wsl[cnt["sl"] % 2]
                cnt["sl"] += 1
                for dc in range(ND):
                    P.dma("pool", sl[:, dc, 0:nh * 64].rearrange("p (h j) -> p h j", j=64), src[:, dc, :, 128:192],
                          writes=[sl] if dc == 0 else (), wnow=() if dc == 0 else [sl])
                QRM = 9
                for tg in (own_tgs if QRM >= 2 else []):
                    hg = load_h(tg)
                    sg = stg[cnt["st"] % 2]
                    cnt["st"] += 1
                    for bi in range(4):
                        r = (NB - NBD) + tg * 4 + bi
                        outs = tm_block(sl, nh * 64, hg, bi)
                        q_ = qb[(cnt.__setitem__("qb", cnt.get("qb", 0) + 1) or cnt["qb"]) % 2]
                        rotary_tm(outs, nh, 64, 32, r, 1, q_)
                        if QRM >= 3:
                            transpose_to_stage(q_, nh // 2, sg, bi, bi == 0)
                    t0 = (tg - OWN0) * 512
                    if QRM >= 4:
                        P.dma("sp", QR_s[h0 // 2:(h0 + nh) // 2, :, t0:t0 + 512].rearrange("h p t -> p h t"), sg[:, 0:nh // 2, :],
                              reads=[sg], wnow=[QR_s])
        if stop_after > 1.7:
            for which in range(3):
                for g in range(3):
                    o0 = c.o_qkvd + which * 3 * DV + g * DV
                    sl = load_slab([(0, DV, w_in_v[:, :, o0:o0 + DV])])
                    tgs = own_tgs if which == 0 else all_tgs
                    for tg in tgs:
                        hg = load_h(tg)
                        if which < 2:
                            sg = stg[cnt["st"] % 2]
                            cnt["st"] += 1
                        else:
                            vs = vst[cnt["vs"] % 2]
                            cnt["vs"] += 1
                        for bi in range(4):
                            r = (NB - NBD) + tg * 4 + bi
                            outs = tm_block(sl, DV, hg, bi)
                            if which < 2:
                                q_ = qb[(cnt.__setitem__("qb", cnt.get("qb", 0) + 1) or cnt["qb"]) % 2]
                                rotary_tm(outs, DH, 128, 16, r, 2, q_)
                                transpose_to_stage(q_, DH, sg, bi, bi == 0)
                            else:
                                col = 0
                                for i, (pf, n) in enumerate(outs):
                                    fw = (bi == 0 and i == 0)
                                    P.op("act", lambda pf=pf, n=n, col=col: A.copy(vs[:, col // 128:(col + n) // 128, bi, :],
                                                                                   pf[:, 0:n].rearrange("p (h d) -> p h d", d=128)), reads=[pf],
                                         writes=[vs] if fw else (), acc=() if fw else [vs])
                                    col += n
                        if which == 0:
                            t0 = (tg - OWN0) * 512
                            P.dma("sp", QD_s[g * DH:(g + 1) * DH, :, t0:t0 + 512].rearrange("h p t -> p h t"), sg[:, 0:DH, :],
                                  reads=[sg], wnow=[QD_s])
                        elif which == 1:
                            P.dma("sp", KD_s[g * DH:(g + 1) * DH, :, tg * 512:(tg + 1) * 512].rearrange("h p t -> p h t"), sg[:, 0:DH, :],
                                  reads=[sg], wnow=[KD_s])
                        else:
                            P.dma("sp", VD_s[g * DH:(g + 1) * DH, :, 4 * tg:4 * tg + 4, :].rearrange("h p b d -> p h (b d)"),
                                  vs[:, :, :, :].rearrange("p h b d -> p h (b d)"), reads=[vs], wnow=[VD_s])
        P.barrier()
    st12.close()

    if stop_after <= 2:
        st0.close()
        return nc, P

    KC = 16
    with ExitStack() as st:
        mmask = P.sbuf(st, "mmask", [128, 4, 512], BF16)
        P.dma("pool", mmask[:, :, :], mmask_d.rearrange("m p q -> p m q"), writes=[mmask])
        QNt = [P.sbuf(st, "QNt%d" % i, [128, 512], BF16) for i in range(2)]
        QRt = [P.sbuf(st, "QRt%d" % i, [64, 512], BF16) for i in range(2)]
        Zt = [P.sbuf(st, "Zt%d" % i, [128, 512], BF16) for i in range(2)]
        KTc = [P.sbuf(st, "KTc%d" % i, [128, KC * 128], BF16) for i in range(3)]
        KPc = [P.sbuf(st, "KPc%d" % i, [64, KC * 128], BF16) for i in range(3)]
        Vc = [P.sbuf(st, "Vc%d" % i, [128, KC, 128], BF16) for i in range(3)]
        ptb = [P.sbuf(st, "ptb%d" % i, [128, 512], BF16) for i in range(3)]
        rD = [P.sbuf(st, "rD%d" % i, [128, 512], F32) for i in range(2)]
        at = [P.sbuf(st, "at%d" % i, [128, 512], F32) for i in range(2)]
        ab = [P.sbuf(st, "ab%d" % i, [128, 512], BF16) for i in range(2)]
        pS = [P.psum(st, "pS%d" % i, [128, 512], F32) for i in range(3)]
        pO = [P.psum(st, "pO%d" % i, [128, 512], F32) for i in range(2)]
        pD = [P.psum(st, "pD%d" % i, [128, 512], F32) for i in range(2)]
        sc_mla = 192.0 ** -0.5
        jobs = [(m, h) for m in range(NU) for h in range(H)]
        chunks = []
        for ji, (m, h) in enumerate(jobs):
            nkb = (NB - CH // 128) + 4 * m + 4
            for c0 in range(0, nkb, KC):
                chunks.append((ji, h, c0, min(KC, nkb - c0)))

        def load_chunk(ci):
            ji, h, c0, n = chunks[ci]
            k_, p_, v_ = KTc[ci % 3], KPc[ci % 3], Vc[ci % 3]
            P.dma("sp", k_[:, 0:n * 128], KT_s[h, :, c0 * 128:(c0 + n) * 128], reads=[KT_s], writes=[k_])
            P.dma("sp", p_[:, 0:n * 128], KPE_s[:, c0 * 128:(c0 + n) * 128], reads=[KPE_s], writes=[p_])
            P.dma("sp", v_[:, 0:n, :], V_s[h, :, c0:c0 + n, :], reads=[V_s], writes=[v_])

        def load_q(ji):
            m, h = jobs[ji]
            t0 = m * 512
            P.dma("sp", QNt[ji % 2][:, :], QN_s[h, :, t0:t0 + 512], reads=[QN_s], writes=[QNt[ji % 2]])
            P.dma("sp", QRt[ji % 2][:, :], QR_s[h // 2, (h % 2) * 64:(h % 2) * 64 + 64, t0:t0 + 512], reads=[QR_s], writes=[QRt[ji % 2]])
            P.dma("sp", Zt[ji % 2][:, :], ZM_s[h, :, t0:t0 + 512], reads=[ZM_s], writes=[Zt[ji % 2]])

        load_q(0)
        load_chunk(0)
        if len(chunks) > 1:
            load_chunk(1)
        ci = 0
        nt = [0]
        for ji, (m, h) in enumerate(jobs):
            if ji + 1 < len(jobs):
                load_q(ji + 1)
            qn, qr, zt = QNt[ji % 2], QRt[ji % 2], Zt[ji % 2]
            po, pd = pO[ji % 2], pD[ji % 2]
            nkb = (NB - CH // 128) + 4 * m + 4
            tl = []
            cj = ci
            kb = 0
            while kb < nkb:
                _, _, c0, n = chunks[cj]
                for i in range(n):
                    tl.append((cj, i, c0 + i, i == n - 1))
                kb = c0 + n
                cj += 1
            ci = cj

            def emit_S(t):
                cj, i, kb, last = t
                k_, p_ = KTc[cj % 3], KPc[cj % 3]
                dj = kb - (nkb - 4)
                ps, pt = pS[nt[0] % 3], ptb[nt[0] % 3]
                nt[0] += 1
                P.op("pe", lambda: mm(ps[:, :], k_[:, i * 128:(i + 1) * 128], qn[:, :], start=True, stop=False),
                     reads=[k_, qn], writes=[ps])
                P.op("pe", lambda: mm(ps[:, :], p_[0:64, i * 128:(i + 1) * 128], qr[0:64, :], start=False, stop=(dj < 0)),
                     reads=[p_, qr], acc=[ps])
                if dj >= 0:
                    P.op("pe", lambda: mm(ps[:, :], identb[:, :], mmask[:, dj, :], start=False, stop=True),
                         reads=[identb, mmask], acc=[ps])
                return ps, pt

            def emit_rest(t, ps, pt):
                cj, i, kb, last = t
                v_ = Vc[cj % 3]
                P.op("act", lambda: A.activation(pt[:, :], ps[:, :], AF.Exp, bias=kbias[:, kb:kb + 1], scale=sc_mla),
                     reads=[ps, kbias], writes=[pt])
                P.op("pe", lambda: mm(po[:, :], v_[:, i, :], pt[:, :], start=(kb == 0), stop=(kb == nkb - 1)),
                     reads=[v_, pt], writes=[po] if kb == 0 else (), acc=() if kb == 0 else [po])
                P.op("pe", lambda: mm(pd[:, :], onesb[:, :], pt[:, :], start=(kb == 0), stop=(kb == nkb - 1)),
                     reads=[onesb, pt], writes=[pd] if kb == 0 else (), acc=() if kb == 0 else [pd])
                if last and cj + 2 < len(chunks):
                    load_chunk(cj + 2)

            cur = emit_S(tl[0])
            for idx in range(len(tl)):
                nxt = emit_S(tl[idx + 1]) if idx + 1 < len(tl) else None
                emit_rest(tl[idx], *cur)
                cur = nxt
            rd_, at_, ab_ = rD[ji % 2], at[ji % 2], ab[ji % 2]
            P.op("dve", lambda: V.reciprocal(rd_[:, :], pd[:, :]), reads=[pd], writes=[rd_])
            P.op("dve", lambda: V.tensor_tensor(at_[:, :], po[:, :], rd_[:, :], ALU.mult), reads=[po, rd_], writes=[at_])
            P.op("dve", lambda: V.tensor_tensor(ab_[:, :], at_[:, :], zt[:, :], ALU.mult), reads=[at_, zt], writes=[ab_])
            P.dma("sp", A_s[h, :, m * 512:(m + 1) * 512], ab_[:, :], reads=[ab_], wnow=[A_s])
        P.barrier()

    if stop_after <= 3:
        st0.close()
        return nc, P
    with ExitStack() as st:
        dmask = P.sbuf(st, "dmask", [128, sum(NDM), 512], BF16)
        for i in range(0, sum(NDM), 4):
            n = min(4, sum(NDM) - i)
            P.dma("pool", dmask[:, i:i + n, :], dmask_d[i:i + n].rearrange("m p q -> p m q"),
                  writes=[dmask] if i == 0 else (), wnow=() if i == 0 else [dmask])
        Qt = [P.sbuf(st, "Qt%d" % i, [128, 512], BF16) for i in range(3)]
        Kt = [P.sbuf(st, "Kt%d" % i, [128, 20 * 128], BF16) for i in range(3)]
        Vt = [P.sbuf(st, "Vt%d" % i, [128, 20, 128], BF16) for i in range(3)]
        Zt = [P.sbuf(st, "Zdt%d" % i, [128, 512], BF16) for i in range(2)]
        ptb = [P.sbuf(st, "ptd%d" % i, [128, 512], BF16) for i in range(3)]
        rD = [P.sbuf(st, "rDd%d" % i, [128, 512], F32) for i in range(2)]
        at = [P.sbuf(st, "atd%d" % i, [128, 512], F32) for i in range(2)]
        ab = [P.sbuf(st, "abd%d" % i, [128, 512], BF16) for i in range(2)]
        pS = [P.psum(st, "pSd%d" % i, [128, 512], F32) for i in range(3)]
        pO = [P.psum(st, "pOd%d" % i, [128, 512], F32) for i in range(2)]
        pD = [P.psum(st, "pDd%d" % i, [128, 512], F32) for i in range(2)]
        sc_dil = 128.0 ** -0.5
        moff = [0, NDM[0], NDM[0] + NDM[1]]
        nback = [1, 4, 16]
        gjobs = [(m, hd, g) for m in range(NU) for hd in range(DH) for g in range(3)]

        def load_g(gi):
            m, hd, g = gjobs[gi]
            gh = g * DH + hd
            nb_ = nback[g] + 4
            b0 = (DBACK // 128) + 4 * m - nback[g]
            P.dma("sp", Qt[gi % 3][:, :], QD_s[gh, :, m * 512:(m + 1) * 512], reads=[QD_s], writes=[Qt[gi % 3]])
            P.dma("sp", Kt[gi % 3][:, 0:nb_ * 128], KD_s[gh, :, b0 * 128:(b0 + nb_) * 128], reads=[KD_s], writes=[Kt[gi % 3]])
            P.dma("sp", Vt[gi % 3][:, 0:nb_, :], VD_s[gh, :, b0:b0 + nb_, :], reads=[VD_s], writes=[Vt[gi % 3]])

        load_g(0)
        load_g(1)
        nt = [0]
        for gi, (m, hd, g) in enumerate(gjobs):
            if gi + 2 < len(gjobs):
                load_g(gi + 2)
            ji = gi // 3
            po, pd = pO[ji % 2], pD[ji % 2]
            if g == 0:
                P.dma("sp", Zt[ji % 2][:, :], ZD_s[hd, :, m * 512:(m + 1) * 512], reads=[ZD_s], writes=[Zt[ji % 2]])
            q_, k_, v_ = Qt[gi % 3], Kt[gi % 3], Vt[gi % 3]
            nb_ = nback[g] + 4
            b0 = (DBACK // 128) + 4 * m - nback[g]
            def emit_S(i):
                ps, pt = pS[nt[0] % 3], ptb[nt[0] % 3]
                nt[0] += 1
                P.op("pe", lambda: mm(ps[:, :], k_[:, i * 128:(i + 1) * 128], q_[:, :], start=True, stop=False),
                     reads=[k_, q_], writes=[ps])
                P.op("pe", lambda: mm(ps[:, :], identb[:, :], dmask[:, moff[g] + i, :], start=False, stop=True),
                     reads=[identb, dmask], acc=[ps])
                return ps, pt

            def emit_rest(i, ps, pt):
                first = (g == 0 and i == 0)
                last = (g == 2 and i == nb_ - 1)
                wb = (NB - NBD) + b0 + i
                P.op("act", lambda: A.activation(pt[:, :], ps[:, :], AF.Exp, bias=kbias[:, wb:wb + 1], scale=sc_dil),
                     reads=[ps, kbias], writes=[pt])
                P.op("pe", lambda: mm(po[:, :], v_[:, i, :], pt[:, :], start=first, stop=last),
                     reads=[v_, pt], writes=[po] if first else (), acc=() if first else [po])
                P.op("pe", lambda: mm(pd[:, :], onesb[:, :], pt[:, :], start=first, stop=last),
                     reads=[onesb, pt], writes=[pd] if first else (), acc=() if first else [pd])

            cur = emit_S(0)
            for i in range(nb_):
                nxt = emit_S(i + 1) if i + 1 < nb_ else None
                emit_rest(i, *cur)
                cur = nxt
            if g == 2:
                rd_, at_, ab_, zt = rD[ji % 2], at[ji % 2], ab[ji % 2], Zt[ji % 2]
                P.op("dve", lambda: V.reciprocal(rd_[:, :], pd[:, :]), reads=[pd], writes=[rd_])
                P.op("dve", lambda: V.tensor_tensor(at_[:, :], po[:, :], rd_[:, :], ALU.mult), reads=[po, rd_], writes=[at_])
                P.op("dve", lambda: V.tensor_tensor(ab_[:, :], at_[:, :], zt[:, :], ALU.mult), reads=[at_, zt], writes=[ab_])
                P.dma("sp", B_s[hd, :, m * 512:(m + 1) * 512], ab_[:, :], reads=[ab_], wnow=[B_s])
        P.barrier()

    if stop_after <= 4:
        st0.close()
        return nc, P
    with ExitStack() as st:
        Wm = P.sbuf(st, "Wm", [128, H, D], BF16)
        Wd = P.sbuf(st, "Wd", [128, DH, D], BF16)
        wm_v = w_mla.rearrange("(h p) n -> p h n", p=128)
        wd_v = w_dil.rearrange("(h p) n -> p h n", p=128)
        first = True
        for n0 in range(0, D, 1024):
            n = min(1024, D - n0)
            P.dma("pool", Wm[:, :, n0:n0 + n], wm_v[:, :, n0:n0 + n], writes=[Wm] if first else (), wnow=() if first else [Wm])
            P.dma("pool", Wd[:, :, n0:n0 + n], wd_v[:, :, n0:n0 + n], writes=[Wd] if first else (), wnow=() if first else [Wd])
            first = False
        aT = [P.sbuf(st, "aT%d" % i, [128, H, 512], BF16) for i in range(2)]
        bT = [P.sbuf(st, "bT%d" % i, [128, DH, 512], BF16) for i in range(2)]
        glm = [P.sbuf(st, "glm%d" % i, [128, 512], BF16) for i in range(2)]
        gld = [P.sbuf(st, "gld%d" % i, [128, 512], BF16) for i in range(2)]
        t1 = [P.sbuf(st, "t1%d" % i, [128, 512], F32) for i in range(2)]
        t2 = [P.sbuf(st, "t2%d" % i, [128, 512], F32) for i in range(2)]
        mst = [P.sbuf(st, "mst%d" % i, [128, ND, 512], BF16) for i in range(2)]
        pY = [P.psum(st, "pY%d" % i, [128, 512], F32) for i in range(2)]
        pZ = [P.psum(st, "pZ%d" % i, [128, 512], F32) for i in range(2)]
        k = 0
        for m in range(NU):
            a_, b_, ms = aT[m % 2], bT[m % 2], mst[m % 2]
            P.dma("sp", a_[:, :, :], A_s[:, :, m * 512:(m + 1) * 512].rearrange("h p t -> p h t"), reads=[A_s], writes=[a_])
            P.dma("sp", b_[:, :, :], B_s[:, :, m * 512:(m + 1) * 512].rearrange("h p t -> p h t"), reads=[B_s], writes=[b_])
            for cc in range(ND):
                gm, gd, py, pz, u1, u2 = glm[k % 2], gld[k % 2], pY[k % 2], pZ[k % 2], t1[k % 2], t2[k % 2]
                k += 1
                P.dma("sp", gm[:, :], GLM_s[cc, :, m * 512:(m + 1) * 512], reads=[GLM_s], writes=[gm])
                P.dma("sp", gd[:, :], GLD_s[cc, :, m * 512:(m + 1) * 512], reads=[GLD_s], writes=[gd])
                for h in range(H):
                    P.op("pe", lambda h=h, cc=cc: mm(py[:, :], Wm[:, h, cc * 128:(cc + 1) * 128], a_[:, h, :], start=(h == 0), stop=(h == H - 1)),
                         reads=[Wm, a_], writes=[py] if h == 0 else (), acc=() if h == 0 else [py])
                for h in range(DH):
                    P.op("pe", lambda h=h, cc=cc: mm(pz[:, :], Wd[:, h, cc * 128:(cc + 1) * 128], b_[:, h, :], start=(h == 0), stop=(h == DH - 1)),
                         reads=[Wd, b_], writes=[pz] if h == 0 else (), acc=() if h == 0 else [pz])
                P.op("dve", lambda: V.tensor_tensor(u1[:, :], py[:, :], gm[:, :], ALU.mult), reads=[py, gm], writes=[u1])
                P.op("dve", lambda: V.tensor_tensor(u2[:, :], pz[:, :], gd[:, :], ALU.mult), reads=[pz, gd], writes=[u2])
                P.op("dve", lambda cc=cc: V.tensor_tensor(ms[:, cc, :], u1[:, :], u2[:, :], ALU.add), reads=[u1, u2],
                     writes=[ms] if cc == 0 else (), acc=() if cc == 0 else [ms])
            P.dma("sp", M_s[:, :, m * 512:(m + 1) * 512].rearrange("c p t -> p c t"), ms[:, :, :], reads=[ms], wnow=[M_s])
        P.barrier()

    with ExitStack() as st:
        Wo = P.sbuf(st, "Wo", [128, ND, D], BF16)
        wo_v = w_o.rearrange("(c p) n -> p c n", p=128)
        first = True
        for n0 in range(0, D, 1024):
            n = min(1024, D - n0)
            P.dma("pool", Wo[:, :, n0:n0 + n], wo_v[:, :, n0:n0 + n], writes=[Wo] if first else (), wnow=() if first else [Wo])
            first = False
        GGb = P.sbuf(st, "GGb", [128, D], F32)
        P.dma("sp", GGb[:, :], modrows[2].partition_broadcast(128), reads=[modrows], writes=[GGb])
        mT = [P.sbuf(st, "mT%d" % i, [128, ND, 512], BF16) for i in range(2)]
        xb = [P.sbuf(st, "xo%d" % i, [128, D], F32) for i in range(2)]
        ob = [P.sbuf(st, "ob%d" % i, [128, D], F32) for i in range(2)]
        jb = P.sbuf(st, "jb", [128, D], BF16)
        ss = [P.sbuf(st, "sso%d" % i, [128, 1], F32) for i in range(2)]
        rs = [P.sbuf(st, "rso%d" % i, [128, 1], F32) for i in range(2)]
        NO = min(512, D)
        NOG = D // NO
        pOut = [P.psum(st, "pOut%d" % i, [128, NO], F32) for i in range(min(8, 2 * NOG))]
        k = 0
        for m in range(NU):
            mt = mT[m % 2]
            P.dma("sp", mt[:, :, :], M_s[:, :, m * 512:(m + 1) * 512].rearrange("c p t -> p c t"), reads=[M_s], writes=[mt])
            for bi in range(4):
                row = (W - CH) + m * 512 + bi * 128
                x_, o_, ss_, rs_ = xb[k % 2], ob[k % 2], ss[k % 2], rs[k % 2]
                P.dma("sp", x_[:, :], xw[row:row + 128, :], writes=[x_])
                pss = []
                for og in range(NOG):
                    po = pOut[(k * NOG + og) % len(pOut)]
                    pss.append(po)
                    for cc in range(ND):
                        P.op("pe", lambda cc=cc, og=og: mm(po[:, :], mt[:, cc, bi * 128:(bi + 1) * 128], Wo[:, cc, og * NO:(og + 1) * NO],
                                                           start=(cc == 0), stop=(cc == ND - 1)),
                             reads=[mt, Wo], writes=[po] if cc == 0 else (), acc=() if cc == 0 else [po])
                    P.op("act" if og % 2 == 0 else "dve",
                         (lambda og=og: A.copy(o_[:, og * NO:(og + 1) * NO], po[:, :])) if og % 2 == 0 else
                         (lambda og=og: V.tensor_copy(o_[:, og * NO:(og + 1) * NO], po[:, :])),
                         reads=[po], writes=[o_] if og == 0 else (), acc=() if og == 0 else [o_])
                P.op("act", lambda: A.activation(jb[:, :], o_[:, :], AF.Square, accum_out=ss_[:, :]), reads=[o_], writes=[jb, ss_])
                rs_from_ss(ss_, rs_, D)
                P.op("dve", lambda: V.scalar_tensor_tensor(o_[:, :], o_[:, :], rs_[:, :], GGb[:, :], ALU.mult, ALU.mult),
                     reads=[rs_, GGb], writes=[o_])
                P.op("dve", lambda: V.tensor_tensor(o_[:, :], o_[:, :], x_[:, :], ALU.add), reads=[x_], writes=[o_])
                orow = m * 512 + bi * 128
                P.dma("sp", y[orow:orow + 128, :], o_[:, :], reads=[o_], wnow=[yB])
                k += 1
        P.barrier()
    st0.close()
    return nc, P


def make_masks():
    kk = np.arange(128)[:, None]
    qq = np.arange(512)[None, :]
    mm_ = np.stack([np.where(j * 128 + kk <= qq, 0.0, NEG) for j in range(4)]).astype(np.float32)
    dms = []
    for (win, d), nb in zip(DIL, (1, 4, 16)):
        for i in range(nb + 4):
            o = i - nb
            delta = qq - 128 * o - kk
            ok = (delta >= 0) & (delta % d == 0) & (delta <= win)
            dms.append(np.where(ok, 0.0, NEG))
    return mm_, np.stack(dms).astype(np.float32)


def make_in_maps(cfg, x, c, positions, w_ada, b_ada, g_pre, g_post, w_in, g_kv, w_kv_b, w_mla_proj, w_dil_proj, w_o):
    cf = cfg
    B = x.shape[0]
    mmask, dmask = make_masks()
    ident = np.eye(128, dtype=np.float32)
    freq = np.tile((1.0 / (THETA ** (np.arange(0, 64, 2, dtype=np.float32) / 64.0))).astype(np.float32)[None, :], (128, 1))
    shared = {
        "b_ada": np.ascontiguousarray(b_ada[0][None, :]), "g_pre": np.ascontiguousarray(g_pre[0][None, :]),
        "g_post": np.ascontiguousarray(g_post[0][None, :]), "g_kv": np.ascontiguousarray(g_kv[0][None, :]),
        "w_ada": np.ascontiguousarray(w_ada[0]), "w_in": np.ascontiguousarray(w_in[0]),
        "w_kv_b": np.ascontiguousarray(w_kv_b[0]), "w_mla_proj": np.ascontiguousarray(w_mla_proj[0]),
        "w_dil_proj": np.ascontiguousarray(w_dil_proj[0]), "w_o": np.ascontiguousarray(w_o[0]),
        "ident": ident, "freq": np.ascontiguousarray(freq), "mmask": mmask, "dmask": dmask,
    }
    maps = []
    for b in range(B):
        for j in range(cf.NCB):
            t_end = cf.CH * (j + 1)
            t0 = t_end - cf.W
            nz = max(0, -t0)
            xw = np.zeros((cf.W, cf.D), np.float32)
            xw[nz:] = x[b, max(t0, 0):t_end]
            pw = np.zeros((cf.W,), np.int32)
            pw[nz:] = positions[b, max(t0, 0):t_end]
            kb = np.zeros((cf.W,), np.float32)
            kb[:nz] = NEG
            m = dict(shared)
            m["xw"] = xw
            m["posw"] = np.ascontiguousarray(pw.reshape(cf.NB, 128).T)
            m["kbias"] = np.ascontiguousarray(kb.reshape(cf.NB, 128).T)
            m["c_col"] = np.ascontiguousarray(c[b].reshape(cf.ND, 128).T)
            maps.append(m)
    return maps


_CACHE = {}


def kernel(x, c, positions, w_ada, b_ada, g_pre, g_post, w_in, g_kv, w_kv_b, w_mla_proj, w_dil_proj, w_o):
    cfg = Cfg()
    t0 = time.time()
    args = [np.asarray(a) for a in (x, c, positions, w_ada, b_ada, g_pre, g_post, w_in, g_kv, w_kv_b, w_mla_proj, w_dil_proj, w_o)]
    maps = make_in_maps(cfg, *args)
    if "nc" not in _CACHE:
        _CACHE["nc"] = build(cfg)[0]
    nc = _CACHE["nc"]
    print("[kernel] build+layout %.1fs" % (time.time() - t0), flush=True)
    res = run_bass_kernel_spmd(nc, maps, core_ids=list(range(len(maps))))
    print("[kernel] launch done %.1fs" % (time.time() - t0), flush=True)
    B = args[0].shape[0]
    out = np.empty((B, cfg.S, cfg.D), np.float32)
    i = 0
    for b in range(B):
        for j in range(cfg.NCB):
            out[b, cfg.CH * j:cfg.CH * (j + 1)] = res.results[i]["y"]
            i += 1
    return out
```

```python
import math
import time
from contextlib import ExitStack

import numpy as np
import concourse.bass as bass
import concourse.mybir as mybir
from concourse.bass_utils import run_bass_kernel_spmd

F32 = mybir.dt.float32
BF16 = mybir.dt.bfloat16
I32 = mybir.dt.int32
AF = mybir.ActivationFunctionType
ALU = mybir.AluOpType

NEG = -30000.0
EPS = 1e-6
THETA = 500000.0
DIL = ((128, 1), (512, 4), (2048, 16))
DBACK = 2048


class Cfg:
    def __init__(s, D=2048, H=16, R=512, DH=8, S=16384, NCB=4):
        s.D, s.H, s.R, s.DH, s.S, s.NCB = D, H, R, DH, S, NCB
        s.ND, s.NR = D // 128, R // 128
        s.CH = S // NCB
        s.W = S
        s.NU = s.CH // 512
        s.NB = s.W // 128
        s.DW = DBACK + s.CH
        s.NBD = s.DW // 128
        s.NGD = s.DW // 512
        s.QM, s.KVA, s.ZM, s.QKVD, s.ZD = H * 192, R + 64, H * 128, 9 * DH * 128, DH * 128
        s.o_q = 0
        s.o_kva = s.QM
        s.o_zm = s.o_kva + s.KVA
        s.o_qkvd = s.o_zm + s.ZM
        s.o_zd = s.o_qkvd + s.QKVD
        s.o_glm = s.o_zd + s.ZD
        s.o_gld = s.o_glm + D
        s.IN = s.o_gld + D
        s.HV = H * 128
        s.DV = DH * 128


class Buf:
    __slots__ = ("t", "name", "writes", "reads", "excl")

    def __init__(self, t, name, excl=False):
        self.t, self.name, self.writes, self.reads, self.excl = t, name, {}, {}, excl

    def __getitem__(self, idx):
        return self.t[idx]


class Prog:
    NDMA = 16
    GEN = 20000

    def __init__(self, nc, stack):
        self.nc = nc
        self.eng = {"pe": nc.tensor, "act": nc.scalar, "dve": nc.vector, "pool": nc.gpsimd, "sp": nc.sync}
        self.semobj, self.cnt, self.ring, self.ring_pos = {}, {}, {}, {}
        self.stack = stack
        self.gen = {}
        for e in ("pe", "act", "dve", "pool"):
            self.gen[e] = 0
            self.semobj[("e", e, 0)] = stack.enter_context(nc.semaphore("s_" + e))
            self.cnt[e] = 0
        for q in ("sp", "pool"):
            self.ring[q] = []
            for i in range(self.NDMA):
                s = stack.enter_context(nc.semaphore("d_%s%d" % (q, i)))
                self.semobj[("d", q, i)] = s
                self.ring[q].append(0)
            self.ring_pos[q] = 0
        self.seen = {e: {} for e in self.eng}
        self.ninst = 0
        self.uid = 0

    def sbuf(self, st, name, shape, dtype):
        self.uid += 1
        return Buf(st.enter_context(self.nc.sbuf_tensor("%s_%d" % (name, self.uid), list(shape), dtype)), name)

    def psum(self, st, name, shape, dtype):
        self.uid += 1
        if dtype == BF16 and shape[-1] * 2 < 2048:
            shape = list(shape[:-1]) + [1024]
        return Buf(st.enter_context(self.nc.psum_tensor("%s_%d" % (name, self.uid), list(shape), dtype)), name, True)

    def dram(self, name, shape, dtype, kind="Internal"):
        return Buf(self.nc.dram_tensor(name, list(shape), dtype, kind=kind).ap(), name)

    def _wait(self, e, key, val):
        if val <= 0 or self.seen[e].get(key, 0) >= val:
            return
        self.seen[e][key] = val
        self.eng[e].wait_ge(self.semobj[key], val)

    def _deps(self, e, reads, writes, wnow, skip_self):
        deps = {}
        for b in reads:
            for k, v in b.writes.items():
                if deps.get(k, 0) < v:
                    deps[k] = v
            if b.excl:
                for k, v in b.reads.items():
                    if deps.get(k, 0) < v:
                        deps[k] = v
        for b in writes:
            for d in (b.writes, b.reads):
                for k, v in d.items():
                    if deps.get(k, 0) < v:
                        deps[k] = v
        for b in wnow:
            for k, v in b.reads.items():
                if deps.get(k, 0) < v:
                    deps[k] = v
        for k, v in deps.items():
            if skip_self and k[0] == "e" and k[1] == e:
                continue
            self._wait(e, k, v)

    def _commit(self, key, val, reads, writes, wnow):
        for b in reads:
            if b.reads.get(key, 0) < val:
                b.reads[key] = val
        for b in writes:
            b.writes = {key: val}
            b.reads = {}
        for b in wnow:
            b.writes[key] = val

    def op(self, e, fn, reads=(), writes=(), acc=()):
        self._deps(e, list(reads) + list(acc), writes, (), e == "pe")
        inst = fn()
        if self.cnt[e] >= self.GEN:
            self.gen[e] += 1
            self.cnt[e] = 0
            self.semobj[("e", e, self.gen[e])] = self.stack.enter_context(self.nc.semaphore("s_%s%d" % (e, self.gen[e])))
        self.cnt[e] += 1
        key = ("e", e, self.gen[e])
        inst.then_inc(self.semobj[key], 1)
        self._commit(key, self.cnt[e], reads, writes, acc)
        self.ninst += 1

    def dma(self, q, out, in_, reads=(), writes=(), wnow=(), **kw):
        pos = self.ring_pos[q]
        self.ring_pos[q] = (pos + 1) % self.NDMA
        key = ("d", q, pos)
        self._wait(q, key, self.ring[q][pos])
        self._deps(q, reads, writes, wnow, False)
        inst = self.eng[q].dma_start(out=out, in_=in_, **kw)
        self.ring[q][pos] += 16
        inst.then_inc(self.semobj[key], 16)
        self._commit(key, self.ring[q][pos], reads, writes, wnow)
        self.ninst += 1

    def barrier(self):
        for e in self.eng:
            for k in list(self.semobj):
                if k[0] == "e":
                    if k[2] != self.gen[k[1]]:
                        continue
                    v = self.cnt[k[1]]
                else:
                    v = self.ring[k[1]][k[2]]
                self._wait(e, k, v)


def build(cfg, stop_after=99):
    c = cfg
    D, H, R, DH, ND, NR, NB, W, CH, NU = c.D, c.H, c.R, c.DH, c.ND, c.NR, c.NB, c.W, c.CH, c.NU
    DW, NBD, NGD, HV, DV = c.DW, c.NBD, c.NGD, c.HV, c.DV
    nc = bass.Bass("TRN2", target_bir_lowering=False)

    def din(name, shape, dt=F32):
        return nc.dram_tensor(name, list(shape), dt, kind="ExternalInput").ap()

    xw = din("xw", [W, D])
    posw = din("posw", [128, NB], I32)
    kbias_d = din("kbias", [128, NB])
    c_col = din("c_col", [128, ND])
    b_ada = din("b_ada", [1, 3 * D])
    g_pre = din("g_pre", [1, D])
    g_post = din("g_post", [1, D])
    g_kv = din("g_kv", [1, R])
    w_ada = din("w_ada", [D, 3 * D])
    w_in = din("w_in", [D, c.IN])
    w_kvb = din("w_kv_b", [R, H * 256])
    w_mla = din("w_mla_proj", [HV, D])
    w_dil = din("w_dil_proj", [DV, D])
    w_o = din("w_o", [D, D])
    ident_d = din("ident", [128, 128])
    freq_d = din("freq", [128, 32])
    mmask_d = din("mmask", [4, 128, 512])
    NDM = [5, 8, 20]
    dmask_d = din("dmask", [sum(NDM), 128, 512])
    y = nc.dram_tensor("y", [CH, D], F32, kind="ExternalOutput").ap()

    st0 = ExitStack()
    P = Prog(nc, st0)
    yB = Buf(y, "y")
    modrows = P.dram("modrows", [3, D], F32)
    hT_s = P.dram("hT_s", [ND, 128, DW], BF16)
    KT_s = P.dram("KT_s", [H, 128, W], BF16)
    KPE_s = P.dram("KPE_s", [64, W], BF16)
    V_s = P.dram("V_s", [H, 128, NB, 128], BF16)
    QN_s = P.dram("QN_s", [H, 128, CH], BF16)
    QR_s = P.dram("QR_s", [H // 2, 128, CH], BF16)
    ZM_s = P.dram("ZM_s", [H, 128, CH], BF16)
    ZD_s = P.dram("ZD_s", [DH, 128, CH], BF16)
    GLM_s = P.dram("GLM_s", [ND, 128, CH], BF16)
    GLD_s = P.dram("GLD_s", [ND, 128, CH], BF16)
    QD_s = P.dram("QD_s", [3 * DH, 128, CH], BF16)
    KD_s = P.dram("KD_s", [3 * DH, 128, DW], BF16)
    VD_s = P.dram("VD_s", [3 * DH, 128, NBD, 128], BF16)
    A_s = P.dram("A_s", [H, 128, CH], BF16)
    B_s = P.dram("B_s", [DH, 128, CH], BF16)
    M_s = P.dram("M_s", [ND, 128, CH], BF16)

    T = nc.tensor
    A = nc.scalar
    V = nc.vector
    mm = T.matmul

    identb = P.sbuf(st0, "identb", [128, 128], BF16)
    onesb = P.sbuf(st0, "onesb", [128, 128], BF16)
    kbias = P.sbuf(st0, "kbias", [128, NB], F32)
    P.dma("pool", identb[:, :], ident_d[:, :], writes=[identb])
    P.dma("sp", kbias[:, :], kbias_d[:, :], writes=[kbias])
    P.op("dve", lambda: V.memset(onesb[:, :], 1.0), writes=[onesb])

    def rs_from_ss(ss, rs, n):
        P.op("act", lambda: A.activation(rs[:, :], ss[:, :], AF.Sqrt, bias=epsb[:, :], scale=1.0 / n),
             reads=[ss, epsb], writes=[rs])
        P.op("dve", lambda: V.reciprocal(rs[:, :], rs[:, :]), reads=[rs], writes=[rs])

    epsb = P.sbuf(st0, "epsb", [128, 1], F32)
    P.op("dve", lambda: V.memset(epsb[:, :], EPS), writes=[epsb])
    npib = P.sbuf(st0, "npib", [128, 1], F32)
    P.op("dve", lambda: V.memset(npib[:, :], -math.pi), writes=[npib])

    st12 = ExitStack()
    COS = P.sbuf(st12, "COS", [128, NB, 32], F32)
    SIN = P.sbuf(st12, "SIN", [128, NB, 32], F32)
    with ExitStack() as st:
        sc = P.sbuf(st, "sc", [128, ND], F32)
        P.dma("sp", sc[:, :], c_col[:, :], writes=[sc])
        P.op("act", lambda: A.activation(sc[:, :], sc[:, :], AF.Silu), reads=[sc], writes=[sc])
        GA = 512 if (3 * D) % 512 == 0 else (3 * D) // 2
        slab = [P.sbuf(st, "aslab%d" % i, [128, ND, GA], F32) for i in range(2)]
        modrow = P.sbuf(st, "modrow", [1, 3 * D], F32)
        rowt = P.sbuf(st, "rowt", [1, 3 * D], F32)
        pm = [P.psum(st, "pm%d" % i, [128, 512], F32) for i in range(2)]
        w_ada_v = w_ada.rearrange("(c p) n -> p c n", p=128)
        ng = 3 * D // GA
        for g in range(ng):
            sl = slab[g % 2]
            P.dma("sp", sl[:, :, :], w_ada_v[:, :, g * GA:(g + 1) * GA], writes=[sl])
            pp = pm[g % 2]
            for dc in range(ND):
                P.op("pe", lambda dc=dc: mm(pp[0:1, 0:GA], sc[:, dc:dc + 1], sl[:, dc, :], start=(dc == 0), stop=(dc == ND - 1)),
                     reads=[sc, sl], writes=[pp] if dc == 0 else (), acc=() if dc == 0 else [pp])
            P.op("dve", lambda g=g: V.tensor_copy(modrow[0:1, g * GA:(g + 1) * GA], pp[0:1, 0:GA]),
                 reads=[pp], acc=[modrow])
        P.dma("sp", rowt[0:1, :], b_ada[:, :], writes=[rowt])
        P.op("dve", lambda: V.tensor_tensor(modrow[0:1, :], modrow[0:1, :], rowt[0:1, :], ALU.add),
             reads=[rowt], writes=[modrow])
        P.dma("sp", rowt[0:1, 0:D], g_pre[:, :], reads=[], writes=[rowt])
        P.dma("sp", rowt[0:1, D:2 * D], g_post[:, :], wnow=[rowt])
        P.op("dve", lambda: V.scalar_tensor_tensor(modrow[0:1, D:2 * D], modrow[0:1, D:2 * D], 1.0, rowt[0:1, 0:D],
                                                   ALU.add, ALU.mult), reads=[rowt], writes=[modrow])
        P.op("dve", lambda: V.tensor_tensor(modrow[0:1, 2 * D:3 * D], modrow[0:1, 2 * D:3 * D], rowt[0:1, D:2 * D], ALU.mult),
             reads=[rowt], writes=[modrow])
        P.dma("sp", modrows[0:1, :], modrow[0:1, D:2 * D], reads=[modrow], wnow=[modrows])
        P.dma("sp", modrows[1:2, :], modrow[0:1, 0:D], reads=[modrow], wnow=[modrows])
        P.dma("sp", modrows[2:3, :], modrow[0:1, 2 * D:3 * D], reads=[modrow], wnow=[modrows])
        P.barrier()
    with ExitStack() as st:
        posi = P.sbuf(st, "posi", [128, NB], I32)
        posf = P.sbuf(st, "posf", [128, NB], F32)
        freq = P.sbuf(st, "freq", [128, 32], F32)
        ang = P.sbuf(st, "ang", [128, NB, 32], F32)
        P.dma("sp", posi[:, :], posw[:, :], writes=[posi])
        P.dma("sp", freq[:, :], freq_d[:, :], writes=[freq])
        P.op("dve", lambda: V.tensor_copy(posf[:, :], posi[:, :]), reads=[posi], writes=[posf])
        for b in range(NB):
            P.op("dve", lambda b=b: V.tensor_scalar(ang[:, b, :], freq[:, :], posf[:, b:b + 1], None, ALU.mult),
                 reads=[posf, freq], writes=[ang] if b == 0 else (), acc=() if b == 0 else [ang])
        angf = ang[:, :, :].rearrange("p a b -> p (a b)")
        ti = P.sbuf(st, "ti", [128, NB * 32], I32)
        nf = P.sbuf(st, "nf", [128, NB * 32], F32)
        msk = P.sbuf(st, "msk", [128, NB * 32], F32)
        C1 = 6.28125
        C2 = 2.0 * math.pi - C1
        PI_LO = 3.1415925
        for tab, sh in ((SIN, 0.0), (COS, 0.5 * math.pi)):
            tf = tab[:, :, :].rearrange("p a b -> p (a b)")
            P.op("dve", lambda tf=tf, sh=sh: V.tensor_scalar(tf, angf, sh, None, ALU.add), reads=[ang], writes=[tab])
            P.op("dve", lambda tf=tf: V.tensor_scalar(nf[:, :], tf, 1.0 / (2.0 * math.pi), None, ALU.mult), reads=[tab], writes=[nf])
            P.op("dve", lambda: V.tensor_copy(ti[:, :], nf[:, :]), reads=[nf], writes=[ti])
            P.op("dve", lambda: V.tensor_copy(nf[:, :], ti[:, :]), reads=[ti], writes=[nf])
            P.op("dve", lambda tf=tf: V.scalar_tensor_tensor(tf, nf[:, :], -C1, tf, ALU.mult, ALU.add), reads=[nf], writes=[tab])
            P.op("dve", lambda tf=tf: V.scalar_tensor_tensor(tf, nf[:, :], -C2, tf, ALU.mult, ALU.add), reads=[nf], writes=[tab])
            P.op("dve", lambda tf=tf: V.tensor_scalar(msk[:, :], tf, math.pi, None, ALU.is_gt), reads=[tab], writes=[msk])
            P.op("dve", lambda tf=tf: V.scalar_tensor_tensor(tf, msk[:, :], -2.0 * math.pi, tf, ALU.mult, ALU.add), reads=[msk], writes=[tab])
            P.op("dve", lambda tf=tf: V.tensor_scalar(msk[:, :], tf, -math.pi, None, ALU.is_lt), reads=[tab], writes=[msk])
            P.op("dve", lambda tf=tf: V.scalar_tensor_tensor(tf, msk[:, :], 2.0 * math.pi, tf, ALU.mult, ALU.add), reads=[msk], writes=[tab])
            P.op("dve", lambda tf=tf: V.tensor_scalar(tf, tf, PI_LO, -PI_LO, ALU.min, ALU.max), reads=[], writes=[tab])
            P.op("act", lambda tf=tf: A.activation(tf, tf, AF.Sin), reads=[], writes=[tab])
        P.barrier()

    if stop_after <= 0:
        st12.close(); st0.close()
        return nc, P
    w_in_v = w_in.rearrange("(c p) n -> p c n", p=128)
    VN = min(512, HV)
    NCG = HV // VN
    HPG = VN // 128
    KH = min(8, H)
    with ExitStack() as st:
        G1b = P.sbuf(st, "G1b", [128, D], F32)
        SHb = P.sbuf(st, "SHb", [128, D], F32)
        gkvb = P.sbuf(st, "gkvb", [128, R], F32)
        P.dma("sp", G1b[:, :], modrows[0].partition_broadcast(128), reads=[modrows], writes=[G1b])
        P.dma("sp", SHb[:, :], modrows[1].partition_broadcast(128), reads=[modrows], writes=[SHb])
        P.dma("sp", gkvb[:, :], g_kv[0].partition_broadcast(128), writes=[gkvb])
        Wkva = P.sbuf(st, "Wkva", [128, ND, c.KVA], BF16)
        P.dma("pool", Wkva[:, :, :], w_in_v[:, :, c.o_kva:c.o_kva + c.KVA], writes=[Wkva])
        WkK = P.sbuf(st, "WkK", [128, NR, HV], BF16)
        WkV = P.sbuf(st, "WkV", [128, NR, HV], BF16)
        wkv_v = w_kvb.rearrange("(c p) (h e) -> p c h e", p=128, e=256)
        for cc in range(NR):
            P.dma("pool", WkK[:, cc, :].rearrange("p (h e) -> p h e", e=128), wkv_v[:, cc, :, 0:128],
                  writes=[WkK] if cc == 0 else (), wnow=() if cc == 0 else [WkK])
            P.dma("pool", WkV[:, cc, :].rearrange("p (h e) -> p h e", e=128), wkv_v[:, cc, :, 128:256],
                  writes=[WkV] if cc == 0 else (), wnow=() if cc == 0 else [WkV])
        xb = [P.sbuf(st, "xb%d" % i, [128, D], F32) for i in range(2)]
        hb = [P.sbuf(st, "hb%d" % i, [128, D], BF16) for i in range(2)]
        ss = [P.sbuf(st, "ss%d" % i, [128, 1], F32) for i in range(2)]
        rs = [P.sbuf(st, "rs%d" % i, [128, 1], F32) for i in range(2)]
        ss2 = [P.sbuf(st, "ssb%d" % i, [128, 1], F32) for i in range(2)]
        rs2 = [P.sbuf(st, "rsb%d" % i, [128, 1], F32) for i in range(2)]
        hTb = [P.sbuf(st, "hTb%d" % i, [128, ND, 128], BF16) for i in range(4)]
        ckvb = [P.sbuf(st, "ckvb%d" % i, [128, R], BF16) for i in range(2)]
        junk = P.sbuf(st, "junk", [128, R], BF16)
        Tt = [P.sbuf(st, "Tt%d" % i, [128, 128], F32) for i in range(2)]
        kpeb = [P.sbuf(st, "kpeb%d" % i, [128, 64], BF16) for i in range(2)]
        ckvTg = [P.sbuf(st, "ckvTg%d" % i, [128, NR, 512], BF16) for i in range(2)]
        kpeTg = [P.sbuf(st, "kpeTg%d" % i, [64, 512], BF16) for i in range(2)]
        Vg = [P.sbuf(st, "Vg%d" % i, [128, H, 4, 128], BF16) for i in range(2)]
        Kst = [P.sbuf(st, "Kst%d" % i, [128, KH, 512], BF16) for i in range(2)]
        pT = [P.psum(st, "pT%d" % i, [128, 512], BF16) for i in range(2)]
        pkv = P.psum(st, "pkv", [128, 512], F32)
        pkp = P.psum(st, "pkp", [128, 512], F32)
        pV = [P.psum(st, "pV%d" % i, [128, 512], F32) for i in range(2)]
        pK = [P.psum(st, "pK%d" % i, [128, 512], F32) for i in range(2)]
        nT = 0
        nV = 0
        nK = 0
        V_s_v = V_s[:, :, :, :].rearrange("h p b d -> p h b d")
        KT_s_v = KT_s[:, :, :].rearrange("h p t -> p h t")
        hT_s_v = hT_s[:, :, :].rearrange("c p t -> p c t")
        for r in range(NB):
            g, bi = r // 4, r % 4
            x_, h_, ss_, rs_ = xb[r % 2], hb[r % 2], ss[r % 2], rs[r % 2]
            P.dma("sp", x_[:, :], xw[r * 128:(r + 1) * 128, :], writes=[x_])
            P.op("act", lambda: A.activation(h_[:, :], x_[:, :], AF.Square, accum_out=ss_[:, :]),
                 reads=[x_], writes=[h_, ss_])
            rs_from_ss(ss_, rs_, D)
            P.op("dve", lambda: V.scalar_tensor_tensor(x_[:, :], x_[:, :], rs_[:, :], G1b[:, :], ALU.mult, ALU.mult),
                 reads=[rs_, G1b], writes=[x_])
            P.op("dve", lambda: V.tensor_tensor(h_[:, :], x_[:, :], SHb[:, :], ALU.add), reads=[x_, SHb], writes=[h_])
            hT_ = hTb[bi]
            for q4 in range(ND // 4 if ND >= 4 else 1):
                nq = min(4, ND)
                pt_ = pT[nT % 2]
                nT += 1
                for a in range(nq):
                    dc = q4 * 4 + a
                    P.op("pe", lambda dc=dc, a=a: T.transpose(pt_[:, a * 128:(a + 1) * 128], h_[:, dc * 128:(dc + 1) * 128], identb[:, :]),
                         reads=[h_, identb], writes=[pt_] if a == 0 else (), acc=() if a == 0 else [pt_])
                src = pt_[:, 0:nq * 128].rearrange("p (a b) -> p a b", b=128)
                dst = hT_[:, q4 * 4:q4 * 4 + nq, :]
                if q4 % 2 == 0:
                    P.op("act", lambda: A.copy(dst, src), reads=[pt_], writes=[hT_] if q4 == 0 else (), acc=() if q4 == 0 else [hT_])
                else:
                    P.op("dve", lambda: V.tensor_copy(dst, src), reads=[pt_], acc=[hT_])
            if r >= NB - NBD:
                rd = r - (NB - NBD)
                P.dma("sp", hT_s_v[:, :, rd * 128:(rd + 1) * 128], hT_[:, :, :], reads=[hT_], wnow=[hT_s])
            for dc in range(ND):
                P.op("pe", lambda dc=dc: mm(pkv[:, 0:R], hT_[:, dc, :], Wkva[:, dc, 0:R], start=(dc == 0), stop=(dc == ND - 1)),
                     reads=[hT_, Wkva], writes=[pkv] if dc == 0 else (), acc=() if dc == 0 else [pkv])
            for dc in range(ND):
                P.op("pe", lambda dc=dc: mm(pkp[:, 0:64], hT_[:, dc, :], Wkva[:, dc, R:R + 64], start=(dc == 0), stop=(dc == ND - 1)),
                     reads=[hT_, Wkva], writes=[pkp] if dc == 0 else (), acc=() if dc == 0 else [pkp])
            s2, r2, ck_ = ss2[r % 2], rs2[r % 2], ckvb[r % 2]
            P.op("act", lambda: A.activation(junk[:, :], pkv[:, 0:R], AF.Square, accum_out=s2[:, :]),
                 reads=[pkv], writes=[junk, s2])
            rs_from_ss(s2, r2, R)
            P.op("dve", lambda: V.scalar_tensor_tensor(ck_[:, :], pkv[:, 0:R], r2[:, :], gkvb[:, :], ALU.mult, ALU.mult),
                 reads=[pkv, r2, gkvb], writes=[ck_])
            t_, kp_ = Tt[r % 2], kpeb[r % 2]
            cb, sb = COS[:, r, :], SIN[:, r, :]
            x1, x2 = pkp[:, 0:32], pkp[:, 32:64]
            P.op("dve", lambda: V.tensor_tensor(t_[:, 0:32], x1, cb, ALU.mult), reads=[pkp, COS], writes=[t_])
            P.op("dve", lambda: V.tensor_tensor(t_[:, 32:64], x2, sb, ALU.mult), reads=[pkp, SIN], acc=[t_])
            P.op("dve", lambda: V.tensor_tensor(t_[:, 64:96], x2, cb, ALU.mult), reads=[pkp, COS], acc=[t_])
            P.op("dve", lambda: V.tensor_tensor(t_[:, 96:128], x1, sb, ALU.mult), reads=[pkp, SIN], acc=[t_])
            P.op("dve", lambda: V.tensor_tensor(kp_[:, 0:32], t_[:, 0:32], t_[:, 32:64], ALU.subtract), reads=[t_], writes=[kp_])
            P.op("dve", lambda: V.tensor_tensor(kp_[:, 32:64], t_[:, 64:96], t_[:, 96:128], ALU.add), reads=[t_], acc=[kp_])
            cg_, kg_ = ckvTg[g % 2], kpeTg[g % 2]
            pt_ = pT[nT % 2]
            nT += 1
            for a in range(NR):
                P.op("pe", lambda a=a: T.transpose(pt_[:, a * 128:(a + 1) * 128], ck_[:, a * 128:(a + 1) * 128], identb[:, :]),
                     reads=[ck_, identb], writes=[pt_] if a == 0 else (), acc=() if a == 0 else [pt_])
            P.op("act", lambda: A.copy(cg_[:, :, bi * 128:(bi + 1) * 128], pt_[:, 0:NR * 128].rearrange("p (a b) -> p a b", b=128)),
                 reads=[pt_], writes=[cg_] if bi == 0 else (), acc=() if bi == 0 else [cg_])
            pt_ = pT[nT % 2]
            nT += 1
            P.op("pe", lambda: T.transpose(pt_[0:64, 0:128], kp_[:, 0:64], identb[:, :]), reads=[kp_, identb], writes=[pt_])
            P.op("dve", lambda: V.tensor_copy(kg_[0:64, bi * 128:(bi + 1) * 128], pt_[0:64, 0:128]),
                 reads=[pt_], writes=[kg_] if bi == 0 else (), acc=() if bi == 0 else [kg_])
            vg_ = Vg[g % 2]
            for cg in range(NCG):
                pv_ = pV[nV % 2]
                nV += 1
                for cc in range(NR):
                    P.op("pe", lambda cc=cc, cg=cg: mm(pv_[:, 0:VN], cg_[:, cc, bi * 128:(bi + 1) * 128], WkV[:, cc, cg * VN:(cg + 1) * VN],
                                                       start=(cc == 0), stop=(cc == NR - 1)),
                         reads=[cg_, WkV], writes=[pv_] if cc == 0 else (), acc=() if cc == 0 else [pv_])
                first = (bi == 0 and cg == 0)
                P.op("act" if cg % 2 == 0 else "dve",
                     (lambda cg=cg: A.copy(vg_[:, cg * HPG:(cg + 1) * HPG, bi, :], pv_[:, 0:VN].rearrange("p (h d) -> p h d", d=128))) if cg % 2 == 0 else
                     (lambda cg=cg: V.tensor_copy(vg_[:, cg * HPG:(cg + 1) * HPG, bi, :], pv_[:, 0:VN].rearrange("p (h d) -> p h d", d=128))),
                     reads=[pv_], writes=[vg_] if first else (), acc=() if first else [vg_])
            if bi == 3:
                for h0 in range(0, H, KH):
                    ks_ = Kst[nK % 2]
                    nK += 1
                    for hh in range(KH):
                        h = h0 + hh
                        pk_ = pK[h % 2]
                        for cc in range(NR):
                            P.op("pe", lambda cc=cc, h=h: mm(pk_[:, :], WkK[:, cc, h * 128:(h + 1) * 128], cg_[:, cc, :],
                                                             start=(cc == 0), stop=(cc == NR - 1)),
                                 reads=[cg_, WkK], writes=[pk_] if cc == 0 else (), acc=() if cc == 0 else [pk_])
                        P.op("act" if h % 2 == 0 else "dve",
                             (lambda hh=hh: A.copy(ks_[:, hh, :], pk_[:, :])) if h % 2 == 0 else
                             (lambda hh=hh: V.tensor_copy(ks_[:, hh, :], pk_[:, :])),
                             reads=[pk_], writes=[ks_] if hh == 0 else (), acc=() if hh == 0 else [ks_])
                    P.dma("sp", KT_s_v[:, h0:h0 + KH, g * 512:(g + 1) * 512], ks_[:, :, :], reads=[ks_], wnow=[KT_s])
                P.dma("sp", KPE_s[:, g * 512:(g + 1) * 512], kg_[0:64, :], reads=[kg_], wnow=[KPE_s])
                P.dma("sp", V_s_v[:, :, 4 * g:4 * g + 4, :].rearrange("p h b d -> p h (b d)"),
                      vg_[:, :, :, :].rearrange("p h b d -> p h (b d)"), reads=[vg_], wnow=[V_s])
        P.barrier()

    if stop_after <= 1:
        st12.close(); st0.close()
        return nc, P
    OWN0 = NGD - NU
    with ExitStack() as st:
        Wsl = [P.sbuf(st, "Wsl%d" % i, [128, ND, 1024], BF16) for i in range(2)]
        hTg = [P.sbuf(st, "hTg%d" % i, [128, ND, 512], BF16) for i in range(2)]
        stg = [P.sbuf(st, "stg%d" % i, [128, 8, 512], BF16) for i in range(2)]
        rot = [P.sbuf(st, "rot%d" % i, [128, 1024], F32) for i in range(2)]
        qb = [P.sbuf(st, "qb%d" % i, [128, 1024], BF16) for i in range(2)]
        vst = [P.sbuf(st, "vst%d" % i, [128, DH, 4, 128], BF16) for i in range(2)]
        pF = [P.psum(st, "pF%d" % i, [128, 512], F32) for i in range(5)]
        pT2 = [P.psum(st, "pTb%d" % i, [128, 512], BF16) for i in range(2)]
        cnt = {"sl": 0, "hg": 0, "pf": 0, "st": 0, "rot": 0, "pt": 0, "vs": 0}

        def load_slab(pieces):
            sl = Wsl[cnt["sl"] % 2]
            cnt["sl"] += 1
            first = True
            for (d0, n, src) in pieces:
                P.dma("pool", sl[:, :, d0:d0 + n], src, writes=[sl] if first else (), wnow=() if first else [sl])
                first = False
            return sl

        def load_h(tg):
            hg = hTg[cnt["hg"] % 2]
            cnt["hg"] += 1
            P.dma("sp", hg[:, :, :], hT_s_v_[:, :, tg * 512:(tg + 1) * 512], reads=[hT_s], writes=[hg])
            return hg

        hT_s_v_ = hT_s[:, :, :].rearrange("c p t -> p c t")

        def fm_job(sl, ncols, func, dst_v, tgs):
            nch = ncols // 128
            for tg in tgs:
                hg = load_h(tg)
                sg = stg[cnt["st"] % 2]
                cnt["st"] += 1
                for cc in range(nch):
                    pf = pF[cnt["pf"] % 5]
                    cnt["pf"] += 1
                    for dc in range(ND):
                        P.op("pe", lambda dc=dc, cc=cc: mm(pf[:, :], sl[:, dc, cc * 128:(cc + 1) * 128], hg[:, dc, :],
                                                           start=(dc == 0), stop=(dc == ND - 1)),
                             reads=[sl, hg], writes=[pf] if dc == 0 else (), acc=() if dc == 0 else [pf])
                    P.op("act", lambda cc=cc: A.activation(sg[:, cc, :], pf[:, :], func), reads=[pf],
                         writes=[sg] if cc == 0 else (), acc=() if cc == 0 else [sg])
                t0 = (tg - OWN0) * 512
                P.dma("sp", dst_v[:, :, t0:t0 + 512], sg[:, 0:nch, :], reads=[sg], wnow=[dst_v_buf[0]])

        dst_v_buf = [None]

        def tm_block(sl, ncols, hg, bi):
            outs = []
            for hf in range((ncols + 511) // 512):
                n = min(512, ncols - hf * 512)
                pf = pF[cnt["pf"] % 5]
                cnt["pf"] += 1
                for dc in range(ND):
                    P.op("pe", lambda dc=dc, hf=hf, n=n: mm(pf[:, 0:n], hg[:, dc, bi * 128:(bi + 1) * 128], sl[:, dc, hf * 512:hf * 512 + n],
                                                            start=(dc == 0), stop=(dc == ND - 1)),
                         reads=[sl, hg], writes=[pf] if dc == 0 else (), acc=() if dc == 0 else [pf])
                outs.append((pf, n))
            return outs

        ROTM = [9]

        def rotary_tm(outs, nh, hd, half, r, step, q_):
            col = 0
            for i, (pf, n) in enumerate(outs):
                P.op("act", lambda pf=pf, n=n, col=col: A.copy(q_[:, col:col + n], pf[:, 0:n]), reads=[pf],
                     writes=[q_] if i == 0 else (), acc=() if i == 0 else [q_])
                col += n
            col = 0
            for i, (pf, n) in enumerate(outs):
                rt = rot[cnt["rot"] % 2]
                cnt["rot"] += 1
                nhh = n // hd
                pv = pf[:, 0:n].rearrange("p (h j) -> p h j", j=hd)
                x1, x2 = pv[:, :, 0:half], pv[:, :, half:2 * half]
                cb = COS[:, r, 0:32:step].unsqueeze(1).broadcast_to([128, nhh, half])
                sb = SIN[:, r, 0:32:step].unsqueeze(1).broadcast_to([128, nhh, half])
                rv = rt[:, 0:4 * nhh * half].rearrange("p (k h j) -> p k h j", k=4, j=half)
                qv = q_[:, col:col + n].rearrange("p (h j) -> p h j", j=hd)
                P.op("dve", lambda: V.tensor_tensor(rv[:, 0], x1, cb, ALU.mult), reads=[pf, COS], writes=[rt])
                P.op("dve", lambda: V.tensor_tensor(rv[:, 1], x2, sb, ALU.mult), reads=[pf, SIN], acc=[rt])
                P.op("dve", lambda: V.tensor_tensor(rv[:, 2], x2, cb, ALU.mult), reads=[pf, COS], acc=[rt])
                P.op("dve", lambda: V.tensor_tensor(rv[:, 3], x1, sb, ALU.mult), reads=[pf, SIN], acc=[rt])
                P.op("dve", lambda: V.tensor_tensor(qv[:, :, 0:half], rv[:, 0], rv[:, 1], ALU.subtract), reads=[rt],
                     writes=[q_] if i == 0 else (), acc=() if i == 0 else [q_])
                P.op("dve", lambda: V.tensor_tensor(qv[:, :, half:2 * half], rv[:, 2], rv[:, 3], ALU.add), reads=[rt], acc=[q_])
                col += n

        def transpose_to_stage(q_, nchunk, sg, bi, first):
            for c0 in range(0, nchunk, 4):
                nq = min(4, nchunk - c0)
                pt_ = pT2[cnt["pt"] % 2]
                cnt["pt"] += 1
                for a in range(nq):
                    cc = c0 + a
                    P.op("pe", lambda cc=cc, a=a: T.transpose(pt_[:, a * 128:(a + 1) * 128], q_[:, cc * 128:(cc + 1) * 128], identb[:, :]),
                         reads=[q_, identb], writes=[pt_] if a == 0 else (), acc=() if a == 0 else [pt_])
                src = pt_[:, 0:nq * 128].rearrange("p (a b) -> p a b", b=128)
                dst = sg[:, c0:c0 + nq, bi * 128:(bi + 1) * 128]
                fw = first and c0 == 0
                if (c0 // 4) % 2 == 0:
                    P.op("act", lambda: A.copy(dst, src), reads=[pt_], writes=[sg] if fw else (), acc=() if fw else [sg])
                else:
                    P.op("dve", lambda: V.tensor_copy(dst, src), reads=[pt_], writes=[sg] if fw else (), acc=() if fw else [sg])

        own_tgs = list(range(OWN0, NGD))
        all_tgs = list(range(NGD))
        if stop_after > 1.1:
            for h0 in range(0, H, 8):
                nh = min(8, H - h0)
                src = w_in_v[:, :, c.o_q + h0 * 192:c.o_q + (h0 + nh) * 192].rearrange("p c (h j) -> p c h j", j=192)
                sl = Wsl[cnt["sl"] % 2]
                cnt["sl"] += 1
                for dc in range(ND):
                    P.dma("pool", sl[:, dc, 0:nh * 128].rearrange("p (h j) -> p h j", j=128), src[:, dc, :, 0:128],
                          writes=[sl] if dc == 0 else (), wnow=() if dc == 0 else [sl])
                dst_v_buf[0] = QN_s
                fm_job(sl, nh * 128, AF.Copy, QN_s[h0:h0 + nh, :, :].rearrange("h p t -> p h t"), own_tgs)
        if stop_after > 1.3:
            for (o0, n_tot, func, dbuf) in ((c.o_zm, c.ZM, AF.Silu, ZM_s), (c.o_zd, c.ZD, AF.Silu, ZD_s),
                                             (c.o_glm, D, AF.Sigmoid, GLM_s), (c.o_gld, D, AF.Sigmoid, GLD_s)):
                for c0 in range(0, n_tot, 1024):
                    n = min(1024, n_tot - c0)
                    sl = load_slab([(0, n, w_in_v[:, :, o0 + c0:o0 + c0 + n])])
                    dst_v_buf[0] = dbuf
                    fm_job(sl, n, func, dbuf[c0 // 128:(c0 + n) // 128, :, :].rearrange("h p t -> p h t"), own_tgs)
        if stop_after > 1.5:
            for h0 in range(0, H, 16):
                nh = min(16, H - h0)
                src = w_in_v[:, :, c.o_q + h0 * 192:c.o_q + (h0 + nh) * 192].rearrange("p c (h j) -> p c h j", j=192)
                sl = Wsl[cnt["sl"] % 2]
                cnt["sl"] += 1
                for dc in range(ND):
                    P.dma("pool", sl[:, dc, 0:nh * 64].rearrange("p (h j) -> p h j", j=64), src[:, dc, :, 128:192],
                          writes=[sl] if dc == 0 else (), wnow=() if dc == 0 else [sl])
                QRM = 9
                for tg in (own_tgs if QRM >= 2 else []):
                    hg = load_h(tg)
                    sg = stg[cnt["st"] % 2]
                    cnt["st"] += 1
                    outs_n = tm_block(sl, nh * 64, hg, 0)
                    for bi in range(4):
                        r = (NB - NBD) + tg * 4 + bi
                        outs = outs_n
                        outs_n = tm_block(sl, nh * 64, hg, bi + 1) if bi < 3 else None
                        q_ = qb[(cnt.__setitem__("qb", cnt.get("qb", 0) + 1) or cnt["qb"]) % 2]
                        rotary_tm(outs, nh, 64, 32, r, 1, q_)
                        if QRM >= 3:
                            transpose_to_stage(q_, nh // 2, sg, bi, bi == 0)
                    t0 = (tg - OWN0) * 512
                    if QRM >= 4:
                        P.dma("sp", QR_s[h0 // 2:(h0 + nh) // 2, :, t0:t0 + 512].rearrange("h p t -> p h t"), sg[:, 0:nh // 2, :],
                              reads=[sg], wnow=[QR_s])
        if stop_after > 1.7:
            for which in range(3):
                for g in range(3):
                    o0 = c.o_qkvd + which * 3 * DV + g * DV
                    sl = load_slab([(0, DV, w_in_v[:, :, o0:o0 + DV])])
                    tgs = own_tgs if which == 0 else all_tgs
                    for tg in tgs:
                        hg = load_h(tg)
                        if which < 2:
                            sg = stg[cnt["st"] % 2]
                            cnt["st"] += 1
                        else:
                            vs = vst[cnt["vs"] % 2]
                            cnt["vs"] += 1
                        outs_n = tm_block(sl, DV, hg, 0)
                        for bi in range(4):
                            r = (NB - NBD) + tg * 4 + bi
                            outs = outs_n
                            outs_n = tm_block(sl, DV, hg, bi + 1) if bi < 3 else None
                            if which < 2:
                                q_ = qb[(cnt.__setitem__("qb", cnt.get("qb", 0) + 1) or cnt["qb"]) % 2]
                                rotary_tm(outs, DH, 128, 16, r, 2, q_)
                                transpose_to_stage(q_, DH, sg, bi, bi == 0)
                            else:
                                col = 0
                                for i, (pf, n) in enumerate(outs):
                                    fw = (bi == 0 and i == 0)
                                    P.op("act", lambda pf=pf, n=n, col=col: A.copy(vs[:, col // 128:(col + n) // 128, bi, :],
                                                                                   pf[:, 0:n].rearrange("p (h d) -> p h d", d=128)), reads=[pf],
                                         writes=[vs] if fw else (), acc=() if fw else [vs])
                                    col += n
                        if which == 0:
                            t0 = (tg - OWN0) * 512
                            P.dma("sp", QD_s[g * DH:(g + 1) * DH, :, t0:t0 + 512].rearrange("h p t -> p h t"), sg[:, 0:DH, :],
                                  reads=[sg], wnow=[QD_s])
                        elif which == 1:
                            P.dma("sp", KD_s[g * DH:(g + 1) * DH, :, tg * 512:(tg + 1) * 512].rearrange("h p t -> p h t"), sg[:, 0:DH, :],
                                  reads=[sg], wnow=[KD_s])
                        else:
                            P.dma("sp", VD_s[g * DH:(g + 1) * DH, :, 4 * tg:4 * tg + 4, :].rearrange("h p b d -> p h (b d)"),
                                  vs[:, :, :, :].rearrange("p h b d -> p h (b d)"), reads=[vs], wnow=[VD_s])
        P.barrier()
    st12.close()

    if stop_after <= 2:
        st0.close()
        return nc, P

    KC = 16
    with ExitStack() as st:
        mmask = P.sbuf(st, "mmask", [128, 4, 512], BF16)
        P.dma("pool", mmask[:, :, :], mmask_d.rearrange("m p q -> p m q"), writes=[mmask])
        QNt = [P.sbuf(st, "QNt%d" % i, [128, 512], BF16) for i in range(2)]
        QRt = [P.sbuf(st, "QRt%d" % i, [64, 512], BF16) for i in range(2)]
        Zt = [P.sbuf(st, "Zt%d" % i, [128, 512], BF16) for i in range(2)]
        KTc = [P.sbuf(st, "KTc%d" % i, [128, KC * 128], BF16) for i in range(3)]
        KPc = [P.sbuf(st, "KPc%d" % i, [64, KC * 128], BF16) for i in range(3)]
        Vc = [P.sbuf(st, "Vc%d" % i, [128, KC, 128], BF16) for i in range(3)]
        ptb = [P.sbuf(st, "ptb%d" % i, [128, 512], BF16) for i in range(3)]
        rD = [P.sbuf(st, "rD%d" % i, [128, 512], F32) for i in range(2)]
        at = [P.sbuf(st, "at%d" % i, [128, 512], F32) for i in range(2)]
        ab = [P.sbuf(st, "ab%d" % i, [128, 512], BF16) for i in range(2)]
        pS = [P.psum(st, "pS%d" % i, [128, 512], F32) for i in range(3)]
        pO = [P.psum(st, "pO%d" % i, [128, 512], F32) for i in range(2)]
        pD = [P.psum(st, "pD%d" % i, [128, 512], F32) for i in range(2)]
        sc_mla = 192.0 ** -0.5
        jobs = [(m, h) for m in range(NU) for h in range(H)]
        chunks = []
        for ji, (m, h) in enumerate(jobs):
            nkb = (NB - CH // 128) + 4 * m + 4
            for c0 in range(0, nkb, KC):
                chunks.append((ji, h, c0, min(KC, nkb - c0)))

        def load_chunk(ci):
            ji, h, c0, n = chunks[ci]
            k_, p_, v_ = KTc[ci % 3], KPc[ci % 3], Vc[ci % 3]
            P.dma("sp", k_[:, 0:n * 128], KT_s[h, :, c0 * 128:(c0 + n) * 128], reads=[KT_s], writes=[k_])
            P.dma("sp", p_[:, 0:n * 128], KPE_s[:, c0 * 128:(c0 + n) * 128], reads=[KPE_s], writes=[p_])
            P.dma("sp", v_[:, 0:n, :], V_s[h, :, c0:c0 + n, :], reads=[V_s], writes=[v_])

        def load_q(ji):
            m, h = jobs[ji]
            t0 = m * 512
            P.dma("sp", QNt[ji % 2][:, :], QN_s[h, :, t0:t0 + 512], reads=[QN_s], writes=[QNt[ji % 2]])
            P.dma("sp", QRt[ji % 2][:, :], QR_s[h // 2, (h % 2) * 64:(h % 2) * 64 + 64, t0:t0 + 512], reads=[QR_s], writes=[QRt[ji % 2]])
            P.dma("sp", Zt[ji % 2][:, :], ZM_s[h, :, t0:t0 + 512], reads=[ZM_s], writes=[Zt[ji % 2]])

        load_q(0)
        load_chunk(0)
        if len(chunks) > 1:
            load_chunk(1)
        ci = 0
        nt = [0]
        for ji, (m, h) in enumerate(jobs):
            if ji + 1 < len(jobs):
                load_q(ji + 1)
            qn, qr, zt = QNt[ji % 2], QRt[ji % 2], Zt[ji % 2]
            po, pd = pO[ji % 2], pD[ji % 2]
            nkb = (NB - CH // 128) + 4 * m + 4
            tl = []
            cj = ci
            kb = 0
            while kb < nkb:
                _, _, c0, n = chunks[cj]
                for i in range(n):
                    tl.append((cj, i, c0 + i, i == n - 1))
                kb = c0 + n
                cj += 1
            ci = cj

            def emit_S(t):
                cj, i, kb, last = t
                k_, p_ = KTc[cj % 3], KPc[cj % 3]
                dj = kb - (nkb - 4)
                ps, pt = pS[nt[0] % 3], ptb[nt[0] % 3]
                nt[0] += 1
                P.op("pe", lambda: mm(ps[:, :], k_[:, i * 128:(i + 1) * 128], qn[:, :], start=True, stop=False),
                     reads=[k_, qn], writes=[ps])
                P.op("pe", lambda: mm(ps[:, :], p_[0:64, i * 128:(i + 1) * 128], qr[0:64, :], start=False, stop=(dj < 0)),
                     reads=[p_, qr], acc=[ps])
                if dj >= 0:
                    P.op("pe", lambda: mm(ps[:, :], identb[:, :], mmask[:, dj, :], start=False, stop=True),
                         reads=[identb, mmask], acc=[ps])
                return ps, pt

            def emit_rest(t, ps, pt):
                cj, i, kb, last = t
                v_ = Vc[cj % 3]
                P.op("act", lambda: A.activation(pt[:, :], ps[:, :], AF.Exp, bias=kbias[:, kb:kb + 1], scale=sc_mla),
                     reads=[ps, kbias], writes=[pt])
                P.op("pe", lambda: mm(po[:, :], v_[:, i, :], pt[:, :], start=(kb == 0), stop=(kb == nkb - 1)),
                     reads=[v_, pt], writes=[po] if kb == 0 else (), acc=() if kb == 0 else [po])
                P.op("pe", lambda: mm(pd[:, :], onesb[:, :], pt[:, :], start=(kb == 0), stop=(kb == nkb - 1)),
                     reads=[onesb, pt], writes=[pd] if kb == 0 else (), acc=() if kb == 0 else [pd])
                if last and cj + 2 < len(chunks):
                    load_chunk(cj + 2)

            cur = emit_S(tl[0])
            for idx in range(len(tl)):
                nxt = emit_S(tl[idx + 1]) if idx + 1 < len(tl) else None
                emit_rest(tl[idx], *cur)
                cur = nxt
            rd_, at_, ab_ = rD[ji % 2], at[ji % 2], ab[ji % 2]
            P.op("dve", lambda: V.reciprocal(rd_[:, :], pd[:, :]), reads=[pd], writes=[rd_])
            P.op("dve", lambda: V.tensor_tensor(at_[:, :], po[:, :], rd_[:, :], ALU.mult), reads=[po, rd_], writes=[at_])
            P.op("dve", lambda: V.tensor_tensor(ab_[:, :], at_[:, :], zt[:, :], ALU.mult), reads=[at_, zt], writes=[ab_])
            P.dma("sp", A_s[h, :, m * 512:(m + 1) * 512], ab_[:, :], reads=[ab_], wnow=[A_s])
        P.barrier()

    if stop_after <= 3:
        st0.close()
        return nc, P
    with ExitStack() as st:
        dmask = P.sbuf(st, "dmask", [128, sum(NDM), 512], BF16)
        for i in range(0, sum(NDM), 4):
            n = min(4, sum(NDM) - i)
            P.dma("pool", dmask[:, i:i + n, :], dmask_d[i:i + n].rearrange("m p q -> p m q"),
                  writes=[dmask] if i == 0 else (), wnow=() if i == 0 else [dmask])
        Qt = [P.sbuf(st, "Qt%d" % i, [128, 512], BF16) for i in range(3)]
        Kt = [P.sbuf(st, "Kt%d" % i, [128, 20 * 128], BF16) for i in range(3)]
        Vt = [P.sbuf(st, "Vt%d" % i, [128, 20, 128], BF16) for i in range(3)]
        Zt = [P.sbuf(st, "Zdt%d" % i, [128, 512], BF16) for i in range(2)]
        ptb = [P.sbuf(st, "ptd%d" % i, [128, 512], BF16) for i in range(3)]
        rD = [P.sbuf(st, "rDd%d" % i, [128, 512], F32) for i in range(2)]
        at = [P.sbuf(st, "atd%d" % i, [128, 512], F32) for i in range(2)]
        ab = [P.sbuf(st, "abd%d" % i, [128, 512], BF16) for i in range(2)]
        pS = [P.psum(st, "pSd%d" % i, [128, 512], F32) for i in range(3)]
        pO = [P.psum(st, "pOd%d" % i, [128, 512], F32) for i in range(2)]
        pD = [P.psum(st, "pDd%d" % i, [128, 512], F32) for i in range(2)]
        sc_dil = 128.0 ** -0.5
        moff = [0, NDM[0], NDM[0] + NDM[1]]
        nback = [1, 4, 16]
        gjobs = [(m, hd, g) for m in range(NU) for hd in range(DH) for g in range(3)]

        def load_g(gi):
            m, hd, g = gjobs[gi]
            gh = g * DH + hd
            nb_ = nback[g] + 4
            b0 = (DBACK // 128) + 4 * m - nback[g]
            P.dma("sp", Qt[gi % 3][:, :], QD_s[gh, :, m * 512:(m + 1) * 512], reads=[QD_s], writes=[Qt[gi % 3]])
            P.dma("sp", Kt[gi % 3][:, 0:nb_ * 128], KD_s[gh, :, b0 * 128:(b0 + nb_) * 128], reads=[KD_s], writes=[Kt[gi % 3]])
            P.dma("sp", Vt[gi % 3][:, 0:nb_, :], VD_s[gh, :, b0:b0 + nb_, :], reads=[VD_s], writes=[Vt[gi % 3]])

        load_g(0)
        load_g(1)
        nt = [0]
        for gi, (m, hd, g) in enumerate(gjobs):
            if gi + 2 < len(gjobs):
                load_g(gi + 2)
            ji = gi // 3
            po, pd = pO[ji % 2], pD[ji % 2]
            if g == 0:
                P.dma("sp", Zt[ji % 2][:, :], ZD_s[hd, :, m * 512:(m + 1) * 512], reads=[ZD_s], writes=[Zt[ji % 2]])
            q_, k_, v_ = Qt[gi % 3], Kt[gi % 3], Vt[gi % 3]
            nb_ = nback[g] + 4
            b0 = (DBACK // 128) + 4 * m - nback[g]
            def emit_S(i):
                ps, pt = pS[nt[0] % 3], ptb[nt[0] % 3]
                nt[0] += 1
                P.op("pe", lambda: mm(ps[:, :], k_[:, i * 128:(i + 1) * 128], q_[:, :], start=True, stop=False),
                     reads=[k_, q_], writes=[ps])
                P.op("pe", lambda: mm(ps[:, :], identb[:, :], dmask[:, moff[g] + i, :], start=False, stop=True),
                     reads=[identb, dmask], acc=[ps])
                return ps, pt

            def emit_rest(i, ps, pt):
                first = (g == 0 and i == 0)
                last = (g == 2 and i == nb_ - 1)
                wb = (NB - NBD) + b0 + i
                P.op("act", lambda: A.activation(pt[:, :], ps[:, :], AF.Exp, bias=kbias[:, wb:wb + 1], scale=sc_dil),
                     reads=[ps, kbias], writes=[pt])
                P.op("pe", lambda: mm(po[:, :], v_[:, i, :], pt[:, :], start=first, stop=last),
                     reads=[v_, pt], writes=[po] if first else (), acc=() if first else [po])
                P.op("pe", lambda: mm(pd[:, :], onesb[:, :], pt[:, :], start=first, stop=last),
                     reads=[onesb, pt], writes=[pd] if first else (), acc=() if first else [pd])

            cur = emit_S(0)
            for i in range(nb_):
                nxt = emit_S(i + 1) if i + 1 < nb_ else None
                emit_rest(i, *cur)
                cur = nxt
            if g == 2:
                rd_, at_, ab_, zt = rD[ji % 2], at[ji % 2], ab[ji % 2], Zt[ji % 2]
                P.op("dve", lambda: V.reciprocal(rd_[:, :], pd[:, :]), reads=[pd], writes=[rd_])
                P.op("dve", lambda: V.tensor_tensor(at_[:, :], po[:, :], rd_[:, :], ALU.mult), reads=[po, rd_], writes=[at_])
                P.op("dve", lambda: V.tensor_tensor(ab_[:, :], at_[:, :], zt[:, :], ALU.mult), reads=[at_, zt], writes=[ab_])
                P.dma("sp", B_s[hd, :, m * 512:(m + 1) * 512], ab_[:, :], reads=[ab_], wnow=[B_s])
        P.barrier()

    if stop_after <= 4:
        st0.close()
        return nc, P
    with ExitStack() as st:
        Wm = P.sbuf(st, "Wm", [128, H, D], BF16)
        Wd = P.sbuf(st, "Wd", [128, DH, D], BF16)
        wm_v = w_mla.rearrange("(h p) n -> p h n", p=128)
        wd_v = w_dil.rearrange("(h p) n -> p h n", p=128)
        first = True
        for n0 in range(0, D, 1024):
            n = min(1024, D - n0)
            P.dma("pool", Wm[:, :, n0:n0 + n], wm_v[:, :, n0:n0 + n], writes=[Wm] if first else (), wnow=() if first else [Wm])
            P.dma("pool", Wd[:, :, n0:n0 + n], wd_v[:, :, n0:n0 + n], writes=[Wd] if first else (), wnow=() if first else [Wd])
            first = False
        aT = [P.sbuf(st, "aT%d" % i, [128, H, 512], BF16) for i in range(2)]
        bT = [P.sbuf(st, "bT%d" % i, [128, DH, 512], BF16) for i in range(2)]
        glm = [P.sbuf(st, "glm%d" % i, [128, 512], BF16) for i in range(2)]
        gld = [P.sbuf(st, "gld%d" % i, [128, 512], BF16) for i in range(2)]
        t1 = [P.sbuf(st, "t1%d" % i, [128, 512], F32) for i in range(2)]
        t2 = [P.sbuf(st, "t2%d" % i, [128, 512], F32) for i in range(2)]
        mst = [P.sbuf(st, "mst%d" % i, [128, ND, 512], BF16) for i in range(2)]
        pY = [P.psum(st, "pY%d" % i, [128, 512], F32) for i in range(2)]
        pZ = [P.psum(st, "pZ%d" % i, [128, 512], F32) for i in range(2)]
        k = 0
        for m in range(NU):
            a_, b_, ms = aT[m % 2], bT[m % 2], mst[m % 2]
            P.dma("sp", a_[:, :, :], A_s[:, :, m * 512:(m + 1) * 512].rearrange("h p t -> p h t"), reads=[A_s], writes=[a_])
            P.dma("sp", b_[:, :, :], B_s[:, :, m * 512:(m + 1) * 512].rearrange("h p t -> p h t"), reads=[B_s], writes=[b_])
            for cc in range(ND):
                gm, gd, py, pz, u1, u2 = glm[k % 2], gld[k % 2], pY[k % 2], pZ[k % 2], t1[k % 2], t2[k % 2]
                k += 1
                P.dma("sp", gm[:, :], GLM_s[cc, :, m * 512:(m + 1) * 512], reads=[GLM_s], writes=[gm])
                P.dma("sp", gd[:, :], GLD_s[cc, :, m * 512:(m + 1) * 512], reads=[GLD_s], writes=[gd])
                for h in range(H):
                    P.op("pe", lambda h=h, cc=cc: mm(py[:, :], Wm[:, h, cc * 128:(cc + 1) * 128], a_[:, h, :], start=(h == 0), stop=(h == H - 1)),
                         reads=[Wm, a_], writes=[py] if h == 0 else (), acc=() if h == 0 else [py])
                for h in range(DH):
                    P.op("pe", lambda h=h, cc=cc: mm(pz[:, :], Wd[:, h, cc * 128:(cc + 1) * 128], b_[:, h, :], start=(h == 0), stop=(h == DH - 1)),
                         reads=[Wd, b_], writes=[pz] if h == 0 else (), acc=() if h == 0 else [pz])
                P.op("dve", lambda: V.tensor_tensor(u1[:, :], py[:, :], gm[:, :], ALU.mult), reads=[py, gm], writes=[u1])
                P.op("dve", lambda: V.tensor_tensor(u2[:, :], pz[:, :], gd[:, :], ALU.mult), reads=[pz, gd], writes=[u2])
                P.op("dve", lambda cc=cc: V.tensor_tensor(ms[:, cc, :], u1[:, :], u2[:, :], ALU.add), reads=[u1, u2],
                     writes=[ms] if cc == 0 else (), acc=() if cc == 0 else [ms])
            P.dma("sp", M_s[:, :, m * 512:(m + 1) * 512].rearrange("c p t -> p c t"), ms[:, :, :], reads=[ms], wnow=[M_s])
        P.barrier()

    with ExitStack() as st:
        Wo = P.sbuf(st, "Wo", [128, ND, D], BF16)
        wo_v = w_o.rearrange("(c p) n -> p c n", p=128)
        first = True
        for n0 in range(0, D, 1024):
            n = min(1024, D - n0)
            P.dma("pool", Wo[:, :, n0:n0 + n], wo_v[:, :, n0:n0 + n], writes=[Wo] if first else (), wnow=() if first else [Wo])
            first = False
        GGb = P.sbuf(st, "GGb", [128, D], F32)
        P.dma("sp", GGb[:, :], modrows[2].partition_broadcast(128), reads=[modrows], writes=[GGb])
        mT = [P.sbuf(st, "mT%d" % i, [128, ND, 512], BF16) for i in range(2)]
        xb = [P.sbuf(st, "xo%d" % i, [128, D], F32) for i in range(2)]
        ob = [P.sbuf(st, "ob%d" % i, [128, D], F32) for i in range(2)]
        jb = P.sbuf(st, "jb", [128, D], BF16)
        ss = [P.sbuf(st, "sso%d" % i, [128, 1], F32) for i in range(2)]
        rs = [P.sbuf(st, "rso%d" % i, [128, 1], F32) for i in range(2)]
        NO = min(512, D)
        NOG = D // NO
        pOut = [P.psum(st, "pOut%d" % i, [128, NO], F32) for i in range(min(8, 2 * NOG))]
        k = 0
        for m in range(NU):
            mt = mT[m % 2]
            P.dma("sp", mt[:, :, :], M_s[:, :, m * 512:(m + 1) * 512].rearrange("c p t -> p c t"), reads=[M_s], writes=[mt])
            for bi in range(4):
                row = (W - CH) + m * 512 + bi * 128
                x_, o_, ss_, rs_ = xb[k % 2], ob[k % 2], ss[k % 2], rs[k % 2]
                P.dma("sp", x_[:, :], xw[row:row + 128, :], writes=[x_])
                pss = []
                for og in range(NOG):
                    po = pOut[(k * NOG + og) % len(pOut)]
                    pss.append(po)
                    for cc in range(ND):
                        P.op("pe", lambda cc=cc, og=og: mm(po[:, :], mt[:, cc, bi * 128:(bi + 1) * 128], Wo[:, cc, og * NO:(og + 1) * NO],
                                                           start=(cc == 0), stop=(cc == ND - 1)),
                             reads=[mt, Wo], writes=[po] if cc == 0 else (), acc=() if cc == 0 else [po])
                    P.op("act" if og % 2 == 0 else "dve",
                         (lambda og=og: A.copy(o_[:, og * NO:(og + 1) * NO], po[:, :])) if og % 2 == 0 else
                         (lambda og=og: V.tensor_copy(o_[:, og * NO:(og + 1) * NO], po[:, :])),
                         reads=[po], writes=[o_] if og == 0 else (), acc=() if og == 0 else [o_])
                P.op("act", lambda: A.activation(jb[:, :], o_[:, :], AF.Square, accum_out=ss_[:, :]), reads=[o_], writes=[jb, ss_])
                rs_from_ss(ss_, rs_, D)
                P.op("dve", lambda: V.scalar_tensor_tensor(o_[:, :], o_[:, :], rs_[:, :], GGb[:, :], ALU.mult, ALU.mult),
                     reads=[rs_, GGb], writes=[o_])
                P.op("dve", lambda: V.tensor_tensor(o_[:, :], o_[:, :], x_[:, :], ALU.add), reads=[x_], writes=[o_])
                orow = m * 512 + bi * 128
                P.dma("sp", y[orow:orow + 128, :], o_[:, :], reads=[o_], wnow=[yB])
                k += 1
        P.barrier()
    st0.close()
    return nc, P


def make_masks():
    kk = np.arange(128)[:, None]
    qq = np.arange(512)[None, :]
    mm_ = np.stack([np.where(j * 128 + kk <= qq, 0.0, NEG) for j in range(4)]).astype(np.float32)
    dms = []
    for (win, d), nb in zip(DIL, (1, 4, 16)):
        for i in range(nb + 4):
            o = i - nb
            delta = qq - 128 * o - kk
            ok = (delta >= 0) & (delta % d == 0) & (delta <= win)
            dms.append(np.where(ok, 0.0, NEG))
    return mm_, np.stack(dms).astype(np.float32)


def make_in_maps(cfg, x, c, positions, w_ada, b_ada, g_pre, g_post, w_in, g_kv, w_kv_b, w_mla_proj, w_dil_proj, w_o):
    cf = cfg
    B = x.shape[0]
    mmask, dmask = make_masks()
    ident = np.eye(128, dtype=np.float32)
    freq = np.tile((1.0 / (THETA ** (np.arange(0, 64, 2, dtype=np.float32) / 64.0))).astype(np.float32)[None, :], (128, 1))
    shared = {
        "b_ada": np.ascontiguousarray(b_ada[0][None, :]), "g_pre": np.ascontiguousarray(g_pre[0][None, :]),
        "g_post": np.ascontiguousarray(g_post[0][None, :]), "g_kv": np.ascontiguousarray(g_kv[0][None, :]),
        "w_ada": np.ascontiguousarray(w_ada[0]), "w_in": np.ascontiguousarray(w_in[0]),
        "w_kv_b": np.ascontiguousarray(w_kv_b[0]), "w_mla_proj": np.ascontiguousarray(w_mla_proj[0]),
        "w_dil_proj": np.ascontiguousarray(w_dil_proj[0]), "w_o": np.ascontiguousarray(w_o[0]),
        "ident": ident, "freq": np.ascontiguousarray(freq), "mmask": mmask, "dmask": dmask,
    }
    maps = []
    for b in range(B):
        for j in range(cf.NCB):
            t_end = cf.CH * (j + 1)
            t0 = t_end - cf.W
            nz = max(0, -t0)
            xw = np.zeros((cf.W, cf.D), np.float32)
            xw[nz:] = x[b, max(t0, 0):t_end]
            pw = np.zeros((cf.W,), np.int32)
            pw[nz:] = positions[b, max(t0, 0):t_end]
            kb = np.zeros((cf.W,), np.float32)
            kb[:nz] = NEG
            m = dict(shared)
            m["xw"] = xw
            m["posw"] = np.ascontiguousarray(pw.reshape(cf.NB, 128).T)
            m["kbias"] = np.ascontiguousarray(kb.reshape(cf.NB, 128).T)
            m["c_col"] = np.ascontiguousarray(c[b].reshape(cf.ND, 128).T)
            maps.append(m)
    return maps


_CACHE = {}


def kernel(x, c, positions, w_ada, b_ada, g_pre, g_post, w_in, g_kv, w_kv_b, w_mla_proj, w_dil_proj, w_o):
    cfg = Cfg()
    t0 = time.time()
    args = [np.asarray(a) for a in (x, c, positions, w_ada, b_ada, g_pre, g_post, w_in, g_kv, w_kv_b, w_mla_proj, w_dil_proj, w_o)]
    maps = make_in_maps(cfg, *args)
    if "nc" not in _CACHE:
        _CACHE["nc"] = build(cfg)[0]
    nc = _CACHE["nc"]
    print("[kernel] build+layout %.1fs" % (time.time() - t0), flush=True)
    res = run_bass_kernel_spmd(nc, maps, core_ids=list(range(len(maps))))
    print("[kernel] launch done %.1fs" % (time.time() - t0), flush=True)
    B = args[0].shape[0]
    out = np.empty((B, cfg.S, cfg.D), np.float32)
    i = 0
    for b in range(B):
        for j in range(cfg.NCB):
            out[b, cfg.CH * j:cfg.CH * (j + 1)] = res.results[i]["y"]
            i += 1
    return out
```

```python
import math
import time
from contextlib import ExitStack

import numpy as np
import concourse.bass as bass
import concourse.mybir as mybir
from concourse.bass_utils import run_bass_kernel_spmd

F32 = mybir.dt.float32
BF16 = mybir.dt.bfloat16
I32 = mybir.dt.int32
AF = mybir.ActivationFunctionType
ALU = mybir.AluOpType

NEG = -30000.0
EPS = 1e-6
THETA = 500000.0
DIL = ((128, 1), (512, 4), (2048, 16))
DBACK = 2048


class Cfg:
    def __init__(s, D=2048, H=16, R=512, DH=8, S=16384, NCB=4):
        s.D, s.H, s.R, s.DH, s.S, s.NCB = D, H, R, DH, S, NCB
        s.ND, s.NR = D // 128, R // 128
        s.CH = S // NCB
        s.W = S
        s.NU = s.CH // 512
        s.NB = s.W // 128
        s.DW = DBACK + s.CH
        s.NBD = s.DW // 128
        s.NGD = s.DW // 512
        s.QM, s.KVA, s.ZM, s.QKVD, s.ZD = H * 192, R + 64, H * 128, 9 * DH * 128, DH * 128
        s.o_q = 0
        s.o_kva = s.QM
        s.o_zm = s.o_kva + s.KVA
        s.o_qkvd = s.o_zm + s.ZM
        s.o_zd = s.o_qkvd + s.QKVD
        s.o_glm = s.o_zd + s.ZD
        s.o_gld = s.o_glm + D
        s.IN = s.o_gld + D
        s.HV = H * 128
        s.DV = DH * 128


class Buf:
    __slots__ = ("t", "name", "writes", "reads", "excl")

    def __init__(self, t, name, excl=False):
        self.t, self.name, self.writes, self.reads, self.excl = t, name, {}, {}, excl

    def __getitem__(self, idx):
        return self.t[idx]


class Prog:
    NDMA = 16
    GEN = 20000

    def __init__(self, nc, stack):
        self.nc = nc
        self.eng = {"pe": nc.tensor, "act": nc.scalar, "dve": nc.vector, "pool": nc.gpsimd, "sp": nc.sync}
        self.semobj, self.cnt, self.ring, self.ring_pos = {}, {}, {}, {}
        self.stack = stack
        self.gen = {}
        for e in ("pe", "act", "dve", "pool"):
            self.gen[e] = 0
            self.semobj[("e", e, 0)] = stack.enter_context(nc.semaphore("s_" + e))
            self.cnt[e] = 0
        for q in ("sp", "pool"):
            self.ring[q] = []
            for i in range(self.NDMA):
                s = stack.enter_context(nc.semaphore("d_%s%d" % (q, i)))
                self.semobj[("d", q, i)] = s
                self.ring[q].append(0)
            self.ring_pos[q] = 0
        self.seen = {e: {} for e in self.eng}
        self.ninst = 0
        self.uid = 0

    def sbuf(self, st, name, shape, dtype):
        self.uid += 1
        return Buf(st.enter_context(self.nc.sbuf_tensor("%s_%d" % (name, self.uid), list(shape), dtype)), name)

    def psum(self, st, name, shape, dtype):
        self.uid += 1
        if dtype == BF16 and shape[-1] * 2 < 2048:
            shape = list(shape[:-1]) + [1024]
        return Buf(st.enter_context(self.nc.psum_tensor("%s_%d" % (name, self.uid), list(shape), dtype)), name, True)

    def dram(self, name, shape, dtype, kind="Internal"):
        return Buf(self.nc.dram_tensor(name, list(shape), dtype, kind=kind).ap(), name)

    def _wait(self, e, key, val):
        if val <= 0 or self.seen[e].get(key, 0) >= val:
            return
        self.seen[e][key] = val
        self.eng[e].wait_ge(self.semobj[key], val)

    def _deps(self, e, reads, writes, wnow, skip_self):
        deps = {}
        for b in reads:
            for k, v in b.writes.items():
                if deps.get(k, 0) < v:
                    deps[k] = v
            if b.excl:
                for k, v in b.reads.items():
                    if deps.get(k, 0) < v:
                        deps[k] = v
        for b in writes:
            for d in (b.writes, b.reads):
                for k, v in d.items():
                    if deps.get(k, 0) < v:
                        deps[k] = v
        for b in wnow:
            for k, v in b.reads.items():
                if deps.get(k, 0) < v:
                    deps[k] = v
        for k, v in deps.items():
            if skip_self and k[0] == "e" and k[1] == e:
                continue
            self._wait(e, k, v)

    def _commit(self, key, val, reads, writes, wnow):
        for b in reads:
            if b.reads.get(key, 0) < val:
                b.reads[key] = val
        for b in writes:
            b.writes = {key: val}
            b.reads = {}
        for b in wnow:
            b.writes[key] = val

    def op(self, e, fn, reads=(), writes=(), acc=()):
        self._deps(e, list(reads) + list(acc), writes, (), e == "pe")
        inst = fn()
        if self.cnt[e] >= self.GEN:
            self.gen[e] += 1
            self.cnt[e] = 0
            self.semobj[("e", e, self.gen[e])] = self.stack.enter_context(self.nc.semaphore("s_%s%d" % (e, self.gen[e])))
        self.cnt[e] += 1
        key = ("e", e, self.gen[e])
        inst.then_inc(self.semobj[key], 1)
        self._commit(key, self.cnt[e], reads, writes, acc)
        self.ninst += 1

    def dma(self, q, out, in_, reads=(), writes=(), wnow=(), **kw):
        pos = self.ring_pos[q]
        self.ring_pos[q] = (pos + 1) % self.NDMA
        key = ("d", q, pos)
        self._wait(q, key, self.ring[q][pos])
        self._deps(q, reads, writes, wnow, False)
        inst = self.eng[q].dma_start(out=out, in_=in_, **kw)
        self.ring[q][pos] += 16
        inst.then_inc(self.semobj[key], 16)
        self._commit(key, self.ring[q][pos], reads, writes, wnow)
        self.ninst += 1

    def barrier(self):
        for e in self.eng:
            for k in list(self.semobj):
                if k[0] == "e":
                    if k[2] != self.gen[k[1]]:
                        continue
                    v = self.cnt[k[1]]
                else:
                    v = self.ring[k[1]][k[2]]
                self._wait(e, k, v)


def build(cfg, stop_after=99):
    c = cfg
    D, H, R, DH, ND, NR, NB, W, CH, NU = c.D, c.H, c.R, c.DH, c.ND, c.NR, c.NB, c.W, c.CH, c.NU
    DW, NBD, NGD, HV, DV = c.DW, c.NBD, c.NGD, c.HV, c.DV
    nc = bass.Bass("TRN2", target_bir_lowering=False)

    def din(name, shape, dt=F32):
        return nc.dram_tensor(name, list(shape), dt, kind="ExternalInput").ap()

    xw = din("xw", [W, D])
    posw = din("posw", [128, NB], I32)
    kbias_d = din("kbias", [128, NB])
    c_col = din("c_col", [128, ND])
    b_ada = din("b_ada", [1, 3 * D])
    g_pre = din("g_pre", [1, D])
    g_post = din("g_post", [1, D])
    g_kv = din("g_kv", [1, R])
    w_ada = din("w_ada", [D, 3 * D])
    w_in = din("w_in", [D, c.IN])
    w_kvb = din("w_kv_b", [R, H * 256])
    w_mla = din("w_mla_proj", [HV, D])
    w_dil = din("w_dil_proj", [DV, D])
    w_o = din("w_o", [D, D])
    ident_d = din("ident", [128, 128])
    freq_d = din("freq", [128, 32])
    mmask_d = din("mmask", [4, 128, 512])
    NDM = [5, 8, 20]
    dmask_d = din("dmask", [sum(NDM), 128, 512])
    y = nc.dram_tensor("y", [CH, D], F32, kind="ExternalOutput").ap()

    st0 = ExitStack()
    P = Prog(nc, st0)
    yB = Buf(y, "y")
    modrows = P.dram("modrows", [3, D], F32)
    hT_s = P.dram("hT_s", [ND, 128, DW], BF16)
    KT_s = P.dram("KT_s", [H, 128, W], BF16)
    KPE_s = P.dram("KPE_s", [64, W], BF16)
    V_s = P.dram("V_s", [H, 128, NB, 128], BF16)
    QN_s = P.dram("QN_s", [H, 128, CH], BF16)
    QR_s = P.dram("QR_s", [H // 2, 128, CH], BF16)
    ZM_s = P.dram("ZM_s", [H, 128, CH], BF16)
    ZD_s = P.dram("ZD_s", [DH, 128, CH], BF16)
    GLM_s = P.dram("GLM_s", [ND, 128, CH], BF16)
    GLD_s = P.dram("GLD_s", [ND, 128, CH], BF16)
    QD_s = P.dram("QD_s", [3 * DH, 128, CH], BF16)
    KD_s = P.dram("KD_s", [3 * DH, 128, DW], BF16)
    VD_s = P.dram("VD_s", [3 * DH, 128, NBD, 128], BF16)
    A_s = P.dram("A_s", [H, 128, CH], BF16)
    B_s = P.dram("B_s", [DH, 128, CH], BF16)
    M_s = P.dram("M_s", [ND, 128, CH], BF16)

    T = nc.tensor
    A = nc.scalar
    V = nc.vector
    mm = T.matmul

    identb = P.sbuf(st0, "identb", [128, 128], BF16)
    onesb = P.sbuf(st0, "onesb", [128, 128], BF16)
    kbias = P.sbuf(st0, "kbias", [128, NB], F32)
    P.dma("pool", identb[:, :], ident_d[:, :], writes=[identb])
    P.dma("sp", kbias[:, :], kbias_d[:, :], writes=[kbias])
    P.op("dve", lambda: V.memset(onesb[:, :], 1.0), writes=[onesb])

    def rs_from_ss(ss, rs, n):
        P.op("act", lambda: A.activation(rs[:, :], ss[:, :], AF.Sqrt, bias=epsb[:, :], scale=1.0 / n),
             reads=[ss, epsb], writes=[rs])
        P.op("dve", lambda: V.reciprocal(rs[:, :], rs[:, :]), reads=[rs], writes=[rs])

    epsb = P.sbuf(st0, "epsb", [128, 1], F32)
    P.op("dve", lambda: V.memset(epsb[:, :], EPS), writes=[epsb])
    npib = P.sbuf(st0, "npib", [128, 1], F32)
    P.op("dve", lambda: V.memset(npib[:, :], -math.pi), writes=[npib])

    st12 = ExitStack()
    COS = P.sbuf(st12, "COS", [128, NB, 32], F32)
    SIN = P.sbuf(st12, "SIN", [128, NB, 32], F32)
    with ExitStack() as st:
        sc = P.sbuf(st, "sc", [128, ND], F32)
        P.dma("sp", sc[:, :], c_col[:, :], writes=[sc])
        P.op("act", lambda: A.activation(sc[:, :], sc[:, :], AF.Silu), reads=[sc], writes=[sc])
        GA = 512 if (3 * D) % 512 == 0 else (3 * D) // 2
        slab = [P.sbuf(st, "aslab%d" % i, [128, ND, GA], F32) for i in range(2)]
        modrow = P.sbuf(st, "modrow", [1, 3 * D], F32)
        rowt = P.sbuf(st, "rowt", [1, 3 * D], F32)
        pm = [P.psum(st, "pm%d" % i, [128, 512], F32) for i in range(2)]
        w_ada_v = w_ada.rearrange("(c p) n -> p c n", p=128)
        ng = 3 * D // GA
        for g in range(ng):
            sl = slab[g % 2]
            P.dma("sp", sl[:, :, :], w_ada_v[:, :, g * GA:(g + 1) * GA], writes=[sl])
            pp = pm[g % 2]
            for dc in range(ND):
                P.op("pe", lambda dc=dc: mm(pp[0:1, 0:GA], sc[:, dc:dc + 1], sl[:, dc, :], start=(dc == 0), stop=(dc == ND - 1)),
                     reads=[sc, sl], writes=[pp] if dc == 0 else (), acc=() if dc == 0 else [pp])
            P.op("dve", lambda g=g: V.tensor_copy(modrow[0:1, g * GA:(g + 1) * GA], pp[0:1, 0:GA]),
                 reads=[pp], acc=[modrow])
        P.dma("sp", rowt[0:1, :], b_ada[:, :], writes=[rowt])
        P.op("dve", lambda: V.tensor_tensor(modrow[0:1, :], modrow[0:1, :], rowt[0:1, :], ALU.add),
             reads=[rowt], writes=[modrow])
        P.dma("sp", rowt[0:1, 0:D], g_pre[:, :], reads=[], writes=[rowt])
        P.dma("sp", rowt[0:1, D:2 * D], g_post[:, :], wnow=[rowt])
        P.op("dve", lambda: V.scalar_tensor_tensor(modrow[0:1, D:2 * D], modrow[0:1, D:2 * D], 1.0, rowt[0:1, 0:D],
                                                   ALU.add, ALU.mult), reads=[rowt], writes=[modrow])
        P.op("dve", lambda: V.tensor_tensor(modrow[0:1, 2 * D:3 * D], modrow[0:1, 2 * D:3 * D], rowt[0:1, D:2 * D], ALU.mult),
             reads=[rowt], writes=[modrow])
        P.dma("sp", modrows[0:1, :], modrow[0:1, D:2 * D], reads=[modrow], wnow=[modrows])
        P.dma("sp", modrows[1:2, :], modrow[0:1, 0:D], reads=[modrow], wnow=[modrows])
        P.dma("sp", modrows[2:3, :], modrow[0:1, 2 * D:3 * D], reads=[modrow], wnow=[modrows])
        P.barrier()
    with ExitStack() as st:
        posi = P.sbuf(st, "posi", [128, NB], I32)
        posf = P.sbuf(st, "posf", [128, NB], F32)
        freq = P.sbuf(st, "freq", [128, 32], F32)
        ang = P.sbuf(st, "ang", [128, NB, 32], F32)
        P.dma("sp", posi[:, :], posw[:, :], writes=[posi])
        P.dma("sp", freq[:, :], freq_d[:, :], writes=[freq])
        P.op("dve", lambda: V.tensor_copy(posf[:, :], posi[:, :]), reads=[posi], writes=[posf])
        for b in range(NB):
            P.op("dve", lambda b=b: V.tensor_scalar(ang[:, b, :], freq[:, :], posf[:, b:b + 1], None, ALU.mult),
                 reads=[posf, freq], writes=[ang] if b == 0 else (), acc=() if b == 0 else [ang])
        angf = ang[:, :, :].rearrange("p a b -> p (a b)")
        ti = P.sbuf(st, "ti", [128, NB * 32], I32)
        nf = P.sbuf(st, "nf", [128, NB * 32], F32)
        msk = P.sbuf(st, "msk", [128, NB * 32], F32)
        C1 = 6.28125
        C2 = 2.0 * math.pi - C1
        PI_LO = 3.1415925
        for tab, sh in ((SIN, 0.0), (COS, 0.5 * math.pi)):
            tf = tab[:, :, :].rearrange("p a b -> p (a b)")
            P.op("dve", lambda tf=tf, sh=sh: V.tensor_scalar(tf, angf, sh, None, ALU.add), reads=[ang], writes=[tab])
            P.op("dve", lambda tf=tf: V.tensor_scalar(nf[:, :], tf, 1.0 / (2.0 * math.pi), None, ALU.mult), reads=[tab], writes=[nf])
            P.op("dve", lambda: V.tensor_copy(ti[:, :], nf[:, :]), reads=[nf], writes=[ti])
            P.op("dve", lambda: V.tensor_copy(nf[:, :], ti[:, :]), reads=[ti], writes=[nf])
            P.op("dve", lambda tf=tf: V.scalar_tensor_tensor(tf, nf[:, :], -C1, tf, ALU.mult, ALU.add), reads=[nf], writes=[tab])
            P.op("dve", lambda tf=tf: V.scalar_tensor_tensor(tf, nf[:, :], -C2, tf, ALU.mult, ALU.add), reads=[nf], writes=[tab])
            P.op("dve", lambda tf=tf: V.tensor_scalar(msk[:, :], tf, math.pi, None, ALU.is_gt), reads=[tab], writes=[msk])
            P.op("dve", lambda tf=tf: V.scalar_tensor_tensor(tf, msk[:, :], -2.0 * math.pi, tf, ALU.mult, ALU.add), reads=[msk], writes=[tab])
            P.op("dve", lambda tf=tf: V.tensor_scalar(msk[:, :], tf, -math.pi, None, ALU.is_lt), reads=[tab], writes=[msk])
            P.op("dve", lambda tf=tf: V.scalar_tensor_tensor(tf, msk[:, :], 2.0 * math.pi, tf, ALU.mult, ALU.add), reads=[msk], writes=[tab])
            P.op("dve", lambda tf=tf: V.tensor_scalar(tf, tf, PI_LO, -PI_LO, ALU.min, ALU.max), reads=[], writes=[tab])
            P.op("act", lambda tf=tf: A.activation(tf, tf, AF.Sin), reads=[], writes=[tab])
        P.barrier()

    if stop_after <= 0:
        st12.close(); st0.close()
        return nc, P
    w_in_v = w_in.rearrange("(c p) n -> p c n", p=128)
    VN = min(512, HV)
    NCG = HV // VN
    HPG = VN // 128
    KH = min(8, H)
    with ExitStack() as st:
        G1b = P.sbuf(st, "G1b", [128, D], F32)
        SHb = P.sbuf(st, "SHb", [128, D], F32)
        gkvb = P.sbuf(st, "gkvb", [128, R], F32)
        P.dma("sp", G1b[:, :], modrows[0].partition_broadcast(128), reads=[modrows], writes=[G1b])
        P.dma("sp", SHb[:, :], modrows[1].partition_broadcast(128), reads=[modrows], writes=[SHb])
        P.dma("sp", gkvb[:, :], g_kv[0].partition_broadcast(128), writes=[gkvb])
        Wkva = P.sbuf(st, "Wkva", [128, ND, c.KVA], BF16)
        P.dma("pool", Wkva[:, :, :], w_in_v[:, :, c.o_kva:c.o_kva + c.KVA], writes=[Wkva])
        WkK = P.sbuf(st, "WkK", [128, NR, HV], BF16)
        WkV = P.sbuf(st, "WkV", [128, NR, HV], BF16)
        wkv_v = w_kvb.rearrange("(c p) (h e) -> p c h e", p=128, e=256)
        for cc in range(NR):
            P.dma("pool", WkK[:, cc, :].rearrange("p (h e) -> p h e", e=128), wkv_v[:, cc, :, 0:128],
                  writes=[WkK] if cc == 0 else (), wnow=() if cc == 0 else [WkK])
            P.dma("pool", WkV[:, cc, :].rearrange("p (h e) -> p h e", e=128), wkv_v[:, cc, :, 128:256],
                  writes=[WkV] if cc == 0 else (), wnow=() if cc == 0 else [WkV])
        xb = [P.sbuf(st, "xb%d" % i, [128, D], F32) for i in range(2)]
        hb = [P.sbuf(st, "hb%d" % i, [128, D], BF16) for i in range(2)]
        ss = [P.sbuf(st, "ss%d" % i, [128, 1], F32) for i in range(2)]
        rs = [P.sbuf(st, "rs%d" % i, [128, 1], F32) for i in range(2)]
        ss2 = [P.sbuf(st, "ssb%d" % i, [128, 1], F32) for i in range(2)]
        rs2 = [P.sbuf(st, "rsb%d" % i, [128, 1], F32) for i in range(2)]
        hTb = [P.sbuf(st, "hTb%d" % i, [128, ND, 128], BF16) for i in range(4)]
        ckvb = [P.sbuf(st, "ckvb%d" % i, [128, R], BF16) for i in range(2)]
        junk = P.sbuf(st, "junk", [128, R], BF16)
        Tt = [P.sbuf(st, "Tt%d" % i, [128, 128], F32) for i in range(2)]
        kpeb = [P.sbuf(st, "kpeb%d" % i, [128, 64], BF16) for i in range(2)]
        ckvTg = [P.sbuf(st, "ckvTg%d" % i, [128, NR, 512], BF16) for i in range(2)]
        kpeTg = [P.sbuf(st, "kpeTg%d" % i, [64, 512], BF16) for i in range(2)]
        Vg = [P.sbuf(st, "Vg%d" % i, [128, H, 4, 128], BF16) for i in range(2)]
        Kst = [P.sbuf(st, "Kst%d" % i, [128, KH, 512], BF16) for i in range(2)]
        pT = [P.psum(st, "pT%d" % i, [128, 512], BF16) for i in range(2)]
        pkv = P.psum(st, "pkv", [128, 512], F32)
        pkp = P.psum(st, "pkp", [128, 512], F32)
        pV = [P.psum(st, "pV%d" % i, [128, 512], F32) for i in range(2)]
        pK = [P.psum(st, "pK%d" % i, [128, 512], F32) for i in range(2)]
        nT = 0
        nV = 0
        nK = 0
        V_s_v = V_s[:, :, :, :].rearrange("h p b d -> p h b d")
        KT_s_v = KT_s[:, :, :].rearrange("h p t -> p h t")
        hT_s_v = hT_s[:, :, :].rearrange("c p t -> p c t")
        for r in range(NB):
            g, bi = r // 4, r % 4
            x_, h_, ss_, rs_ = xb[r % 2], hb[r % 2], ss[r % 2], rs[r % 2]
            P.dma("sp", x_[:, :], xw[r * 128:(r + 1) * 128, :], writes=[x_])
            P.op("act", lambda: A.activation(h_[:, :], x_[:, :], AF.Square, accum_out=ss_[:, :]),
                 reads=[x_], writes=[h_, ss_])
            rs_from_ss(ss_, rs_, D)
            P.op("dve", lambda: V.scalar_tensor_tensor(x_[:, :], x_[:, :], rs_[:, :], G1b[:, :], ALU.mult, ALU.mult),
                 reads=[rs_, G1b], writes=[x_])
            P.op("dve", lambda: V.tensor_tensor(h_[:, :], x_[:, :], SHb[:, :], ALU.add), reads=[x_, SHb], writes=[h_])
            hT_ = hTb[bi]
            for q4 in range(ND // 4 if ND >= 4 else 1):
                nq = min(4, ND)
                pt_ = pT[nT % 2]
                nT += 1
                for a in range(nq):
                    dc = q4 * 4 + a
                    P.op("pe", lambda dc=dc, a=a: T.transpose(pt_[:, a * 128:(a + 1) * 128], h_[:, dc * 128:(dc + 1) * 128], identb[:, :]),
                         reads=[h_, identb], writes=[pt_] if a == 0 else (), acc=() if a == 0 else [pt_])
                src = pt_[:, 0:nq * 128].rearrange("p (a b) -> p a b", b=128)
                dst = hT_[:, q4 * 4:q4 * 4 + nq, :]
                if q4 % 2 == 0:
                    P.op("act", lambda: A.copy(dst, src), reads=[pt_], writes=[hT_] if q4 == 0 else (), acc=() if q4 == 0 else [hT_])
                else:
                    P.op("dve", lambda: V.tensor_copy(dst, src), reads=[pt_], acc=[hT_])
            if r >= NB - NBD:
                rd = r - (NB - NBD)
                P.dma("sp", hT_s_v[:, :, rd * 128:(rd + 1) * 128], hT_[:, :, :], reads=[hT_], wnow=[hT_s])
            for dc in range(ND):
                P.op("pe", lambda dc=dc: mm(pkv[:, 0:R], hT_[:, dc, :], Wkva[:, dc, 0:R], start=(dc == 0), stop=(dc == ND - 1)),
                     reads=[hT_, Wkva], writes=[pkv] if dc == 0 else (), acc=() if dc == 0 else [pkv])
            for dc in range(ND):
                P.op("pe", lambda dc=dc: mm(pkp[:, 0:64], hT_[:, dc, :], Wkva[:, dc, R:R + 64], start=(dc == 0), stop=(dc == ND - 1)),
                     reads=[hT_, Wkva], writes=[pkp] if dc == 0 else (), acc=() if dc == 0 else [pkp])
            s2, r2, ck_ = ss2[r % 2], rs2[r % 2], ckvb[r % 2]
            P.op("act", lambda: A.activation(junk[:, :], pkv[:, 0:R], AF.Square, accum_out=s2[:, :]),
                 reads=[pkv], writes=[junk, s2])
            rs_from_ss(s2, r2, R)
            P.op("dve", lambda: V.scalar_tensor_tensor(ck_[:, :], pkv[:, 0:R], r2[:, :], gkvb[:, :], ALU.mult, ALU.mult),
                 reads=[pkv, r2, gkvb], writes=[ck_])
            t_, kp_ = Tt[r % 2], kpeb[r % 2]
            cb, sb = COS[:, r, :], SIN[:, r, :]
            x1, x2 = pkp[:, 0:32], pkp[:, 32:64]
            P.op("dve", lambda: V.tensor_tensor(t_[:, 0:32], x1, cb, ALU.mult), reads=[pkp, COS], writes=[t_])
            P.op("dve", lambda: V.tensor_tensor(t_[:, 32:64], x2, sb, ALU.mult), reads=[pkp, SIN], acc=[t_])
            P.op("dve", lambda: V.tensor_tensor(t_[:, 64:96], x2, cb, ALU.mult), reads=[pkp, COS], acc=[t_])
            P.op("dve", lambda: V.tensor_tensor(t_[:, 96:128], x1, sb, ALU.mult), reads=[pkp, SIN], acc=[t_])
            P.op("dve", lambda: V.tensor_tensor(kp_[:, 0:32], t_[:, 0:32], t_[:, 32:64], ALU.subtract), reads=[t_], writes=[kp_])
            P.op("dve", lambda: V.tensor_tensor(kp_[:, 32:64], t_[:, 64:96], t_[:, 96:128], ALU.add), reads=[t_], acc=[kp_])
            cg_, kg_ = ckvTg[g % 2], kpeTg[g % 2]
            pt_ = pT[nT % 2]
            nT += 1
            for a in range(NR):
                P.op("pe", lambda a=a: T.transpose(pt_[:, a * 128:(a + 1) * 128], ck_[:, a * 128:(a + 1) * 128], identb[:, :]),
                     reads=[ck_, identb], writes=[pt_] if a == 0 else (), acc=() if a == 0 else [pt_])
            P.op("act", lambda: A.copy(cg_[:, :, bi * 128:(bi + 1) * 128], pt_[:, 0:NR * 128].rearrange("p (a b) -> p a b", b=128)),
                 reads=[pt_], writes=[cg_] if bi == 0 else (), acc=() if bi == 0 else [cg_])
            pt_ = pT[nT % 2]
            nT += 1
            P.op("pe", lambda: T.transpose(pt_[0:64, 0:128], kp_[:, 0:64], identb[:, :]), reads=[kp_, identb], writes=[pt_])
            P.op("dve", lambda: V.tensor_copy(kg_[0:64, bi * 128:(bi + 1) * 128], pt_[0:64, 0:128]),
                 reads=[pt_], writes=[kg_] if bi == 0 else (), acc=() if bi == 0 else [kg_])
            vg_ = Vg[g % 2]
            for cg in range(NCG):
                pv_ = pV[nV % 2]
                nV += 1
                for cc in range(NR):
                    P.op("pe", lambda cc=cc, cg=cg: mm(pv_[:, 0:VN], cg_[:, cc, bi * 128:(bi + 1) * 128], WkV[:, cc, cg * VN:(cg + 1) * VN],
                                                       start=(cc == 0), stop=(cc == NR - 1)),
                         reads=[cg_, WkV], writes=[pv_] if cc == 0 else (), acc=() if cc == 0 else [pv_])
                first = (bi == 0 and cg == 0)
                P.op("act" if cg % 2 == 0 else "dve",
                     (lambda cg=cg: A.copy(vg_[:, cg * HPG:(cg + 1) * HPG, bi, :], pv_[:, 0:VN].rearrange("p (h d) -> p h d", d=128))) if cg % 2 == 0 else
                     (lambda cg=cg: V.tensor_copy(vg_[:, cg * HPG:(cg + 1) * HPG, bi, :], pv_[:, 0:VN].rearrange("p (h d) -> p h d", d=128))),
                     reads=[pv_], writes=[vg_] if first else (), acc=() if first else [vg_])
            if bi == 3:
                for h0 in range(0, H, KH):
                    ks_ = Kst[nK % 2]
                    nK += 1
                    for hh in range(KH):
                        h = h0 + hh
                        pk_ = pK[h % 2]
                        for cc in range(NR):
                            P.op("pe", lambda cc=cc, h=h: mm(pk_[:, :], WkK[:, cc, h * 128:(h + 1) * 128], cg_[:, cc, :],
                                                             start=(cc == 0), stop=(cc == NR - 1)),
                                 reads=[cg_, WkK], writes=[pk_] if cc == 0 else (), acc=() if cc == 0 else [pk_])
                        P.op("act" if h % 2 == 0 else "dve",
                             (lambda hh=hh: A.copy(ks_[:, hh, :], pk_[:, :])) if h % 2 == 0 else
                             (lambda hh=hh: V.tensor_copy(ks_[:, hh, :], pk_[:, :])),
                             reads=[pk_], writes=[ks_] if hh == 0 else (), acc=() if hh == 0 else [ks_])
                    P.dma("sp", KT_s_v[:, h0:h0 + KH, g * 512:(g + 1) * 512], ks_[:, :, :], reads=[ks_], wnow=[KT_s])
                P.dma("sp", KPE_s[:, g * 512:(g + 1) * 512], kg_[0:64, :], reads=[kg_], wnow=[KPE_s])
                P.dma("sp", V_s_v[:, :, 4 * g:4 * g + 4, :].rearrange("p h b d -> p h (b d)"),
                      vg_[:, :, :, :].rearrange("p h b d -> p h (b d)"), reads=[vg_], wnow=[V_s])
        P.barrier()

    if stop_after <= 1:
        st12.close(); st0.close()
        return nc, P
    OWN0 = NGD - NU
    with ExitStack() as st:
        Wsl = [P.sbuf(st, "Wsl%d" % i, [128, ND, 1024], BF16) for i in range(2)]
        hTg = [P.sbuf(st, "hTg%d" % i, [128, ND, 512], BF16) for i in range(2)]
        stg = [P.sbuf(st, "stg%d" % i, [128, 8, 512], BF16) for i in range(2)]
        rot = [P.sbuf(st, "rot%d" % i, [128, 1024], F32) for i in range(2)]
        qb = [P.sbuf(st, "qb%d" % i, [128, 1024], BF16) for i in range(2)]
        vst = [P.sbuf(st, "vst%d" % i, [128, DH, 4, 128], BF16) for i in range(2)]
        pF = [P.psum(st, "pF%d" % i, [128, 512], F32) for i in range(5)]
        pT2 = [P.psum(st, "pTb%d" % i, [128, 512], BF16) for i in range(2)]
        cnt = {"sl": 0, "hg": 0, "pf": 0, "st": 0, "rot": 0, "pt": 0, "vs": 0}

        def load_slab(pieces):
            sl = Wsl[cnt["sl"] % 2]
            cnt["sl"] += 1
            first = True
            for (d0, n, src) in pieces:
                P.dma("pool", sl[:, :, d0:d0 + n], src, writes=[sl] if first else (), wnow=() if first else [sl])
                first = False
            return sl

        def load_h(tg):
            hg = hTg[cnt["hg"] % 2]
            cnt["hg"] += 1
            P.dma("sp", hg[:, :, :], hT_s_v_[:, :, tg * 512:(tg + 1) * 512], reads=[hT_s], writes=[hg])
            return hg

        hT_s_v_ = hT_s[:, :, :].rearrange("c p t -> p c t")

        def h_iter(tgs):
            nxt = load_h(tgs[0])
            for i_, tg_ in enumerate(tgs):
                cur = nxt
                nxt = load_h(tgs[i_ + 1]) if i_ + 1 < len(tgs) else None
                yield tg_, cur

        def fm_job(sl, ncols, func, dst_v, tgs):
            nch = ncols // 128
            for tg, hg in h_iter(tgs):
                sg = stg[cnt["st"] % 2]
                cnt["st"] += 1
                for cc in range(nch):
                    pf = pF[cnt["pf"] % 5]
                    cnt["pf"] += 1
                    for dc in range(ND):
                        P.op("pe", lambda dc=dc, cc=cc: mm(pf[:, :], sl[:, dc, cc * 128:(cc + 1) * 128], hg[:, dc, :],
                                                           start=(dc == 0), stop=(dc == ND - 1)),
                             reads=[sl, hg], writes=[pf] if dc == 0 else (), acc=() if dc == 0 else [pf])
                    P.op("act", lambda cc=cc: A.activation(sg[:, cc, :], pf[:, :], func), reads=[pf],
                         writes=[sg] if cc == 0 else (), acc=() if cc == 0 else [sg])
                t0 = (tg - OWN0) * 512
                P.dma("sp", dst_v[:, :, t0:t0 + 512], sg[:, 0:nch, :], reads=[sg], wnow=[dst_v_buf[0]])

        dst_v_buf = [None]

        def tm_block(sl, ncols, hg, bi):
            outs = []
            for hf in range((ncols + 511) // 512):
                n = min(512, ncols - hf * 512)
                pf = pF[cnt["pf"] % 5]
                cnt["pf"] += 1
                for dc in range(ND):
                    P.op("pe", lambda dc=dc, hf=hf, n=n: mm(pf[:, 0:n], hg[:, dc, bi * 128:(bi + 1) * 128], sl[:, dc, hf * 512:hf * 512 + n],
                                                            start=(dc == 0), stop=(dc == ND - 1)),
                         reads=[sl, hg], writes=[pf] if dc == 0 else (), acc=() if dc == 0 else [pf])
                outs.append((pf, n))
            return outs

        ROTM = [9]

        def rotary_tm(outs, nh, hd, half, r, step, q_):
            col = 0
            for i, (pf, n) in enumerate(outs):
                P.op("act", lambda pf=pf, n=n, col=col: A.copy(q_[:, col:col + n], pf[:, 0:n]), reads=[pf],
                     writes=[q_] if i == 0 else (), acc=() if i == 0 else [q_])
                col += n
            col = 0
            for i, (pf, n) in enumerate(outs):
                rt = rot[cnt["rot"] % 2]
                cnt["rot"] += 1
                nhh = n // hd
                pv = pf[:, 0:n].rearrange("p (h j) -> p h j", j=hd)
                x1, x2 = pv[:, :, 0:half], pv[:, :, half:2 * half]
                cb = COS[:, r, 0:32:step].unsqueeze(1).broadcast_to([128, nhh, half])
                sb = SIN[:, r, 0:32:step].unsqueeze(1).broadcast_to([128, nhh, half])
                rv = rt[:, 0:4 * nhh * half].rearrange("p (k h j) -> p k h j", k=4, j=half)
                qv = q_[:, col:col + n].rearrange("p (h j) -> p h j", j=hd)
                P.op("dve", lambda: V.tensor_tensor(rv[:, 0], x1, cb, ALU.mult), reads=[pf, COS], writes=[rt])
                P.op("dve", lambda: V.tensor_tensor(rv[:, 1], x2, sb, ALU.mult), reads=[pf, SIN], acc=[rt])
                P.op("dve", lambda: V.tensor_tensor(rv[:, 2], x2, cb, ALU.mult), reads=[pf, COS], acc=[rt])
                P.op("dve", lambda: V.tensor_tensor(rv[:, 3], x1, sb, ALU.mult), reads=[pf, SIN], acc=[rt])
                P.op("dve", lambda: V.tensor_tensor(qv[:, :, 0:half], rv[:, 0], rv[:, 1], ALU.subtract), reads=[rt],
                     writes=[q_] if i == 0 else (), acc=() if i == 0 else [q_])
                P.op("dve", lambda: V.tensor_tensor(qv[:, :, half:2 * half], rv[:, 2], rv[:, 3], ALU.add), reads=[rt], acc=[q_])
                col += n

        def transpose_to_stage(q_, nchunk, sg, bi, first):
            for c0 in range(0, nchunk, 4):
                nq = min(4, nchunk - c0)
                pt_ = pT2[cnt["pt"] % 2]
                cnt["pt"] += 1
                for a in range(nq):
                    cc = c0 + a
                    P.op("pe", lambda cc=cc, a=a: T.transpose(pt_[:, a * 128:(a + 1) * 128], q_[:, cc * 128:(cc + 1) * 128], identb[:, :]),
                         reads=[q_, identb], writes=[pt_] if a == 0 else (), acc=() if a == 0 else [pt_])
                src = pt_[:, 0:nq * 128].rearrange("p (a b) -> p a b", b=128)
                dst = sg[:, c0:c0 + nq, bi * 128:(bi + 1) * 128]
                fw = first and c0 == 0
                if (c0 // 4) % 2 == 0:
                    P.op("act", lambda: A.copy(dst, src), reads=[pt_], writes=[sg] if fw else (), acc=() if fw else [sg])
                else:
                    P.op("dve", lambda: V.tensor_copy(dst, src), reads=[pt_], writes=[sg] if fw else (), acc=() if fw else [sg])

        own_tgs = list(range(OWN0, NGD))
        all_tgs = list(range(NGD))
        if stop_after > 1.1:
            for h0 in range(0, H, 8):
                nh = min(8, H - h0)
                src = w_in_v[:, :, c.o_q + h0 * 192:c.o_q + (h0 + nh) * 192].rearrange("p c (h j) -> p c h j", j=192)
                sl = Wsl[cnt["sl"] % 2]
                cnt["sl"] += 1
                for dc in range(ND):
                    P.dma("pool", sl[:, dc, 0:nh * 128].rearrange("p (h j) -> p h j", j=128), src[:, dc, :, 0:128],
                          writes=[sl] if dc == 0 else (), wnow=() if dc == 0 else [sl])
                dst_v_buf[0] = QN_s
                fm_job(sl, nh * 128, AF.Copy, QN_s[h0:h0 + nh, :, :].rearrange("h p t -> p h t"), own_tgs)
        if stop_after > 1.3:
            for (o0, n_tot, func, dbuf) in ((c.o_zm, c.ZM, AF.Silu, ZM_s), (c.o_zd, c.ZD, AF.Silu, ZD_s),
                                             (c.o_glm, D, AF.Sigmoid, GLM_s), (c.o_gld, D, AF.Sigmoid, GLD_s)):
                for c0 in range(0, n_tot, 1024):
                    n = min(1024, n_tot - c0)
                    sl = load_slab([(0, n, w_in_v[:, :, o0 + c0:o0 + c0 + n])])
                    dst_v_buf[0] = dbuf
                    fm_job(sl, n, func, dbuf[c0 // 128:(c0 + n) // 128, :, :].rearrange("h p t -> p h t"), own_tgs)
        if stop_after > 1.5:
            for h0 in range(0, H, 16):
                nh = min(16, H - h0)
                src = w_in_v[:, :, c.o_q + h0 * 192:c.o_q + (h0 + nh) * 192].rearrange("p c (h j) -> p c h j", j=192)
                sl = Wsl[cnt["sl"] % 2]
                cnt["sl"] += 1
                for dc in range(ND):
                    P.dma("pool", sl[:, dc, 0:nh * 64].rearrange("p (h j) -> p h j", j=64), src[:, dc, :, 128:192],
                          writes=[sl] if dc == 0 else (), wnow=() if dc == 0 else [sl])
                QRM = 9
                for tg, hg in h_iter(own_tgs):
                    sg = stg[cnt["st"] % 2]
                    cnt["st"] += 1
                    outs_n = tm_block(sl, nh * 64, hg, 0)
                    for bi in range(4):
                        r = (NB - NBD) + tg * 4 + bi
                        outs = outs_n
                        outs_n = tm_block(sl, nh * 64, hg, bi + 1) if bi < 3 else None
                        q_ = qb[(cnt.__setitem__("qb", cnt.get("qb", 0) + 1) or cnt["qb"]) % 2]
                        rotary_tm(outs, nh, 64, 32, r, 1, q_)
                        if QRM >= 3:
                            transpose_to_stage(q_, nh // 2, sg, bi, bi == 0)
                    t0 = (tg - OWN0) * 512
                    if QRM >= 4:
                        P.dma("sp", QR_s[h0 // 2:(h0 + nh) // 2, :, t0:t0 + 512].rearrange("h p t -> p h t"), sg[:, 0:nh // 2, :],
                              reads=[sg], wnow=[QR_s])
        if stop_after > 1.7:
            for which in range(3):
                for g in range(3):
                    o0 = c.o_qkvd + which * 3 * DV + g * DV
                    sl = load_slab([(0, DV, w_in_v[:, :, o0:o0 + DV])])
                    tgs = own_tgs if which == 0 else all_tgs
                    for tg, hg in h_iter(tgs):
                        if which < 2:
                            sg = stg[cnt["st"] % 2]
                            cnt["st"] += 1
                        else:
                            vs = vst[cnt["vs"] % 2]
                            cnt["vs"] += 1
                        outs_n = tm_block(sl, DV, hg, 0)
                        for bi in range(4):
                            r = (NB - NBD) + tg * 4 + bi
                            outs = outs_n
                            outs_n = tm_block(sl, DV, hg, bi + 1) if bi < 3 else None
                            if which < 2:
                                q_ = qb[(cnt.__setitem__("qb", cnt.get("qb", 0) + 1) or cnt["qb"]) % 2]
                                rotary_tm(outs, DH, 128, 16, r, 2, q_)
                                transpose_to_stage(q_, DH, sg, bi, bi == 0)
                            else:
                                col = 0
                                for i, (pf, n) in enumerate(outs):
                                    fw = (bi == 0 and i == 0)
                                    P.op("act", lambda pf=pf, n=n, col=col: A.copy(vs[:, col // 128:(col + n) // 128, bi, :],
                                                                                   pf[:, 0:n].rearrange("p (h d) -> p h d", d=128)), reads=[pf],
                                         writes=[vs] if fw else (), acc=() if fw else [vs])
                                    col += n
                        if which == 0:
                            t0 = (tg - OWN0) * 512
                            P.dma("sp", QD_s[g * DH:(g + 1) * DH, :, t0:t0 + 512].rearrange("h p t -> p h t"), sg[:, 0:DH, :],
                                  reads=[sg], wnow=[QD_s])
                        elif which == 1:
                            P.dma("sp", KD_s[g * DH:(g + 1) * DH, :, tg * 512:(tg + 1) * 512].rearrange("h p t -> p h t"), sg[:, 0:DH, :],
                                  reads=[sg], wnow=[KD_s])
                        else:
                            P.dma("sp", VD_s[g * DH:(g + 1) * DH, :, 4 * tg:4 * tg + 4, :].rearrange("h p b d -> p h (b d)"),
                                  vs[:, :, :, :].rearrange("p h b d -> p h (b d)"), reads=[vs], wnow=[VD_s])
        P.barrier()
    st12.close()

    if stop_after <= 2:
        st0.close()
        return nc, P

    KC = 16
    with ExitStack() as st:
        mmask = P.sbuf(st, "mmask", [128, 4, 512], BF16)
        P.dma("pool", mmask[:, :, :], mmask_d.rearrange("m p q -> p m q"), writes=[mmask])
        QNt = [P.sbuf(st, "QNt%d" % i, [128, 512], BF16) for i in range(2)]
        QRt = [P.sbuf(st, "QRt%d" % i, [64, 512], BF16) for i in range(2)]
        Zt = [P.sbuf(st, "Zt%d" % i, [128, 512], BF16) for i in range(2)]
        KTc = [P.sbuf(st, "KTc%d" % i, [128, KC * 128], BF16) for i in range(3)]
        KPc = [P.sbuf(st, "KPc%d" % i, [64, KC * 128], BF16) for i in range(3)]
        Vc = [P.sbuf(st, "Vc%d" % i, [128, KC, 128], BF16) for i in range(3)]
        ptb = [P.sbuf(st, "ptb%d" % i, [128, 512], BF16) for i in range(3)]
        rD = [P.sbuf(st, "rD%d" % i, [128, 512], F32) for i in range(2)]
        at = [P.sbuf(st, "at%d" % i, [128, 512], F32) for i in range(2)]
        ab = [P.sbuf(st, "ab%d" % i, [128, 512], BF16) for i in range(2)]
        pS = [P.psum(st, "pS%d" % i, [128, 512], F32) for i in range(3)]
        pO = [P.psum(st, "pO%d" % i, [128, 512], F32) for i in range(2)]
        pD = [P.psum(st, "pD%d" % i, [128, 512], F32) for i in range(2)]
        sc_mla = 192.0 ** -0.5
        jobs = [(m, h) for m in range(NU) for h in range(H)]
        chunks = []
        for ji, (m, h) in enumerate(jobs):
            nkb = (NB - CH // 128) + 4 * m + 4
            for c0 in range(0, nkb, KC):
                chunks.append((ji, h, c0, min(KC, nkb - c0)))

        def load_chunk(ci):
            ji, h, c0, n = chunks[ci]
            k_, p_, v_ = KTc[ci % 3], KPc[ci % 3], Vc[ci % 3]
            P.dma("sp", k_[:, 0:n * 128], KT_s[h, :, c0 * 128:(c0 + n) * 128], reads=[KT_s], writes=[k_])
            P.dma("sp", p_[:, 0:n * 128], KPE_s[:, c0 * 128:(c0 + n) * 128], reads=[KPE_s], writes=[p_])
            P.dma("sp", v_[:, 0:n, :], V_s[h, :, c0:c0 + n, :], reads=[V_s], writes=[v_])

        def load_q(ji):
            m, h = jobs[ji]
            t0 = m * 512
            P.dma("sp", QNt[ji % 2][:, :], QN_s[h, :, t0:t0 + 512], reads=[QN_s], writes=[QNt[ji % 2]])
            P.dma("sp", QRt[ji % 2][:, :], QR_s[h // 2, (h % 2) * 64:(h % 2) * 64 + 64, t0:t0 + 512], reads=[QR_s], writes=[QRt[ji % 2]])
            P.dma("sp", Zt[ji % 2][:, :], ZM_s[h, :, t0:t0 + 512], reads=[ZM_s], writes=[Zt[ji % 2]])

        load_q(0)
        load_chunk(0)
        if len(chunks) > 1:
            load_chunk(1)
        ci = 0
        nt = [0]
        for ji, (m, h) in enumerate(jobs):
            if ji + 1 < len(jobs):
                load_q(ji + 1)
            qn, qr, zt = QNt[ji % 2], QRt[ji % 2], Zt[ji % 2]
            po, pd = pO[ji % 2], pD[ji % 2]
            nkb = (NB - CH // 128) + 4 * m + 4
            tl = []
            cj = ci
            kb = 0
            while kb < nkb:
                _, _, c0, n = chunks[cj]
                for i in range(n):
                    tl.append((cj, i, c0 + i, i == n - 1))
                kb = c0 + n
                cj += 1
            ci = cj

            def emit_S(t):
                cj, i, kb, last = t
                k_, p_ = KTc[cj % 3], KPc[cj % 3]
                dj = kb - (nkb - 4)
                ps, pt = pS[nt[0] % 3], ptb[nt[0] % 3]
                nt[0] += 1
                P.op("pe", lambda: mm(ps[:, :], k_[:, i * 128:(i + 1) * 128], qn[:, :], start=True, stop=False),
                     reads=[k_, qn], writes=[ps])
                P.op("pe", lambda: mm(ps[:, :], p_[0:64, i * 128:(i + 1) * 128], qr[0:64, :], start=False, stop=(dj < 0)),
                     reads=[p_, qr], acc=[ps])
                if dj >= 0:
                    P.op("pe", lambda: mm(ps[:, :], identb[:, :], mmask[:, dj, :], start=False, stop=True),
                         reads=[identb, mmask], acc=[ps])
                return ps, pt

            def emit_rest(t, ps, pt):
                cj, i, kb, last = t
                v_ = Vc[cj % 3]
                P.op("act", lambda: A.activation(pt[:, :], ps[:, :], AF.Exp, bias=kbias[:, kb:kb + 1], scale=sc_mla),
                     reads=[ps, kbias], writes=[pt])
                P.op("pe", lambda: mm(po[:, :], v_[:, i, :], pt[:, :], start=(kb == 0), stop=(kb == nkb - 1)),
                     reads=[v_, pt], writes=[po] if kb == 0 else (), acc=() if kb == 0 else [po])
                P.op("pe", lambda: mm(pd[:, :], onesb[:, :], pt[:, :], start=(kb == 0), stop=(kb == nkb - 1)),
                     reads=[onesb, pt], writes=[pd] if kb == 0 else (), acc=() if kb == 0 else [pd])
                if last and cj + 2 < len(chunks):
                    load_chunk(cj + 2)

            cur = emit_S(tl[0])
            for idx in range(len(tl)):
                nxt = emit_S(tl[idx + 1]) if idx + 1 < len(tl) else None
                emit_rest(tl[idx], *cur)
                cur = nxt
            rd_, at_, ab_ = rD[ji % 2], at[ji % 2], ab[ji % 2]
            P.op("dve", lambda: V.reciprocal(rd_[:, :], pd[:, :]), reads=[pd], writes=[rd_])
            P.op("dve", lambda: V.tensor_tensor(at_[:, :], po[:, :], rd_[:, :], ALU.mult), reads=[po, rd_], writes=[at_])
            P.op("dve", lambda: V.tensor_tensor(ab_[:, :], at_[:, :], zt[:, :], ALU.mult), reads=[at_, zt], writes=[ab_])
            P.dma("sp", A_s[h, :, m * 512:(m + 1) * 512], ab_[:, :], reads=[ab_], wnow=[A_s])
        P.barrier()

    if stop_after <= 3:
        st0.close()
        return nc, P
    with ExitStack() as st:
        dmask = P.sbuf(st, "dmask", [128, sum(NDM), 512], BF16)
        for i in range(0, sum(NDM), 4):
            n = min(4, sum(NDM) - i)
            P.dma("pool", dmask[:, i:i + n, :], dmask_d[i:i + n].rearrange("m p q -> p m q"),
                  writes=[dmask] if i == 0 else (), wnow=() if i == 0 else [dmask])
        Qt = [P.sbuf(st, "Qt%d" % i, [128, 512], BF16) for i in range(3)]
        Kt = [P.sbuf(st, "Kt%d" % i, [128, 20 * 128], BF16) for i in range(3)]
        Vt = [P.sbuf(st, "Vt%d" % i, [128, 20, 128], BF16) for i in range(3)]
        Zt = [P.sbuf(st, "Zdt%d" % i, [128, 512], BF16) for i in range(2)]
        ptb = [P.sbuf(st, "ptd%d" % i, [128, 512], BF16) for i in range(3)]
        rD = [P.sbuf(st, "rDd%d" % i, [128, 512], F32) for i in range(2)]
        at = [P.sbuf(st, "atd%d" % i, [128, 512], F32) for i in range(2)]
        ab = [P.sbuf(st, "abd%d" % i, [128, 512], BF16) for i in range(2)]
        pS = [P.psum(st, "pSd%d" % i, [128, 512], F32) for i in range(3)]
        pO = [P.psum(st, "pOd%d" % i, [128, 512], F32) for i in range(2)]
        pD = [P.psum(st, "pDd%d" % i, [128, 512], F32) for i in range(2)]
        sc_dil = 128.0 ** -0.5
        moff = [0, NDM[0], NDM[0] + NDM[1]]
        nback = [1, 4, 16]
        gjobs = [(m, hd, g) for m in range(NU) for hd in range(DH) for g in range(3)]

        def load_g(gi):
            m, hd, g = gjobs[gi]
            gh = g * DH + hd
            nb_ = nback[g] + 4
            b0 = (DBACK // 128) + 4 * m - nback[g]
            P.dma("sp", Qt[gi % 3][:, :], QD_s[gh, :, m * 512:(m + 1) * 512], reads=[QD_s], writes=[Qt[gi % 3]])
            P.dma("sp", Kt[gi % 3][:, 0:nb_ * 128], KD_s[gh, :, b0 * 128:(b0 + nb_) * 128], reads=[KD_s], writes=[Kt[gi % 3]])
            P.dma("sp", Vt[gi % 3][:, 0:nb_, :], VD_s[gh, :, b0:b0 + nb_, :], reads=[VD_s], writes=[Vt[gi % 3]])

        load_g(0)
        load_g(1)
        nt = [0]
        for gi, (m, hd, g) in enumerate(gjobs):
            if gi + 2 < len(gjobs):
                load_g(gi + 2)
            ji = gi // 3
            po, pd = pO[ji % 2], pD[ji % 2]
            if g == 0:
                P.dma("sp", Zt[ji % 2][:, :], ZD_s[hd, :, m * 512:(m + 1) * 512], reads=[ZD_s], writes=[Zt[ji % 2]])
            q_, k_, v_ = Qt[gi % 3], Kt[gi % 3], Vt[gi % 3]
            nb_ = nback[g] + 4
            b0 = (DBACK // 128) + 4 * m - nback[g]
            def emit_S(i):
                ps, pt = pS[nt[0] % 3], ptb[nt[0] % 3]
                nt[0] += 1
                P.op("pe", lambda: mm(ps[:, :], k_[:, i * 128:(i + 1) * 128], q_[:, :], start=True, stop=False),
                     reads=[k_, q_], writes=[ps])
                P.op("pe", lambda: mm(ps[:, :], identb[:, :], dmask[:, moff[g] + i, :], start=False, stop=True),
                     reads=[identb, dmask], acc=[ps])
                return ps, pt

            def emit_rest(i, ps, pt):
                first = (g == 0 and i == 0)
                last = (g == 2 and i == nb_ - 1)
                wb = (NB - NBD) + b0 + i
                P.op("act", lambda: A.activation(pt[:, :], ps[:, :], AF.Exp, bias=kbias[:, wb:wb + 1], scale=sc_dil),
                     reads=[ps, kbias], writes=[pt])
                P.op("pe", lambda: mm(po[:, :], v_[:, i, :], pt[:, :], start=first, stop=last),
                     reads=[v_, pt], writes=[po] if first else (), acc=() if first else [po])
                P.op("pe", lambda: mm(pd[:, :], onesb[:, :], pt[:, :], start=first, stop=last),
                     reads=[onesb, pt], writes=[pd] if first else (), acc=() if first else [pd])

            cur = emit_S(0)
            for i in range(nb_):
                nxt = emit_S(i + 1) if i + 1 < nb_ else None
                emit_rest(i, *cur)
                cur = nxt
            if g == 2:
                rd_, at_, ab_, zt = rD[ji % 2], at[ji % 2], ab[ji % 2], Zt[ji % 2]
                P.op("dve", lambda: V.reciprocal(rd_[:, :], pd[:, :]), reads=[pd], writes=[rd_])
                P.op("dve", lambda: V.tensor_tensor(at_[:, :], po[:, :], rd_[:, :], ALU.mult), reads=[po, rd_], writes=[at_])
                P.op("dve", lambda: V.tensor_tensor(ab_[:, :], at_[:, :], zt[:, :], ALU.mult), reads=[at_, zt], writes=[ab_])
                P.dma("sp", B_s[hd, :, m * 512:(m + 1) * 512], ab_[:, :], reads=[ab_], wnow=[B_s])
        P.barrier()

    if stop_after <= 4:
        st0.close()
        return nc, P
    with ExitStack() as st:
        Wm = P.sbuf(st, "Wm", [128, H, D], BF16)
        Wd = P.sbuf(st, "Wd", [128, DH, D], BF16)
        wm_v = w_mla.rearrange("(h p) n -> p h n", p=128)
        wd_v = w_dil.rearrange("(h p) n -> p h n", p=128)
        first = True
        for n0 in range(0, D, 1024):
            n = min(1024, D - n0)
            P.dma("pool", Wm[:, :, n0:n0 + n], wm_v[:, :, n0:n0 + n], writes=[Wm] if first else (), wnow=() if first else [Wm])
            P.dma("pool", Wd[:, :, n0:n0 + n], wd_v[:, :, n0:n0 + n], writes=[Wd] if first else (), wnow=() if first else [Wd])
            first = False
        aT = [P.sbuf(st, "aT%d" % i, [128, H, 512], BF16) for i in range(2)]
        bT = [P.sbuf(st, "bT%d" % i, [128, DH, 512], BF16) for i in range(2)]
        glm = [P.sbuf(st, "glm%d" % i, [128, 512], BF16) for i in range(2)]
        gld = [P.sbuf(st, "gld%d" % i, [128, 512], BF16) for i in range(2)]
        t1 = [P.sbuf(st, "t1%d" % i, [128, 512], F32) for i in range(2)]
        t2 = [P.sbuf(st, "t2%d" % i, [128, 512], F32) for i in range(2)]
        mst = [P.sbuf(st, "mst%d" % i, [128, ND, 512], BF16) for i in range(2)]
        pY = [P.psum(st, "pY%d" % i, [128, 512], F32) for i in range(2)]
        pZ = [P.psum(st, "pZ%d" % i, [128, 512], F32) for i in range(2)]
        k = 0
        for m in range(NU):
            a_, b_, ms = aT[m % 2], bT[m % 2], mst[m % 2]
            P.dma("sp", a_[:, :, :], A_s[:, :, m * 512:(m + 1) * 512].rearrange("h p t -> p h t"), reads=[A_s], writes=[a_])
            P.dma("sp", b_[:, :, :], B_s[:, :, m * 512:(m + 1) * 512].rearrange("h p t -> p h t"), reads=[B_s], writes=[b_])
            for cc in range(ND):
                gm, gd, py, pz, u1, u2 = glm[k % 2], gld[k % 2], pY[k % 2], pZ[k % 2], t1[k % 2], t2[k % 2]
                k += 1
                P.dma("sp", gm[:, :], GLM_s[cc, :, m * 512:(m + 1) * 512], reads=[GLM_s], writes=[gm])
                P.dma("sp", gd[:, :], GLD_s[cc, :, m * 512:(m + 1) * 512], reads=[GLD_s], writes=[gd])
                for h in range(H):
                    P.op("pe", lambda h=h, cc=cc: mm(py[:, :], Wm[:, h, cc * 128:(cc + 1) * 128], a_[:, h, :], start=(h == 0), stop=(h == H - 1)),
                         reads=[Wm, a_], writes=[py] if h == 0 else (), acc=() if h == 0 else [py])
                for h in range(DH):
                    P.op("pe", lambda h=h, cc=cc: mm(pz[:, :], Wd[:, h, cc * 128:(cc + 1) * 128], b_[:, h, :], start=(h == 0), stop=(h == DH - 1)),
                         reads=[Wd, b_], writes=[pz] if h == 0 else (), acc=() if h == 0 else [pz])
                P.op("dve", lambda: V.tensor_tensor(u1[:, :], py[:, :], gm[:, :], ALU.mult), reads=[py, gm], writes=[u1])
                P.op("dve", lambda: V.tensor_tensor(u2[:, :], pz[:, :], gd[:, :], ALU.mult), reads=[pz, gd], writes=[u2])
                P.op("dve", lambda cc=cc: V.tensor_tensor(ms[:, cc, :], u1[:, :], u2[:, :], ALU.add), reads=[u1, u2],
                     writes=[ms] if cc == 0 else (), acc=() if cc == 0 else [ms])
            P.dma("sp", M_s[:, :, m * 512:(m + 1) * 512].rearrange("c p t -> p c t"), ms[:, :, :], reads=[ms], wnow=[M_s])
        P.barrier()

    with ExitStack() as st:
        Wo = P.sbuf(st, "Wo", [128, ND, D], BF16)
        wo_v = w_o.rearrange("(c p) n -> p c n", p=128)
        first = True
        for n0 in range(0, D, 1024):
            n = min(1024, D - n0)
            P.dma("pool", Wo[:, :, n0:n0 + n], wo_v[:, :, n0:n0 + n], writes=[Wo] if first else (), wnow=() if first else [Wo])
            first = False
        GGb = P.sbuf(st, "GGb", [128, D], F32)
        P.dma("sp", GGb[:, :], modrows[2].partition_broadcast(128), reads=[modrows], writes=[GGb])
        mT = [P.sbuf(st, "mT%d" % i, [128, ND, 512], BF16) for i in range(2)]
        xb = [P.sbuf(st, "xo%d" % i, [128, D], F32) for i in range(2)]
        ob = [P.sbuf(st, "ob%d" % i, [128, D], F32) for i in range(2)]
        jb = P.sbuf(st, "jb", [128, D], BF16)
        ss = [P.sbuf(st, "sso%d" % i, [128, 1], F32) for i in range(2)]
        rs = [P.sbuf(st, "rso%d" % i, [128, 1], F32) for i in range(2)]
        NO = min(512, D)
        NOG = D // NO
        pOut = [P.psum(st, "pOut%d" % i, [128, NO], F32) for i in range(min(8, 2 * NOG))]
        k = 0
        for m in range(NU):
            mt = mT[m % 2]
            P.dma("sp", mt[:, :, :], M_s[:, :, m * 512:(m + 1) * 512].rearrange("c p t -> p c t"), reads=[M_s], writes=[mt])
            for bi in range(4):
                row = (W - CH) + m * 512 + bi * 128
                x_, o_, ss_, rs_ = xb[k % 2], ob[k % 2], ss[k % 2], rs[k % 2]
                P.dma("sp", x_[:, :], xw[row:row + 128, :], writes=[x_])
                pss = []
                for og in range(NOG):
                    po = pOut[(k * NOG + og) % len(pOut)]
                    pss.append(po)
                    for cc in range(ND):
                        P.op("pe", lambda cc=cc, og=og: mm(po[:, :], mt[:, cc, bi * 128:(bi + 1) * 128], Wo[:, cc, og * NO:(og + 1) * NO],
                                                           start=(cc == 0), stop=(cc == ND - 1)),
                             reads=[mt, Wo], writes=[po] if cc == 0 else (), acc=() if cc == 0 else [po])
                    P.op("act" if og % 2 == 0 else "dve",
                         (lambda og=og: A.copy(o_[:, og * NO:(og + 1) * NO], po[:, :])) if og % 2 == 0 else
                         (lambda og=og: V.tensor_copy(o_[:, og * NO:(og + 1) * NO], po[:, :])),
                         reads=[po], writes=[o_] if og == 0 else (), acc=() if og == 0 else [o_])
                P.op("act", lambda: A.activation(jb[:, :], o_[:, :], AF.Square, accum_out=ss_[:, :]), reads=[o_], writes=[jb, ss_])
                rs_from_ss(ss_, rs_, D)
                P.op("dve", lambda: V.scalar_tensor_tensor(o_[:, :], o_[:, :], rs_[:, :], GGb[:, :], ALU.mult, ALU.mult),
                     reads=[rs_, GGb], writes=[o_])
                P.op("dve", lambda: V.tensor_tensor(o_[:, :], o_[:, :], x_[:, :], ALU.add), reads=[x_], writes=[o_])
                orow = m * 512 + bi * 128
                P.dma("sp", y[orow:orow + 128, :], o_[:, :], reads=[o_], wnow=[yB])
                k += 1
        P.barrier()
    st0.close()
    return nc, P


def make_masks():
    kk = np.arange(128)[:, None]
    qq = np.arange(512)[None, :]
    mm_ = np.stack([np.where(j * 128 + kk <= qq, 0.0, NEG) for j in range(4)]).astype(np.float32)
    dms = []
    for (win, d), nb in zip(DIL, (1, 4, 16)):
        for i in range(nb + 4):
            o = i - nb
            delta = qq - 128 * o - kk
            ok = (delta >= 0) & (delta % d == 0) & (delta <= win)
            dms.append(np.where(ok, 0.0, NEG))
    return mm_, np.stack(dms).astype(np.float32)


def make_in_maps(cfg, x, c, positions, w_ada, b_ada, g_pre, g_post, w_in, g_kv, w_kv_b, w_mla_proj, w_dil_proj, w_o):
    cf = cfg
    B = x.shape[0]
    mmask, dmask = make_masks()
    ident = np.eye(128, dtype=np.float32)
    freq = np.tile((1.0 / (THETA ** (np.arange(0, 64, 2, dtype=np.float32) / 64.0))).astype(np.float32)[None, :], (128, 1))
    shared = {
        "b_ada": np.ascontiguousarray(b_ada[0][None, :]), "g_pre": np.ascontiguousarray(g_pre[0][None, :]),
        "g_post": np.ascontiguousarray(g_post[0][None, :]), "g_kv": np.ascontiguousarray(g_kv[0][None, :]),
        "w_ada": np.ascontiguousarray(w_ada[0]), "w_in": np.ascontiguousarray(w_in[0]),
        "w_kv_b": np.ascontiguousarray(w_kv_b[0]), "w_mla_proj": np.ascontiguousarray(w_mla_proj[0]),
        "w_dil_proj": np.ascontiguousarray(w_dil_proj[0]), "w_o": np.ascontiguousarray(w_o[0]),
        "ident": ident, "freq": np.ascontiguousarray(freq), "mmask": mmask, "dmask": dmask,
    }
    maps = []
    for b in range(B):
        for j in range(cf.NCB):
            t_end = cf.CH * (j + 1)
            t0 = t_end - cf.W
            nz = max(0, -t0)
            xw = np.zeros((cf.W, cf.D), np.float32)
            xw[nz:] = x[b, max(t0, 0):t_end]
            pw = np.zeros((cf.W,), np.int32)
            pw[nz:] = positions[b, max(t0, 0):t_end]
            kb = np.zeros((cf.W,), np.float32)
            kb[:nz] = NEG
            m = dict(shared)
            m["xw"] = xw
            m["posw"] = np.ascontiguousarray(pw.reshape(cf.NB, 128).T)
            m["kbias"] = np.ascontiguousarray(kb.reshape(cf.NB, 128).T)
            m["c_col"] = np.ascontiguousarray(c[b].reshape(cf.ND, 128).T)
            maps.append(m)
    return maps


_CACHE = {}


def kernel(x, c, positions, w_ada, b_ada, g_pre, g_post, w_in, g_kv, w_kv_b, w_mla_proj, w_dil_proj, w_o):
    cfg = Cfg()
    t0 = time.time()
    args = [np.asarray(a) for a in (x, c, positions, w_ada, b_ada, g_pre, g_post, w_in, g_kv, w_kv_b, w_mla_proj, w_dil_proj, w_o)]
    maps = make_in_maps(cfg, *args)
    if "nc" not in _CACHE:
        _CACHE["nc"] = build(cfg)[0]
    nc = _CACHE["nc"]
    print("[kernel] build+layout %.1fs" % (time.time() - t0), flush=True)
    res = run_bass_kernel_spmd(nc, maps, core_ids=list(range(len(maps))))
    print("[kernel] launch done %.1fs" % (time.time() - t0), flush=True)
    B = args[0].shape[0]
    out = np.empty((B, cfg.S, cfg.D), np.float32)
    i = 0
    for b in range(B):
        for j in range(cfg.NCB):
            out[b, cfg.CH * j:cfg.CH * (j + 1)] = res.results[i]["y"]
            i += 1
    return out
```
